# Optimizing a Trainium2 kernel written in Bass

```python
import math
import jax, jax.numpy as jnp
from jax import lax
import numpy as np

D_MODEL = 1024
BATCH = 2
SEQ = 8192
DEPTH = 2

HEAD_DIM = 64
D_PLE = 256
A_WIDTH = D_MODEL // 2
A_HEADS = A_WIDTH // HEAD_DIM
MOBA_BLOCK = 256
MOBA_TOPK = 3
Q_BLOCK = 128
REL_BUCKETS = 32
REL_MAX_DIST = 128
B_WIDTH = D_MODEL // 4
POOL_WINDOWS = (2, 4, 8, 16)
B_GROUPS = len(POOL_WINDOWS)
B_GROUP = B_WIDTH // B_GROUPS
C_WIDTH = D_MODEL // 4
C_HEADS = C_WIDTH // HEAD_DIM
SGU_CHUNK = 128

D_MIX = A_WIDTH + B_WIDTH + C_WIDTH
IN_WIDTHS = (A_WIDTH,) * 4 + (B_WIDTH,) * 2 + (C_WIDTH,) * 3
D_IN = sum(IN_WIDTHS)
EPS = 1e-6
NEG = -1e30

kernel_name = "hybrid_moba_pool_sgu_trunk"


def rms_norm(x, g):
    x32 = x.astype(jnp.float32)
    y = x32 * lax.rsqrt(jnp.mean(x32 * x32, axis=-1, keepdims=True) + EPS)
    return (y * g.astype(jnp.float32)).astype(x.dtype)


def t5_bucket(n):
    max_exact = REL_BUCKETS // 2
    nf = jnp.maximum(n, 1).astype(jnp.float32)
    large = max_exact + (jnp.log(nf / max_exact) / math.log(REL_MAX_DIST / max_exact)
                         * (REL_BUCKETS - max_exact)).astype(jnp.int32)
    large = jnp.minimum(large, REL_BUCKETS - 1)
    return jnp.where(n < max_exact, n, large)


def moba_attention(q, k, v, rel_bias):
    B, S, H, D = q.shape
    nb = -(-S // MOBA_BLOCK)
    pad = nb * MOBA_BLOCK - S
    topk = min(MOBA_TOPK, nb)
    scale = D ** -0.5
    qh = jnp.transpose(q, (0, 2, 1, 3))
    kh = jnp.pad(jnp.transpose(k, (0, 2, 1, 3)), ((0, 0), (0, 0), (0, pad), (0, 0)))
    vh = jnp.pad(jnp.transpose(v, (0, 2, 1, 3)), ((0, 0), (0, 0), (0, pad), (0, 0)))
    k_blocks = kh.reshape(B, H, nb, MOBA_BLOCK, D)
    v_blocks = vh.reshape(B, H, nb, MOBA_BLOCK, D)
    k_mean = jnp.mean(k_blocks.astype(jnp.float32), axis=3)
    bias_tab = jnp.transpose(rel_bias).astype(jnp.float32)
    b_ix = jnp.arange(B)[:, None, None, None]
    h_ix = jnp.arange(H)[None, :, None, None]
    h_ix5 = jnp.arange(H)[None, :, None, None, None]
    blk_ids = jnp.arange(nb)
    in_blk = jnp.arange(MOBA_BLOCK)

    def q_block(qi):
        start = qi * Q_BLOCK
        qb = lax.dynamic_slice_in_dim(qh, start, Q_BLOCK, axis=2).astype(jnp.float32)
        t = start + jnp.arange(Q_BLOCK)
        own = start // MOBA_BLOCK
        gate = jnp.einsum('bhqd,bhnd->bhqn', qb, k_mean)
        gate = jnp.where(blk_ids < own, gate, NEG)
        top_val, top_idx = lax.top_k(gate, topk)
        sel_valid = top_val > NEG * 0.5
        k_sel = k_blocks[b_ix, h_ix, top_idx].astype(jnp.float32)
        v_sel = v_blocks[b_ix, h_ix, top_idx].astype(jnp.float32)
        sel_pos = top_idx[..., None] * MOBA_BLOCK + in_blk
        sel_bias = bias_tab[h_ix5, t5_bucket(t[:, None, None] - sel_pos)]
        sel_logits = jnp.einsum('bhqd,bhqnkd->bhqnk', qb, k_sel) * scale + sel_bias
        sel_logits = jnp.where(sel_valid[..., None], sel_logits, NEG)
        k_own = lax.dynamic_slice_in_dim(kh, own * MOBA_BLOCK, MOBA_BLOCK, axis=2).astype(jnp.float32)
        v_own = lax.dynamic_slice_in_dim(vh, own * MOBA_BLOCK, MOBA_BLOCK, axis=2).astype(jnp.float32)
        rel = t[:, None] - (own * MOBA_BLOCK + in_blk)[None, :]
        own_bias = bias_tab[:, t5_bucket(jnp.maximum(rel, 0))][None]
        own_logits = jnp.einsum('bhqd,bhkd->bhqk', qb, k_own) * scale + own_bias
        own_logits = jnp.where(rel >= 0, own_logits, NEG)
        logits = jnp.concatenate(
            [sel_logits.reshape(B, H, Q_BLOCK, topk * MOBA_BLOCK), own_logits], axis=-1)
        probs = jax.nn.softmax(logits, axis=-1)
        p_sel = probs[..., :topk * MOBA_BLOCK].reshape(B, H, Q_BLOCK, topk, MOBA_BLOCK)
        p_own = probs[..., topk * MOBA_BLOCK:]
        out = (jnp.einsum('bhqnk,bhqnkd->bhqd', p_sel, v_sel)
               + jnp.einsum('bhqk,bhkd->bhqd', p_own, v_own))
        return out.astype(q.dtype)

    outs = lax.map(q_block, jnp.arange(S // Q_BLOCK))
    return jnp.transpose(outs, (1, 0, 3, 2, 4)).reshape(B, S, H * D)


def multiscale_pool(xb, w_pool, pool_scale):
    B, S, C = xb.shape
    x32 = xb.astype(jnp.float32)
    cs = jnp.cumsum(x32, axis=1)
    count = jnp.arange(1, S + 1, dtype=jnp.float32)[None, :, None]
    groups = []
    for gi, w in enumerate(POOL_WINDOWS):
        c = cs[..., gi * B_GROUP:(gi + 1) * B_GROUP]
        lag = jnp.pad(c, ((0, 0), (w, 0), (0, 0)))[:, :S]
        mean = (c - lag) / jnp.minimum(count, float(w))
        groups.append(mean - x32[..., gi * B_GROUP:(gi + 1) * B_GROUP])
    pooled = jnp.stack(groups, axis=2)
    mixed = jnp.einsum('bsgc,gcd->bsgd', pooled, w_pool.astype(jnp.float32)).reshape(B, S, C)
    return (mixed * pool_scale.astype(jnp.float32)).astype(xb.dtype)


def spatial_gating(u, v, w_s, b_s):
    B, S, C = v.shape
    v32 = v.astype(jnp.float32)
    mu = jnp.mean(v32, axis=-1, keepdims=True)
    var = jnp.mean(jnp.square(v32 - mu), axis=-1, keepdims=True)
    vn = ((v32 - mu) * lax.rsqrt(var + EPS)).reshape(B, S // SGU_CHUNK, SGU_CHUNK, C_HEADS, C // C_HEADS)
    w_c = jnp.tril(w_s.astype(jnp.float32))
    mixed = (jnp.einsum('hts,bnshc->bnthc', w_c, vn)
             + jnp.transpose(b_s.astype(jnp.float32))[None, None, :, :, None])
    return (u.astype(jnp.float32) * mixed.reshape(B, S, C)).astype(u.dtype)


def setup_inputs(seed: int = 0) -> dict:
    key = jax.random.key(seed)
    ks = jax.random.split(key, 14)
    f32 = jnp.float32
    T = SGU_CHUNK
    return {
        'x': jax.random.normal(ks[0], (BATCH, SEQ, D_MODEL), f32),
        'p': jax.random.normal(ks[1], (DEPTH, BATCH, SEQ, D_PLE), f32),
        'norm_g': 1.0 + 0.1 * jax.random.normal(ks[2], (DEPTH, D_MODEL), f32),
        'w_in': jax.random.normal(ks[3], (DEPTH, D_MODEL, D_IN), f32) * D_MODEL ** -0.5,
        'w_out': jax.random.normal(ks[4], (DEPTH, D_MIX, D_MODEL), f32) * D_MIX ** -0.5,
        'rel_bias': 0.5 * jax.random.normal(ks[5], (REL_BUCKETS, A_HEADS), f32),
        'pool_w': jax.random.normal(ks[6], (DEPTH, B_GROUPS, B_GROUP, B_GROUP), f32) * B_GROUP ** -0.5,
        'pool_scale': 1.0 + 0.1 * jax.random.normal(ks[7], (DEPTH, B_WIDTH), f32),
        'sgu_w': jax.random.normal(ks[8], (DEPTH, C_HEADS, T, T), f32) * T ** -0.5,
        'sgu_b': 1.0 + 0.1 * jax.random.normal(ks[9], (DEPTH, C_HEADS, T), f32),
        'ple_w': jax.random.normal(ks[10], (DEPTH, D_PLE, D_MODEL), f32) * D_PLE ** -0.5,
        'ple_gate_w': jax.random.normal(ks[11], (DEPTH, D_MODEL, D_MODEL), f32) * D_MODEL ** -0.5,
        'final_g': 1.0 + 0.1 * jax.random.normal(ks[12], (D_MODEL,), f32),
    }


def reference(x, p, norm_g, w_in, w_out, rel_bias, pool_w, pool_scale, sgu_w, sgu_b,
              ple_w, ple_gate_w, final_g):
    B, S, _ = x.shape
    split_at = [int(s) for s in np.cumsum(IN_WIDTHS)[:-1]]
    h = x
    for i in range(DEPTH):
        hn = rms_norm(h, norm_g[i])
        z = hn @ w_in[i]
        qa, ka, va, ga, xb, gb, uc, vc, gc = jnp.split(z, split_at, axis=-1)
        ya = moba_attention(qa.reshape(B, S, A_HEADS, HEAD_DIM),
                            ka.reshape(B, S, A_HEADS, HEAD_DIM),
                            va.reshape(B, S, A_HEADS, HEAD_DIM), rel_bias) * jax.nn.silu(ga)
        yb = multiscale_pool(xb, pool_w[i], pool_scale[i]) * jax.nn.silu(gb)
        yc = spatial_gating(uc, vc, sgu_w[i], sgu_b[i]) * jax.nn.silu(gc)
        h = h + jnp.concatenate([ya, yb, yc], axis=-1) @ w_out[i]
        h = h + (p[i] @ ple_w[i]) * jax.nn.sigmoid(h @ ple_gate_w[i])
    return rms_norm(h, final_g)
```

```python
import numpy as np
import ml_dtypes
from contextlib import ExitStack

import concourse.bass as bass
import concourse.mybir as mybir
from concourse.bass_utils import run_bass_kernel_spmd

F32 = mybir.dt.float32
BF16 = mybir.dt.bfloat16
AF = mybir.ActivationFunctionType
ALU = mybir.AluOpType
AX = mybir.AxisListType

NCORES = 8
S_LEN = 8192
D = 1024
DIN = 3328
NTOK = 2048
NTAB = 1280
TOFF = 511
NEGM = -30000.0
EPS = 1e-6
TL = [sorted([r, 7 - r, 8 + r, 15 - r]) for r in range(4)]
OWNER = {}
for _r in range(4):
    for _lt, _T in enumerate(TL[_r]):
        OWNER[_T] = (_r, _lt)


STRICT = True


class Op:
    __slots__ = ("eng", "fn", "reads", "writes", "dma", "deps", "signal", "count", "sem",
                 "target", "prev_target", "idx", "inc", "carry")

    def __init__(self, eng, fn, reads, writes, dma):
        self.eng = eng
        self.fn = fn
        self.reads = tuple(reads)
        self.writes = tuple(writes)
        self.dma = dma
        self.deps = []
        self.signal = dma
        self.count = 0
        self.sem = None
        self.target = 0
        self.prev_target = 0
        self.inc = 16
        self.carry = []


class SemPool:
    ENGS = ("pe", "act", "dve", "pool", "sp")

    def __init__(self, nc, es, n_dma_sems=12):
        self.n_dma_sems = n_dma_sems
        self.esem = {e: es.enter_context(nc.semaphore("s_" + e)) for e in self.ENGS}
        self.dsem = {(e, j): es.enter_context(nc.semaphore("d_%s_%d" % (e, j)))
                     for e in ("sp", "pool", "act", "cc") for j in range(n_dma_sems if e != "cc" else 4)}
        self.ecount = {e: 0 for e in self.ENGS}
        self.dcount = {k: 0 for k in self.dsem}
        self.dnext = {e: 0 for e in self.ENGS + ("cc",)}
        self.carry = {}


class Sched:
    ENGS = ("pe", "act", "dve", "pool", "sp")

    def __init__(self, nc, pool=None, n_dma_sems=12):
        self.nc = nc
        self.ops = []
        self.n_dma_sems = n_dma_sems
        self.pool_ = pool
        self.drain_cc = True

    def cc(self, fn, reads=(), writes=()):
        op = self.add("pool", fn, reads, writes, dma=True)
        op.inc = 1
        return op

    def add(self, eng, fn, reads=(), writes=(), dma=False):
        ex = [k for k in list(reads) + list(writes) if k.startswith("PS:")]
        reads = list(reads) + [k for k in ex if k not in reads]
        writes = list(writes) + [k for k in ex if k not in writes]
        op = Op(eng, fn, reads, writes, dma)
        op.idx = len(self.ops)
        self.ops.append(op)
        return op

    def pe(self, fn, reads=(), writes=()):
        return self.add("pe", fn, reads, writes)

    def act(self, fn, reads=(), writes=()):
        return self.add("act", fn, reads, writes)

    def dve(self, fn, reads=(), writes=()):
        return self.add("dve", fn, reads, writes)

    def pool(self, fn, reads=(), writes=()):
        return self.add("pool", fn, reads, writes)

    def dma(self, q, fn, reads=(), writes=()):
        return self.add(q, fn, reads, writes, dma=True)

    def analyze(self):
        last_writer = {}
        readers = {}
        for op in self.ops:
            raw = set()
            other = set()
            for k in op.reads:
                w = last_writer.get(k)
                if w is not None:
                    raw.add(w)
            for k in op.writes:
                w = last_writer.get(k)
                if w is not None:
                    other.add(w)
                for r in readers.get(k, ()):
                    other.add(r)
            deps = []
            for d in raw | other:
                if d is op:
                    continue
                if (not d.dma) and (not op.dma) and d.eng == op.eng:
                    if op.eng == "pe" or (d not in raw and not STRICT):
                        continue
                deps.append(d)
            deps.sort(key=lambda o: o.idx)
            if self.pool_ is not None:
                for k in op.reads:
                    if k not in last_writer and k in self.pool_.carry:
                        op.carry.append(self.pool_.carry[k])
            op.deps = deps
            for d in deps:
                d.signal = True
            for k in op.reads:
                readers.setdefault(k, []).append(op)
            for k in op.writes:
                last_writer[k] = op
                readers[k] = []
        P = self.pool_
        cnt = dict(P.ecount) if P else {e: 0 for e in self.ENGS}
        dcnt = dict(P.dnext) if P else {e: 0 for e in self.ENGS}
        self.dma_uses = dict(P.dcount) if P else {}
        for op in self.ops:
            if op.dma:
                if op.inc == 1 and P:
                    j = dcnt["cc"] % 4
                    dcnt["cc"] += 1
                    key = ("cc", j)
                else:
                    j = dcnt[op.eng] % self.n_dma_sems
                    dcnt[op.eng] += 1
                    key = (op.eng, j)
                prev = self.dma_uses.get(key, 0)
                op.sem = key
                op.prev_target = prev
                op.target = prev + op.inc
                self.dma_uses[key] = op.target
            elif op.signal:
                cnt[op.eng] += 1
                op.count = cnt[op.eng]
        if P:
            P.ecount = cnt
            P.dnext = dcnt
            P.dcount = dict(self.dma_uses)
            for op in self.ops:
                if op.dma and op.inc == 1:
                    for k in op.writes:
                        P.carry[k] = (op.sem, op.target)

    def emit(self):
        nc = self.nc
        self.analyze()
        with ExitStack() as es:
            if self.pool_ is not None:
                esem, dsem = self.pool_.esem, self.pool_.dsem
            else:
                esem = {e: es.enter_context(nc.semaphore("s_" + e)) for e in self.ENGS}
                dsem = {}
                for key in self.dma_uses:
                    dsem[key] = es.enter_context(nc.semaphore("d_%s_%d" % key))
            block = es.enter_context(nc.Block())
            per_eng = {e: [o for o in self.ops if o.eng == e] for e in self.ENGS}

            def run(eng_name, h):
                waited = {}

                def w(sem_key, sem, val):
                    if val <= 0 or waited.get(sem_key, 0) >= val:
                        return
                    waited[sem_key] = val
                    h.wait_ge(sem, val)

                for op in per_eng[eng_name]:
                    for d in op.deps:
                        if d.dma:
                            w(d.sem, dsem[d.sem], d.target)
                        else:
                            w(d.eng, esem[d.eng], d.count)
                    for (ck, ct) in op.carry:
                        w(ck, dsem[ck], ct)
                    if op.dma and op.prev_target > 0:
                        w(op.sem, dsem[op.sem], op.prev_target)
                    inst = op.fn(h)
                    if op.dma:
                        inst.then_inc(dsem[op.sem], op.inc)
                    elif op.signal:
                        inst.then_inc(esem[op.eng], 1)
                for key, tgt in self.dma_uses.items():
                    if key[0] == eng_name or (key[0] == "cc" and eng_name == "pool" and self.drain_cc):
                        w(key, dsem[key], tgt)

            @block.tensor
            def _(h):
                run("pe", h)

            @block.scalar
            def _(h):
                run("act", h)

            @block.vector
            def _(h):
                run("dve", h)

            @block.gpsimd
            def _(h):
                run("pool", h)

            @block.sync
            def _(h):
                run("sp", h)


class Ctx:
    def __init__(self, nc, es):
        self.nc = nc
        self.es = es

    def sb(self, name, shape, dt):
        return self.es.enter_context(self.nc.sbuf_tensor(name, shape, dt))

    def ps(self, name, shape, dt):
        return self.es.enter_context(self.nc.psum_tensor(name, shape, dt))

    def din(self, name, shape, dt):
        return self.nc.dram_tensor(name, shape, dt, kind="ExternalInput").ap()

    def dout(self, name, shape, dt):
        return self.nc.dram_tensor(name, shape, dt, kind="ExternalOutput").ap()


C_Q, C_K, C_V, C_GA, C_XB, C_GB, C_UC, C_VC, C_GC = 0, 512, 1024, 1536, 2048, 2304, 2560, 2816, 3072


def emit_phase_a(nc, S, C, t, pfx="", ntiles=4, nsub=4, dofm=True, stage=99):
    h, w_in, g, ident = t["h"], t["w_in"], t["g"], t["ident"]
    Qtok = t.get("Qtok")
    SGA, VN, XB, GBT, UGT = (t[k] for k in ("SGA", "VN", "XB", "GBT", "UGT"))
    KT, Vd = t.get("KT"), t.get("Vd")
    K = lambda s: pfx + s
    PK = lambda s: "PS:" + pfx + s
    wb = C.sb(K("wb"), [128, 8, DIN], BF16)
    wst = [C.sb(K("wst%d" % i), [128, DIN - C_GA], F32) for i in range(3)]
    gt = C.sb(K("gt"), [128, 8], F32)
    idt = C.sb(K("idt"), [128, 128], BF16)
    epst = C.sb(K("epst"), [128, 1], F32)
    ht = [C.sb(K("ht%d" % i), [128, 4, D], F32) for i in range(2)]
    junk = C.sb(K("junk"), [128, D], F32)
    ss = [C.sb(K("ss%d" % i), [128, 4], F32) for i in range(2)]
    rstd = [C.sb(K("rstd%d" % i), [128, 4], F32) for i in range(2)]
    hn = C.sb(K("hn"), [128, 4, D], BF16)
    hnT = [C.sb(K("hnT%d" % i), [128, 8, 512], BF16) for i in range(4)]
    qst = [C.sb(K("qst%d" % i), [128, 512], BF16) for i in range(2)]
    gast = [C.sb(K("gast%d" % i), [128, 512], BF16) for i in range(2)]
    xbst = [C.sb(K("xbst%d" % i), [128, 256], BF16) for i in range(2)]
    vnst = [C.sb(K("vnst%d" % i), [128, 256], BF16) for i in range(2)]
    vst = [C.sb(K("vst%d" % i), [128, 4, 512], BF16) for i in range(2)]
    bst = C.sb(K("bst"), [128, 4, 6], F32)
    mv = C.sb(K("mv"), [128, 4, 2], F32)
    vrs = C.sb(K("vrs"), [128, 4], F32)
    fst = [C.sb(K("fst%d" % i), [128, 512], BF16) for i in range(3)]
    sgc = C.sb(K("sgc"), [128, 2, 512], F32)
    tp = C.ps(K("tp"), [128, 8, 128], BF16)
    pq = C.ps(K("pq"), [128, 512], F32)
    pv = C.ps(K("pv"), [128, 512], F32)
    pg = C.ps(K("pg"), [128, 512], F32)
    px = C.ps(K("px"), [128, 512], F32)
    pf = [C.ps(K("pf%d" % i), [128, 512], F32) for i in range(2)]

    S.dma("sp", lambda e: e.dma_start(out=idt[:], in_=ident), writes=[K("idt")])
    S.dma("sp", lambda e: e.dma_start(out=gt[:], in_=g), writes=[K("gt")])
    S.pool(lambda e: e.memset(epst[:], EPS), writes=[K("epst")])
    nws = 0
    for (c0, c1, tag) in ((0, C_GA, "q"), (C_GA, DIN, "r")):
        for c in range(8):
            b = nws % 3
            nws += 1
            S.dma("sp", lambda e, c=c, b=b, c0=c0, c1=c1: e.dma_start(out=wst[b][:, 0:c1 - c0],
                                                                      in_=w_in[c * 128:(c + 1) * 128, c0:c1]),
                  writes=[K("wst%d" % b)])
            if c % 2 == 0:
                S.act(lambda e, c=c, b=b, c0=c0, c1=c1: e.activation(wb[:, c, c0:c1], wst[b][:, 0:c1 - c0], AF.Copy,
                                                                     scale=gt[:, c:c + 1]),
                      reads=[K("wst%d" % b), K("gt")], writes=[K("wb%s%d" % (tag, c))])
            else:
                S.dve(lambda e, c=c, b=b, c0=c0, c1=c1: e.tensor_scalar(wb[:, c, c0:c1], wst[b][:, 0:c1 - c0],
                                                                        gt[:, c:c + 1], None, ALU.mult),
                      reads=[K("wst%d" % b), K("gt")], writes=[K("wb%s%d" % (tag, c))])
    WBQ = [K("wbq%d" % c) for c in range(8)]
    WBR = [K("wbr%d" % c) for c in range(8)]
    hv = h.rearrange("(l s p) f -> l p s f", s=4, p=128)
    state = {"nfm": 0}

    def norm_tile(lt):
        hb = lt % 2
        HT = K("ht%d" % hb)
        S.dma("sp", lambda e: e.dma_start(out=ht[hb][:], in_=hv[lt]), writes=[HT])
        for sub in range(4):
            S.act(lambda e, sub=sub: e.activation(junk[:], ht[hb][:, sub, :], AF.Square,
                                                  accum_out=ss[hb][:, sub:sub + 1]),
                  reads=[HT], writes=[K("junk"), K("ss%d" % hb)])
        S.act(lambda e: e.activation(rstd[hb][:], ss[hb][:], AF.Sqrt, bias=epst[:], scale=1.0 / D),
              reads=[K("ss%d" % hb), K("epst")], writes=[K("rstd%d" % hb)])
        S.dve(lambda e: e.reciprocal(rstd[hb][:], rstd[hb][:]), reads=[K("rstd%d" % hb)], writes=[K("rstd%d" % hb)])

    def tr_sub(lt, sub):
        hb = lt % 2
        HT = K("ht%d" % hb)
        S.dve(lambda e: e.tensor_scalar(hn[:, sub, :], ht[hb][:, sub, :], rstd[hb][:, sub:sub + 1], None, ALU.mult),
              reads=[HT, K("rstd%d" % hb)], writes=[K("hn%d" % sub)])
        for c in range(8):
            S.pe(lambda e, c=c: e.transpose(tp[:, c, :], hn[:, sub, c * 128:(c + 1) * 128], idt[:]),
                 reads=[K("hn%d" % sub), K("idt")], writes=[PK("tp")])
        S.act(lambda e: e.copy(hnT[lt][:, :, sub * 128:(sub + 1) * 128], tp[:]),
              reads=[PK("tp")], writes=[K("hnT%d_%d" % (lt, sub))])

    def proj_sub(lt, sub, part):
        hb = lt % 2
        sb2 = sub % 2
        HNT = K("hnT%d_%d" % (lt, sub))
        row = (lt * 4 + sub) * 128

        def proj(ps_ap, col, n, pskey):
            for c in range(8):
                S.pe(lambda e, c=c: e.matmul(ps_ap, hnT[lt][:, c, sub * 128:(sub + 1) * 128],
                                             wb[:, c, col:col + n], start=(c == 0), stop=(c == 7)),
                     reads=[HNT, (WBQ if col < C_GA else WBR)[c]], writes=[pskey])

        if part == 2:
            return proj_sub2(lt, sub, proj)
        proj(pq[:], C_Q, 512, PK("pq"))
        S.act(lambda e: e.copy(qst[sb2][:], pq[:]), reads=[PK("pq")], writes=[K("qst%d" % sb2)])
        if "Qs2" in t:
            S.dma("sp", lambda e: e.dma_start(
                out=t["Qs2"][row // 1024, :, :, (row % 1024) // 128, :].rearrange("d p c -> p d c"),
                in_=qst[sb2][:].rearrange("p (d c) -> p d c", c=128)),
                reads=[K("qst%d" % sb2)], writes=[K("Qtok%d" % row)])
        else:
            S.dma("sp", lambda e: e.dma_start(out=Qtok[row:row + 128, :], in_=qst[sb2][:]),
                  reads=[K("qst%d" % sb2)], writes=[K("Qtok%d" % row)])
        proj(pv[:], C_V, 512, PK("pv"))
        S.dve(lambda e: e.tensor_copy(vst[hb][:, sub, :], pv[:]), reads=[PK("pv")], writes=[K("vst%d_%d" % (hb, sub))])

    def proj_sub2(lt, sub, proj):
        sb2 = sub % 2
        row = (lt * 4 + sub) * 128
        proj(pg[:], C_GA, 512, PK("pg"))
        S.act(lambda e: e.activation(gast[sb2][:], pg[:], AF.Silu), reads=[PK("pg")], writes=[K("gast%d" % sb2)])
        S.dma("sp", lambda e: e.dma_start(out=SGA[row:row + 128, :], in_=gast[sb2][:]),
              reads=[K("gast%d" % sb2)], writes=[K("SGA%d" % row)])
        proj(px[:, 0:256], C_XB, 256, PK("px"))
        proj(px[:, 256:512], C_VC, 256, PK("px"))
        S.act(lambda e: e.copy(xbst[sb2][:], px[:, 0:256]), reads=[PK("px")], writes=[K("xbst%d" % sb2)])
        S.dma("sp", lambda e: e.dma_start(out=XB[row:row + 128, :], in_=xbst[sb2][:]),
              reads=[K("xbst%d" % sb2)], writes=[K("XB%d" % row)])
        S.dve(lambda e: e.bn_stats(bst[:, sub, :], px[:, 256:512]), reads=[PK("px")], writes=[K("bst")])
        S.dve(lambda e: e.bn_aggr(mv[:, sub, :], bst[:, sub, :]), reads=[K("bst")], writes=[K("mv")])
        S.act(lambda e: e.activation(vrs[:, sub:sub + 1], mv[:, sub, 1:2], AF.Sqrt, bias=epst[:], scale=1.0),
              reads=[K("mv"), K("epst")], writes=[K("vrs")])
        S.dve(lambda e: e.reciprocal(vrs[:, sub:sub + 1], vrs[:, sub:sub + 1]), reads=[K("vrs")], writes=[K("vrs")])
        S.dve(lambda e: e.tensor_scalar(vnst[sb2][:], px[:, 256:512], mv[:, sub, 0:1], vrs[:, sub:sub + 1],
                                        ALU.subtract, ALU.mult),
              reads=[PK("px"), K("mv"), K("vrs")], writes=[K("vnst%d" % sb2)])
        S.dma("sp", lambda e: e.dma_start(out=VN[row:row + 128, :], in_=vnst[sb2][:]),
              reads=[K("vnst%d" % sb2)], writes=[K("VN%d" % row)])

    def fm_tile(lt, part):
        hb = lt % 2
        for hh in range(8 if part == 1 else 0):
            S.dma("sp", lambda e, hh=hh: e.dma_start(
                out=(t["Vd2"][lt // 2, hh, :, (lt % 2) * 4:(lt % 2) * 4 + 4, :] if "Vd2" in t
                     else Vd[hh, :, lt * 4:(lt + 1) * 4, :]), in_=vst[hb][:, :, hh * 64:(hh + 1) * 64]),
                reads=[K("vst%d_%d" % (hb, s_)) for s_ in range(4)], writes=[K("Vd%d_%d" % (lt, hh))])
        HNTA = [K("hnT%d_%d" % (lt, s_)) for s_ in range(4)]
        tsl = slice(lt * 512, (lt + 1) * 512)

        def fproj(col):
            pb_ = state["nfm"] % 2
            fb_ = state["nfm"] % 3
            state["nfm"] += 1
            for c in range(8):
                S.pe(lambda e, c=c: e.matmul(pf[pb_][:], wb[:, c, col:col + 128], hnT[lt][:, c, :],
                                             start=(c == 0), stop=(c == 7)),
                     reads=HNTA + [(WBQ if col < C_GA else WBR)[c]], writes=[PK("pf%d" % pb_)])
            return pb_, fb_

        for i in range(4 if part == 1 else 0):
            pb_, fb_ = fproj(C_K + i * 128)
            S.dve(lambda e, pb_=pb_, fb_=fb_: e.tensor_copy(fst[fb_][:], pf[pb_][:]), reads=[PK("pf%d" % pb_)],
                  writes=[K("fst%d" % fb_)])
            S.dma("sp", lambda e, fb_=fb_, i=i: e.dma_start(
                out=(t["KT2"][lt // 2, 2 * i:2 * i + 2, :, (lt % 2) * 512:(lt % 2) * 512 + 512] if "KT2" in t
                     else KT[2 * i:2 * i + 2, :, tsl]).rearrange("h d t -> (h d) t"), in_=fst[fb_][:]),
                reads=[K("fst%d" % fb_)], writes=[K("KT%d_%d" % (lt, i))])
        if part == 1:
            return
        for i in range(2):
            pb_, fb_ = fproj(C_GB + i * 128)
            S.act(lambda e, pb_=pb_, fb_=fb_: e.activation(fst[fb_][:], pf[pb_][:], AF.Silu), reads=[PK("pf%d" % pb_)],
                  writes=[K("fst%d" % fb_)])
            S.dma("sp", lambda e, fb_=fb_, i=i: e.dma_start(out=GBT[i * 128:(i + 1) * 128, tsl], in_=fst[fb_][:]),
                  reads=[K("fst%d" % fb_)], writes=[K("GBT%d_%d" % (lt, i))])
        for i in range(2):
            pb_ = state["nfm"] % 2
            state["nfm"] += 1
            for c in range(8):
                S.pe(lambda e, c=c, i=i, pb_=pb_: e.matmul(pf[pb_][:], wb[:, c, C_GC + i * 128:C_GC + (i + 1) * 128],
                                                           hnT[lt][:, c, :], start=(c == 0), stop=(c == 7)),
                     reads=HNTA + [WBR[c]], writes=[PK("pf%d" % pb_)])
            S.act(lambda e, pb_=pb_, i=i: e.activation(sgc[:, i, :], pf[pb_][:], AF.Silu), reads=[PK("pf%d" % pb_)],
                  writes=[K("sgc%d" % i)])
        for i in range(2):
            pb_, fb_ = fproj(C_UC + i * 128)
            S.dve(lambda e, pb_=pb_, fb_=fb_, i=i: e.tensor_tensor(fst[fb_][:], pf[pb_][:], sgc[:, i, :], ALU.mult),
                  reads=[PK("pf%d" % pb_), K("sgc%d" % i)], writes=[K("fst%d" % fb_)])
            S.dma("sp", lambda e, fb_=fb_, i=i: e.dma_start(out=UGT[i * 128:(i + 1) * 128, tsl], in_=fst[fb_][:]),
                  reads=[K("fst%d" % fb_)], writes=[K("UGT%d_%d" % (lt, i))])

    norm_tile(0)
    tr_sub(0, 0)
    for lt in range(ntiles):
        if lt + 1 < ntiles:
            norm_tile(lt + 1)
        for sub in range(4):
            if sub < 3:
                tr_sub(lt, sub + 1)
            elif lt + 1 < ntiles:
                tr_sub(lt + 1, 0)
            proj_sub(lt, sub, 1)
        fm_tile(lt, 1)
        if "after_tile" in t:
            t["after_tile"](lt, S, K)
    for n2, lt in enumerate([ntiles - 1] + list(range(ntiles - 1))):
        for sub in range(4):
            proj_sub(lt, sub, 2)
        if lt == ntiles - 1 and "after_xb" in t:
            t["after_xb"](S, K)
        fm_tile(lt, 2)
        if "after_p2" in t:
            t["after_p2"](n2)


def build_a(**kw):
    nc = bass.Bass("TRN2", target_bir_lowering=False)
    with ExitStack() as es:
        C = Ctx(nc, es)
        t = dict(
            h=C.din("h", [NTOK, D], F32), w_in=C.din("w_in", [D, DIN], F32), g=C.din("g", [128, 8], F32),
            ident=C.din("ident", [128, 128], BF16),
            Qtok=C.dout("Qtok", [NTOK, 512], BF16), SGA=C.dout("SGA", [NTOK, 512], BF16),
            VN=C.dout("VN", [NTOK, 256], BF16), XB=C.dout("XB", [NTOK, 256], BF16),
            KT=C.dout("KT", [8, 64, NTOK], BF16), Vd=C.dout("Vd", [8, 128, 16, 64], BF16),
            GBT=C.dout("GBT", [256, NTOK], BF16), UGT=C.dout("UGT", [256, NTOK], BF16))
        S = Sched(nc)
        emit_phase_a(nc, S, C, t, **kw)
        S.emit()
    return nc


def emit_phase_b1(nc, S, C, t, pfx="", ntile=16, nheads=2):
    rb, Os = t["rb"], t.get("Os")
    if "Qsrc" in t:
        Qsrc, Ksrc, Vsrc = t["Qsrc"], t["Ksrc"], t["Vsrc"]
    else:
        Qg, KTg, Vdg = t["Qg"], t["KTg"], t["Vdg"]
        Qsrc = lambda e, s_: Qg[s_].rearrange("(k p) c -> p k c", p=128)
        Ksrc = lambda e, s_, hl: KTg[s_, hl]
        Vsrc = lambda e, s_, hl: Vdg[s_, hl]
    XK = list(t.get("xkeys", []))
    ident, identf, Erows, onehot, masks, tabd = (t[k] for k in ("ident", "identf", "Erows", "onehot", "masks", "tabd"))
    K = lambda s: pfx + s
    PK = lambda s: "PS:" + pfx + s
    idt = C.sb(K("idt"), [128, 128], BF16)
    idf = C.sb(K("idf"), [128, 128], F32)
    Kaug = [C.sb(K("Kaug%d" % i), [96, S_LEN], BF16) for i in range(2)]
    Vaug = [C.sb(K("Vaug%d" % i), [128, 64, 66], BF16) for i in range(2)]
    Qall = C.sb(K("Qall"), [128, 64, 128], BF16)
    Oall = C.sb(K("Oall"), [128, 64, 128], BF16)
    TB = C.sb(K("TB"), [128, 2, 6, 512], BF16)
    msk = C.sb(K("msk"), [128, 3, 1024], F32)
    ksf = C.sb(K("ksf"), [64, 32], F32)
    ksb = [C.sb(K("ksb%d" % i), [64, 32], BF16) for i in range(2)]
    rbx = C.sb(K("rbx"), [32, 2], F32)
    rbrep = [C.sb(K("rbrep%d" % i), [33, 128], F32) for i in range(2)]
    oh = C.sb(K("oh"), [33, NTAB], F32)
    tabf = C.sb(K("tabf"), [128, 2, NTAB], BF16)
    Qaug = [C.sb(K("Qaug%d" % i), [96, 512], BF16) for i in range(2)]
    gsb = C.sb(K("gsb"), [128, 4, 32], F32)
    top8 = C.sb(K("top8"), [128, 4, 8], F32)
    sel = C.sb(K("sel"), [128, 4, 32], F32)
    Mtok = C.sb(K("Mtok"), [128, 4, 32], BF16)
    PT = [C.sb(K("PT%d" % i), [128, 512], BF16) for i in range(4)]
    osb = [C.sb(K("osb%d" % i), [65, 512], F32) for i in range(2)]
    rden = C.sb(K("rden"), [128, 4], F32)
    SPS = [C.ps(K("sps%d" % i), [128, 512], F32) for i in range(4)]
    OT = [C.ps(K("ot%d" % i), [128, 512], F32) for i in range(2)]
    PQM = C.ps(K("pqm"), [128, 1024], BF16)
    PGO = C.ps(K("pgo"), [128, 512], F32)

    S.dma("sp", lambda e: e.dma_start(out=idt[:], in_=ident), writes=[K("idt")])
    S.dma("sp", lambda e: e.dma_start(out=idf[:], in_=identf), writes=[K("idf")])
    S.dma("sp", lambda e: e.dma_start(out=msk[:], in_=masks.rearrange("m p n -> p m n")), writes=[K("msk")])
    S.dma("sp", lambda e: e.dma_start(out=oh[:], in_=onehot), writes=[K("oh")])
    if "Qdirect" in t:
        for j in range(2):
            S.dma("act", lambda e, j=j: e.dma_start(
                out=Qall[:, :, :].rearrange("p (s j k) c -> p s j (k c)", s=4, j=2, k=8)[:, :, j, :],
                in_=t["Qdirect"](e, j)), reads=XK + ["QsG%d" % j], writes=[K("Qall%d_%d" % (s_, j)) for s_ in range(4)])
    else:
        for s_ in range(4):
            S.dma("sp", lambda e, s_=s_: e.dma_start(out=Qall[:, s_ * 16:(s_ + 1) * 16, :], in_=Qsrc(e, s_)),
                  reads=XK, writes=[K("Qall%d_0" % s_), K("Qall%d_1" % s_)])
    for hl in range(nheads):
        S.dma("sp", lambda e, hl=hl: e.dma_start(out=Kaug[hl][64:96, :], in_=Erows), writes=[K("KE%d" % hl)])
        for s in range(4):
            for hf in range(2):
                if "Kdirect" in t:
                    if s == 0:
                        S.dma("act", lambda e, hl=hl, hf=hf: e.dma_start(
                            out=Kaug[hl][0:64, :].rearrange("p (s j t) -> p s j t", s=4, j=2)[:, :, hf, :],
                            in_=t["Kdirect"](e, hl, hf)), reads=XK + ["KTG%d" % hf],
                            writes=[K("Kd%d_%d_%d" % (hl, s_, hf)) for s_ in range(4)])
                else:
                    S.dma("sp", lambda e, hl=hl, s=s, hf=hf: e.dma_start(
                        out=Kaug[hl][0:64, s * 2048 + hf * 1024:s * 2048 + (hf + 1) * 1024],
                        in_=Ksrc(e, s, hl)[:, hf * 1024:(hf + 1) * 1024]), reads=XK,
                        writes=[K("Kd%d_%d_%d" % (hl, s, hf))])
                S.dma("sp", lambda e, hl=hl, s=s, hf=hf: e.dma_start(
                    out=Vaug[hl][:, s * 16 + hf * 8:s * 16 + (hf + 1) * 8, 0:64],
                    in_=Vsrc(e, s, hl)[:, hf * 8:(hf + 1) * 8, :]), reads=XK, writes=[K("Vd%d_%d_%d" % (hl, s, hf))])
        S.pool(lambda e, hl=hl: e.memset(Vaug[hl][:, :, 64:65], 1.0), writes=[K("V1%d" % hl)])
    S.dma("sp", lambda e: e.dma_start(out=rbx[0:32, :], in_=rb), writes=[K("rbx")])
    for hl in range(nheads):
        S.pool(lambda e, hl=hl: e.memset(rbrep[hl][:, :], 1.0), writes=[K("rbrep%d" % hl)])
        S.dve(lambda e, hl=hl: e.tensor_scalar(rbrep[hl][0:32, :], rbrep[hl][0:32, :], rbx[0:32, hl:hl + 1], None,
                                               ALU.mult),
              reads=[K("rbrep%d" % hl), K("rbx")], writes=[K("rbrep%d" % hl)])
        for ci, (c0, c1) in enumerate(((0, 512), (512, 1024), (1024, NTAB))):
            S.pe(lambda e, hl=hl, c0=c0, c1=c1: e.matmul(SPS[0][:, 0:c1 - c0], rbrep[hl][:, :], oh[:, c0:c1],
                                                         start=True, stop=True),
                 reads=[K("rbrep%d" % hl), K("oh")], writes=[PK("sps0")])
            S.dve(lambda e, hl=hl, c0=c0, c1=c1: e.tensor_copy(tabf[:, hl, c0:c1], SPS[0][:, 0:c1 - c0]),
                  reads=[PK("sps0")], writes=[K("tabf%d_%d" % (hl, ci))])
        S.dma("sp", lambda e, hl=hl: e.dma_start(out=tabd[hl], in_=tabf[:, hl, :]),
              reads=[K("tabf%d_%d" % (hl, ci)) for ci in range(3)], writes=[K("tabd%d" % hl)])
        for i in range(6):
            Dv = 256 - 128 * i
            src = bass.AP(tensor=tabd.tensor, offset=hl * 128 * NTAB + TOFF + Dv, ap=[[NTAB - 1, 128], [1, 512]])
            S.dma("sp", lambda e, hl=hl, i=i, src=src: e.dma_start(out=TB[:, hl, i, :], in_=src),
                  reads=[K("tabd%d" % hl)], writes=[K("TB%d" % hl)])
    KALL = lambda hl: [K("KE%d" % hl)] + [K("Kd%d_%d_%d" % (hl, s, hf)) for s in range(4) for hf in range(2)]
    VALL = lambda hl: [K("V1%d" % hl)] + [K("Vd%d_%d_%d" % (hl, s, hf)) for s in range(4) for hf in range(2)]

    def build_steps(idx, hl, T):
        qb = idx % 2
        QA, QM = K("QA%d" % qb), K("QM%d" % qb)
        steps = []

        def s1():
            for sub in range(4):
                blk = T * 4 + sub
                S.pe(lambda e, sub=sub, blk=blk: e.transpose(PQM[0:64, sub * 128:(sub + 1) * 128],
                                                             Qall[:, blk, hl * 64:(hl + 1) * 64], idt[:]),
                     reads=[K("Qall%d_0" % (blk // 16)), K("Qall%d_1" % (blk // 16)), K("idt")], writes=[PK("pqm")])
            S.act(lambda e: e.mul(Qaug[qb][0:64, :], PQM[0:64, 0:512], 0.125), reads=[PK("pqm")], writes=[QA])

        def s2():
            for sub in range(4):
                S.pe(lambda e, sub=sub: e.matmul(PGO[:, sub * 32:(sub + 1) * 32],
                                                 Qaug[qb][0:64, sub * 128:(sub + 1) * 128], ksb[hl][:, :],
                                                 start=True, stop=True),
                     reads=[QA, K("ksb%d" % hl)], writes=[PK("pgo")])

        def s3():
            for sub in range(4):
                own = 2 * T + sub // 2
                osl = slice(own * 32, (own + 1) * 32)
                S.dve(lambda e, sub=sub, osl=osl: e.tensor_tensor(gsb[:, sub, :], PGO[:, sub * 32:(sub + 1) * 32],
                                                                  msk[:, 0, osl], ALU.add),
                      reads=[PK("pgo"), K("msk")], writes=[K("gsb")])
                S.dve(lambda e, sub=sub: e.max(top8[:, sub, :], gsb[:, sub, :]), reads=[K("gsb")], writes=[K("top8")])
                S.dve(lambda e, sub=sub, osl=osl: e.scalar_tensor_tensor(sel[:, sub, :], gsb[:, sub, :],
                                                                         top8[:, sub, 2:3], msk[:, 1, osl],
                                                                         ALU.is_ge, ALU.mult),
                      reads=[K("gsb"), K("top8"), K("msk")], writes=[K("sel")])
                S.dve(lambda e, sub=sub, osl=osl: e.tensor_tensor(sel[:, sub, :], sel[:, sub, :], msk[:, 2, osl],
                                                                  ALU.add),
                      reads=[K("sel"), K("msk")], writes=[K("sel")])
                S.dve(lambda e, sub=sub: e.tensor_scalar(Mtok[:, sub, :], sel[:, sub, :], -1.0, -NEGM,
                                                         ALU.add, ALU.mult),
                      reads=[K("sel")], writes=[K("Mtok")])

        def s4():
            for sub in range(4):
                S.pe(lambda e, sub=sub: e.transpose(PQM[0:32, 512 + sub * 128:512 + (sub + 1) * 128],
                                                    Mtok[:, sub, :], idt[:]),
                     reads=[K("Mtok"), K("idt")], writes=[PK("pqm")])
            S.act(lambda e: e.copy(Qaug[qb][64:96, :], PQM[0:32, 512:1024]), reads=[PK("pqm")], writes=[QM])

        return [s1, s2, s3, s4]

    tiles = [(hl, T) for T in range(ntile) for hl in range(nheads)]
    if nheads < 2 or ntile < 16:
        S.pool(lambda e: e.memset(Oall[:], 0.0),
               writes=[K("Oall%d_%d" % (h_, b_)) for h_ in range(2) for b_ in range(64)])
    state = {'nkt_total': 0}
    for hl in range(nheads):
        S.dve(lambda e, hl=hl: e.tensor_reduce(ksf[:, :], Kaug[hl][0:64, :].rearrange("p (j k) -> p j k", k=256),
                                               AX.X, ALU.add),
              reads=KALL(hl), writes=[K("ksf")])
        S.dve(lambda e, hl=hl: e.tensor_copy(ksb[hl][:, :], ksf[:, :]), reads=[K("ksf")], writes=[K("ksb%d" % hl)])

    NB = len(SPS)
    LAG = NB - 1
    steps = []
    for idx, (hl, T) in enumerate(tiles):
        nkt = 4 * T + 4
        for kt in range(nkt):
            steps.append((idx, hl, T, kt, nkt))
    first_step = {}
    for gi, st in enumerate(steps):
        first_step.setdefault(st[0], gi)
    actions = {}

    def at(gi, fn):
        actions.setdefault(max(gi, 0), []).append(fn)

    for st in build_steps(0, *tiles[0]):
        at(0, st)
    for idx in range(1, len(tiles)):
        f0 = first_step[idx - 1]
        n_prev = first_step[idx] - f0
        ready_by = first_step[idx] - LAG
        bs = build_steps(idx, *tiles[idx])
        offs = (0, 2, 3, 6)
        for k_, st in enumerate(bs):
            at(min(f0 + offs[k_], ready_by), st)

    def col0(T, kt):
        return max(0, kt - 4 * T) * 128

    def emit_qk(gi):
        idx, hl, T, kt, nkt = steps[gi]
        sb = gi % NB
        qb = idx % 2
        QA, QM = K("QA%d" % qb), K("QM%d" % qb)
        special = kt >= 4 * T - 1
        c0 = col0(T, kt)
        S.pe(lambda e: e.matmul(SPS[sb][:, c0:512], Kaug[hl][0:96, kt * 128:(kt + 1) * 128], Qaug[qb][0:96, c0:512],
                                start=True, stop=not special),
             reads=KALL(hl) + [QA, QM], writes=[PK("sps%d" % sb)])
        if special:
            i = kt - (4 * T - 2)
            S.pe(lambda e: e.matmul(SPS[sb][:, c0:512], idt[:, :], TB[:, hl, i, c0:512], start=False, stop=True),
                 reads=[K("idt"), K("TB%d" % hl)], writes=[PK("sps%d" % sb)])
        S.act(lambda e: e.activation(PT[sb][:, c0:512], SPS[sb][:, c0:512], AF.Exp),
              reads=[PK("sps%d" % sb)], writes=[K("PT%d" % sb)])

    def emit_pv(gi):
        idx, hl, T, kt, nkt = steps[gi]
        sb = gi % NB
        ob = idx % 2
        c0 = col0(T, kt)
        S.pe(lambda e: e.matmul(OT[ob][0:65, c0:512], Vaug[hl][:, kt, 0:65], PT[sb][:, c0:512],
                                start=(kt == 0), stop=(kt == nkt - 1)),
             reads=VALL(hl) + [K("PT%d" % sb)], writes=[PK("ot%d" % ob)])
        if kt == nkt - 1:
            S.act(lambda e: e.copy(osb[ob][:, :], OT[ob][0:65, :]), reads=[PK("ot%d" % ob)], writes=[K("osb%d" % ob)])

            def epi2():
                for sub in range(4):
                    S.pe(lambda e, sub=sub: e.transpose(PGO[:, 128 + sub * 66:128 + sub * 66 + 65],
                                                        osb[ob][0:65, sub * 128:(sub + 1) * 128], idf[0:65, 0:65]),
                         reads=[K("osb%d" % ob), K("idf")], writes=[PK("pgo")])
                otp = PGO[:, 128:392].rearrange("p (s c) -> p s c", c=66)
                S.dve(lambda e: e.reciprocal(rden[:, :], otp[:, :, 64]), reads=[PK("pgo")], writes=[K("rden")])
                for sub in range(4):
                    blk = T * 4 + sub
                    S.dve(lambda e, sub=sub, blk=blk: e.tensor_scalar(Oall[:, blk, hl * 64:(hl + 1) * 64],
                                                                      otp[:, sub, 0:64], rden[:, sub:sub + 1], None,
                                                                      ALU.mult),
                          reads=[PK("pgo"), K("rden")], writes=[K("Oall%d_%d" % (hl, blk))])
                if "after_epi" in t:
                    t["after_epi"](S, K, hl, T)
                if "Os2" in t and hl == nheads - 1 and T % 2 == 1:
                    d_, th = T // 4, (T % 4) // 2
                    S.dma("sp", lambda e: e.dma_start(
                        out=t["Os2"][th, d_].rearrange("(k p) c -> p k c", p=128),
                        in_=Oall[:, d_ * 16 + th * 8:d_ * 16 + th * 8 + 8, :]),
                        reads=[K("Oall%d_%d" % (h_, b_)) for h_ in range(nheads)
                               for b_ in range(d_ * 16 + th * 8, d_ * 16 + th * 8 + 8)],
                        writes=[K("Os%d_%d" % (d_, th))])
                    if "after_out" in t and d_ == 3 and th == 0:
                        t["after_out"](S, K, th)
            at(gi + LAG + 3, epi2)

    nsteps = len(steps)
    for gi in range(nsteps + LAG + 4):
        for fn in actions.pop(gi, []):
            fn()
        if gi < nsteps:
            emit_qk(gi)
        if 0 <= gi - LAG < nsteps:
            emit_pv(gi - LAG)
    for gi in sorted(actions):
        for fn in actions[gi]:
            fn()
    OKEYS = [K("Oall%d_%d" % (hl, blk)) for hl in range(nheads) for blk in range(4 * ntile)]
    if "Os2" not in t:
        S.dma("sp", lambda e: e.dma_start(out=Os.rearrange("d (k p) c -> p (d k) c", p=128)[:, 0:4 * ntile, :],
                                          in_=Oall[:, 0:4 * ntile, :]), reads=OKEYS, writes=[K("Os")])


def build_b1(**kw):
    nc = bass.Bass("TRN2", target_bir_lowering=False)
    with ExitStack() as es:
        C = Ctx(nc, es)
        t = dict(
            Qg=C.din("Qg", [4, NTOK, 128], BF16), KTg=C.din("KTg", [4, 2, 64, NTOK], BF16),
            Vdg=C.din("Vdg", [4, 2, 128, 16, 64], BF16), rb=C.din("rb", [32, 2], F32),
            ident=C.din("ident", [128, 128], BF16), identf=C.din("identf", [128, 128], F32),
            Erows=C.din("Erows", [32, S_LEN], BF16), onehot=C.din("onehot", [33, NTAB], F32),
            masks=C.din("masks", [3, 128, 1024], F32),
            tabd=nc.dram_tensor("tabd", [2, 128, NTAB], BF16, kind="Internal").ap(),
            Os=C.dout("Os", [4, NTOK, 128], BF16))
        S = Sched(nc)
        emit_phase_b1(nc, S, C, t, **kw)
        S.emit()
    return nc


def t5_bucket_np(n):
    n = np.asarray(n)
    nf = np.maximum(n, 1).astype(np.float32)
    large = 16 + (np.log(nf / 16) / np.float32(np.log(128 / 16)) * 16).astype(np.int32)
    large = np.minimum(large, 31)
    return np.where(n < 16, n, large)


def make_consts():
    c = {}
    c["ident"] = np.eye(128, dtype=np.float32).astype(ml_dtypes.bfloat16)
    c["identf"] = np.eye(128, dtype=np.float32)
    E = np.zeros((32, S_LEN), np.float32)
    for j in range(32):
        E[j, j * 256:(j + 1) * 256] = 1.0
    c["Erows"] = E.astype(ml_dtypes.bfloat16)
    n = np.arange(NTAB) - TOFF
    bk = t5_bucket_np(np.maximum(n, 0))
    oh = np.zeros((33, NTAB), np.float32)
    oh[bk, np.arange(NTAB)] = 1.0
    oh[31, :] -= 1.0
    oh[:32, n < 0] = 0.0
    oh[32, n < 0] = NEGM
    c["onehot"] = oh
    j = np.arange(32)[None, :]
    o = np.arange(32)[:, None]
    m = np.zeros((3, 32, 32), np.float32)
    m[0] = np.where(j >= o, -1e30, 0.0)
    m[1] = (j < o)
    m[2] = (j == o)
    c["masks"] = np.ascontiguousarray(np.broadcast_to(m.reshape(3, 1, 1024), (3, 128, 1024)))
    return c


def emit_phase_b2(nc, S, C, t, pfx="", last=False, nblk=16):
    K = lambda s: pfx + s
    PK = lambda s: "PS:" + pfx + s
    Og = t.get("Og")
    SGA, VN, XB, GBT, UGT = (t[k] for k in ("SGA", "VN", "XB", "GBT", "UGT"))
    XBlast, halosel, Afirst, Amat, trilm = (t[k] for k in ("XBlast", "halosel", "Afirst", "Amat", "trilm"))
    h, p, hout = t["h"], t["p"], t["hout"]
    idt = C.sb(K("idt"), [128, 128], BF16)
    Oall = C.sb(K("Oall"), [128, 16, 512], BF16)
    SGAs = C.sb(K("SGAs"), [128, 16, 512], BF16)
    VNs = C.sb(K("VNs"), [128, 16, 256], BF16)
    XBs = C.sb(K("XBs"), [128, 16, 256], BF16)
    GBs = C.sb(K("GBs"), [128, 2, NTOK], BF16)
    UGs = C.sb(K("UGs"), [128, 2, NTOK], BF16)
    XBl = C.sb(K("XBl"), [128, 4, 256], BF16)
    hs = C.sb(K("hs"), [128, 4], F32)
    xacc = C.sb(K("xacc"), [128, 256], F32)
    xbp0 = C.sb(K("xbp0"), [128, 256], BF16)
    Af = C.sb(K("Af"), [128, 4, 128], BF16)
    Am = C.sb(K("Am"), [128, 4, 2, 128], BF16)
    wpre = t.get("wpre")
    if wpre is None:
        wst = [C.sb(K("wst%d" % i), [128, 1024], F32) for i in range(4)]
        wob = C.sb(K("wob"), [128, 8, 1024], BF16)
        wgb = C.sb(K("wgb"), [128, 8, 1024], BF16)
        wpb = C.sb(K("wpb"), [128, 2, 1024], BF16)
    else:
        wob, wgb, wpb = wpre["wob"], wpre["wgb"], wpre["wpb"]
    wbdf = C.sb(K("wbdf"), [128, 2, 128], F32)
    wbd = C.sb(K("wbd"), [128, 2, 128], BF16)
    tril = C.sb(K("tril"), [128, 128], F32)
    swf = C.sb(K("swf"), [128, 4, 128], F32)
    swb = C.sb(K("swb"), [128, 4, 128], BF16)
    WsT = C.sb(K("WsT"), [128, 4, 128], BF16)
    bsf = C.sb(K("bsf"), [1, 512], F32)
    bsb = C.sb(K("bsb"), [1, 512], BF16)
    ones = C.sb(K("ones"), [1, 64], BF16)
    psc = C.sb(K("psc"), [128, 2], F32)
    yab = [C.sb(K("yab%d" % i), [128, 512], BF16) for i in range(2)]
    ycT = [C.sb(K("ycT%d" % i), [128, 8, 128], BF16) for i in range(2)]
    plsb = C.sb(K("plsb"), [128, 2, 128], BF16)
    ht = [C.sb(K("ht%d" % i), [128, D], F32) for i in range(3)]
    pt = [C.sb(K("pt%d" % i), [128, 256], F32) for i in range(2)]
    pb = [C.sb(K("pb%d" % i), [128, 256], BF16) for i in range(2)]
    pT = [C.sb(K("pT%d" % i), [128, 2, 128], BF16) for i in range(4)]
    hnew = [C.sb(K("hnew%d" % i), [128, D], F32) for i in range(3)]
    hb = [C.sb(K("hb%d" % i), [128, D], BF16) for i in range(2)]
    hT = [C.sb(K("hT%d" % i), [128, 8, 128], BF16) for i in range(2)]
    sig = C.sb(K("sig"), [128, D], F32)
    tmp = C.sb(K("tmp"), [128, D], F32)
    h2 = [C.sb(K("h2%d" % i), [128, D], F32) for i in range(2)]
    PT1 = C.ps(K("ps_t1"), [128, 1024], BF16)
    PT2 = C.ps(K("ps_t2"), [128, 8, 128], BF16)
    PPS = C.ps(K("pps"), [128, 512], F32)
    BO = [C.ps(K("bo%d" % i), [128, 512], F32) for i in range(2)]
    BG = [C.ps(K("bg%d" % i), [128, 512], F32) for i in range(2)]
    BP = C.ps(K("bp"), [128, 512], F32)
    if last:
        gfin = C.sb(K("gfin"), [128, D], F32)
        junk = C.sb(K("junk"), [128, D], F32)
        ss = C.sb(K("ss"), [128, 1], F32)
        epst = C.sb(K("epst"), [128, 1], F32)
        S.dma("sp", lambda e: e.dma_start(out=gfin[:], in_=t["final_g"].partition_broadcast(128)), writes=[K("gfin")])
        S.pool(lambda e: e.memset(epst[:], EPS), writes=[K("epst")])

    ld = lambda dst, src, key, q="sp": S.dma(q, lambda e: e.dma_start(out=dst, in_=src), writes=[key])
    ld(idt[:], t["ident"], K("idt"))
    ld(hs[:], halosel, K("hs"))
    S.dma("sp", lambda e: e.dma_start(out=XBl[:], in_=XBlast.rearrange("s p c -> p s c")), reads=list(t.get("xkeys", [])),
          writes=[K("XBl")])
    ld(Af[:], Afirst.rearrange("g s t -> s g t"), K("Af"))
    ld(Am[:], Amat.rearrange("g a s t -> s g a t"), K("Am"))
    ld(tril[:], trilm, K("tril"))
    ld(swf[:], t["sgu_w"].rearrange("h t s -> t h s"), K("swf"))
    ld(bsf[:], t["sgu_b"], K("bsf"))
    ld(psc[:], t["pscale"], K("psc"))
    S.pool(lambda e: e.memset(wbdf[:], 0.0), writes=[K("wbdf")])
    for g in range(4):
        c, gi = g // 2, g % 2
        S.dma("sp", lambda e, g=g, c=c, gi=gi: e.dma_start(out=wbdf[gi * 64:(gi + 1) * 64, c, gi * 64:(gi + 1) * 64],
                                                           in_=t["pool_w"][g]),
              reads=[K("wbdf")], writes=[K("wbdf%d" % g)])
    S.dve(lambda e: e.tensor_copy(wbd[:], wbdf[:]), reads=[K("wbdf")] + [K("wbdf%d" % g) for g in range(4)],
          writes=[K("wbd")])
    S.pool(lambda e: e.memset(ones[:], 1.0), writes=[K("ones")])
    S.dve(lambda e: e.tensor_copy(bsb[:], bsf[:]), reads=[K("bsf")], writes=[K("bsb")])
    for hh in range(4):
        S.dve(lambda e, hh=hh: e.tensor_tensor(swb[:, hh, :], swf[:, hh, :], tril[:], ALU.mult),
              reads=[K("swf"), K("tril")], writes=[K("swb%d" % hh)])
        S.pe(lambda e, hh=hh: e.transpose(PT1[:, hh * 128:(hh + 1) * 128], swb[:, hh, :], idt[:]),
             reads=[K("swb%d" % hh), K("idt")], writes=[PK("pt1")])
    S.act(lambda e: e.copy(WsT[:], PT1[:, 0:512].rearrange("p (h t) -> p h t", t=128)), reads=[PK("pt1")],
          writes=[K("WsT")])
    XK = list(t.get("xkeys", []))
    Osrc = t["Osrc"] if "Osrc" in t else (lambda e, hp: Og[hp].rearrange("(k p) c -> p k c", p=128))
    okeys = t.get("okeys", {})

    def load_oall(hf):
        for s_ in range(4):
            S.dma("sp", lambda e, s_=s_: e.dma_start(out=Oall[:, hf * 8:(hf + 1) * 8, s_ * 128:(s_ + 1) * 128],
                                                     in_=Osrc(e, s_)[:, hf * 8:(hf + 1) * 8, :]),
                  reads=XK + list(okeys.get(hf, [])), writes=[K("Oall%d_%d" % (s_, hf))])

    load_oall(0)
    for hf in range(2):
        bs_, ts_ = slice(hf * 8, (hf + 1) * 8), slice(hf * 1024, (hf + 1) * 1024)
        ld(SGAs[:, bs_, :], SGA.rearrange("(k p) c -> p k c", p=128)[:, bs_, :], K("SGAs%d" % hf))
        ld(XBs[:, bs_, :], XB.rearrange("(k p) c -> p k c", p=128)[:, bs_, :], K("XBs%d" % hf))
        ld(VNs[:, bs_, :], VN.rearrange("(k p) c -> p k c", p=128)[:, bs_, :], K("VNs%d" % hf))
        ld(GBs[:, :, ts_], GBT.rearrange("(c p) t -> p c t", p=128)[:, :, ts_], K("GBs%d" % hf))
        ld(UGs[:, :, ts_], UGT.rearrange("(c p) t -> p c t", p=128)[:, :, ts_], K("UGs%d" % hf))
    if "after_loads" in t:
        t["after_loads"](S, [K(n_ + str(hf)) for n_ in ("SGAs", "XBs", "VNs", "GBs", "UGs") for hf in range(2)]
                         + [K("Oall%d_0" % s_) for s_ in range(4)])
    nw = 0
    for (src, dstt, nch, key) in ((t["w_out"], wob, 8, "wob"), (t["ple_gate_w"], wgb, 8, "wgb"), (t["ple_w"], wpb, 2, "wpb")):
        for c in range(nch if wpre is None else 0):
            b = nw % 4
            nw += 1
            S.dma("sp", lambda e, src=src, c=c, b=b: e.dma_start(out=wst[b][:], in_=src[c * 128:(c + 1) * 128, :]),
                  writes=[K("wst%d" % b)])
            if c % 2 == 0:
                S.act(lambda e, dstt=dstt, c=c, b=b: e.copy(dstt[:, c, :], wst[b][:]), reads=[K("wst%d" % b)],
                      writes=[K("%s%d" % (key, c))])
            else:
                S.dve(lambda e, dstt=dstt, c=c, b=b: e.tensor_copy(dstt[:, c, :], wst[b][:]), reads=[K("wst%d" % b)],
                      writes=[K("%s%d" % (key, c))])
    WOB = [K("wob%d" % c) for c in range(8)]
    WGB = [K("wgb%d" % c) for c in range(8)]
    WPB = [K("wpb%d" % c) for c in range(2)]
    S.dve(lambda e: e.tensor_scalar(xacc[:], XBl[:, 0, :], hs[:, 0:1], None, ALU.mult), reads=[K("XBl"), K("hs")],
          writes=[K("xacc")])
    for s in range(1, 4):
        S.dve(lambda e, s=s: e.scalar_tensor_tensor(xacc[:], XBl[:, s, :], hs[:, s:s + 1], xacc[:], ALU.mult, ALU.add),
              reads=[K("XBl"), K("hs"), K("xacc")], writes=[K("xacc")])
    S.dve(lambda e: e.tensor_copy(xbp0[:], xacc[:]), reads=[K("xacc")], writes=[K("xbp0")])

    hv = h.rearrange("(k p) f -> k p f", p=128)
    pv = p.rearrange("(k p) f -> k p f", p=128)
    ov = hout.rearrange("(k p) f -> k p f", p=128)

    def stageP1(blk):
        b2, b3, b4 = blk % 2, blk % 3, blk % 4
        HT, PTK = K("ht%d" % b3), K("ptk%d" % b2)
        S.dma("sp", lambda e: e.dma_start(out=ht[b3][:], in_=hv[blk]), writes=[HT])
        S.dma("sp", lambda e: e.dma_start(out=pt[b2][:], in_=pv[blk]), writes=[PTK])
        S.dve(lambda e: e.tensor_tensor(yab[b2][:], Oall[:, blk, :], SGAs[:, blk, :], ALU.mult),
              reads=[K("Oall%d_%d" % (s_, blk // 8)) for s_ in range(4)] + [K("SGAs%d" % (blk // 8))], writes=[K("yab%d" % b2)])
        S.act(lambda e: e.copy(pb[b2][:], pt[b2][:]), reads=[PTK], writes=[K("pb%d" % b2)])
        for c in range(4):
            S.pe(lambda e, c=c: e.transpose(PT1[:, c * 128:(c + 1) * 128], yab[b2][:, c * 128:(c + 1) * 128], idt[:]),
                 reads=[K("yab%d" % b2), K("idt")], writes=[PK("pt1")])
        for c in range(2):
            S.pe(lambda e, c=c: e.transpose(PT1[:, 512 + c * 128:512 + (c + 1) * 128], pb[b2][:, c * 128:(c + 1) * 128],
                                            idt[:]),
                 reads=[K("pb%d" % b2), K("idt")], writes=[PK("pt1")])
        S.act(lambda e: e.copy(ycT[b2][:, 0:4, :], PT1[:, 0:512].rearrange("p (c t) -> p c t", t=128)),
              reads=[PK("pt1")], writes=[K("ycT%da" % b2)])
        S.act(lambda e: e.copy(pT[b4][:], PT1[:, 512:768].rearrange("p (c t) -> p c t", t=128)), reads=[PK("pt1")],
              writes=[K("pT%d" % b4)])
        for c in range(2):
            for gi in range(2):
                g = 2 * c + gi
                osl = slice(gi * 64, (gi + 1) * 64)
                csl = slice(c * 128, (c + 1) * 128)
                a_d = Af[:, g, :] if blk == 0 else Am[:, g, 0, :]
                S.pe(lambda e, g=g, osl=osl, a_d=a_d, csl=csl: e.matmul(PPS[osl, csl], XBs[:, blk, g * 64:(g + 1) * 64], a_d,
                                                                       start=True, stop=False),
                     reads=[K("XBs%d" % (blk // 8)), K("Af"), K("Am")], writes=[PK("pps")])
                if blk == 0:
                    S.pe(lambda e, g=g, osl=osl, csl=csl: e.matmul(PPS[osl, csl], xbp0[64:128, g * 64:(g + 1) * 64],
                                                                   Am[64:128, g, 1, :], start=False, stop=True),
                         reads=[K("xbp0"), K("Am")], writes=[PK("pps")])
                else:
                    S.pe(lambda e, g=g, osl=osl, csl=csl: e.matmul(PPS[osl, csl], XBs[64:128, blk - 1, g * 64:(g + 1) * 64],
                                                                   Am[64:128, g, 1, :], start=False, stop=True),
                         reads=[K("XBs%d" % ((blk - 1) // 8)), K("Am")], writes=[PK("pps")])
        for c in range(2):
            for hi in range(2):
                hh = 2 * c + hi
                osl = slice(hi * 64, (hi + 1) * 64)
                csl = slice(256 + c * 128, 256 + (c + 1) * 128)
                S.pe(lambda e, hh=hh, osl=osl, csl=csl: e.matmul(PPS[osl, csl], VNs[:, blk, hh * 64:(hh + 1) * 64],
                                                                 WsT[:, hh, :], start=True, stop=False),
                     reads=[K("VNs%d" % (blk // 8)), K("WsT")], writes=[PK("pps")])
                S.pe(lambda e, hh=hh, osl=osl, csl=csl: e.matmul(PPS[osl, csl], ones[0:1, :],
                                                                 bsb[0:1, hh * 128:(hh + 1) * 128], start=False, stop=True),
                     reads=[K("ones"), K("bsb")], writes=[PK("pps")])
        tsl = slice(blk * 128, (blk + 1) * 128)
        S.dve(lambda e: e.tensor_copy(plsb[:], PPS[:, 0:256].rearrange("p (c t) -> p c t", t=128)), reads=[PK("pps")],
              writes=[K("plsb")])
        for c in range(2):
            S.dve(lambda e, c=c: e.tensor_tensor(ycT[b2][:, 6 + c, :], PPS[:, 256 + c * 128:256 + (c + 1) * 128],
                                                 UGs[:, c, tsl], ALU.mult),
                  reads=[PK("pps"), K("UGs%d" % (blk // 8))], writes=[K("ycT%dc%d" % (b2, c))])

    def stageP2(blk):
        b2 = blk % 2
        tsl = slice(blk * 128, (blk + 1) * 128)
        for c in range(2):
            S.pe(lambda e, c=c: e.matmul(PPS[:, c * 128:(c + 1) * 128], wbd[:, c, :], plsb[:, c, :], start=True, stop=True),
                 reads=[K("wbd"), K("plsb")], writes=[PK("pps")])
        for c in range(2):
            S.dve(lambda e, c=c: e.scalar_tensor_tensor(ycT[b2][:, 4 + c, :], PPS[:, c * 128:(c + 1) * 128], psc[:, c:c + 1],
                                                        GBs[:, c, tsl], ALU.mult, ALU.mult),
                  reads=[PK("pps"), K("psc"), K("GBs%d" % (blk // 8))], writes=[K("ycT%db%d" % (b2, c))])

    def stageO(blk):
        b2, b3 = blk % 2, blk % 3
        HT = K("ht%d" % b3)
        YC = K("ycT%d" % b2)
        YCALL = [YC + "a", YC + "b0", YC + "b1", YC + "c0", YC + "c1"]
        for nb in range(2):
            nsl = slice(nb * 512, (nb + 1) * 512)
            for c in range(8):
                S.pe(lambda e, nb=nb, c=c, nsl=nsl: e.matmul(BO[nb][:, :], ycT[b2][:, c, :], wob[:, c, nsl],
                                                             start=(c == 0), stop=(c == 7)),
                     reads=YCALL + [WOB[c]], writes=[PK("bo%d" % nb)])
            S.dve(lambda e, nb=nb, nsl=nsl: e.tensor_tensor(hnew[b3][:, nsl], BO[nb][:, :], ht[b3][:, nsl], ALU.add),
                  reads=[PK("bo%d" % nb), HT], writes=[K("hnew%d_%d" % (b3, nb))])
            S.act(lambda e, nb=nb, nsl=nsl: e.copy(hb[b2][:, nsl], hnew[b3][:, nsl]),
                  reads=[K("hnew%d_%d" % (b3, nb))], writes=[K("hb%d_%d" % (b2, nb))])

    def stageT(blk):
        b2 = blk % 2
        for c in range(8):
            S.pe(lambda e, c=c: e.transpose(PT2[:, c, :], hb[b2][:, c * 128:(c + 1) * 128], idt[:]),
                 reads=[K("hb%d_%d" % (b2, c // 4)), K("idt")], writes=[PK("pt2")])
        S.act(lambda e: e.copy(hT[b2][:], PT2[:]), reads=[PK("pt2")], writes=[K("hT%d" % b2)])

    def stageG(blk):
        b2, b3, b4 = blk % 2, blk % 3, blk % 4
        for nb in range(2):
            nsl = slice(nb * 512, (nb + 1) * 512)
            for c in range(8):
                S.pe(lambda e, nb=nb, c=c, nsl=nsl: e.matmul(BG[nb][:, :], hT[b2][:, c, :], wgb[:, c, nsl],
                                                             start=(c == 0), stop=(c == 7)),
                     reads=[K("hT%d" % b2), WGB[c]], writes=[PK("bg%d" % nb)])
            S.act(lambda e, nb=nb, nsl=nsl: e.activation(sig[:, nsl], BG[nb][:, :], AF.Sigmoid),
                  reads=[PK("bg%d" % nb)], writes=[K("sig%d" % nb)])
            for c in range(2):
                S.pe(lambda e, nb=nb, c=c, nsl=nsl: e.matmul(BP[:, :], pT[b4][:, c, :], wpb[:, c, nsl],
                                                             start=(c == 0), stop=(c == 1)),
                     reads=[K("pT%d" % b4), WPB[c]], writes=[PK("bp")])
            S.dve(lambda e, nb=nb, nsl=nsl: e.tensor_tensor(tmp[:, nsl], BP[:, :], sig[:, nsl], ALU.mult),
                  reads=[PK("bp"), K("sig%d" % nb)], writes=[K("tmp%d" % nb)])
            S.dve(lambda e, nb=nb, nsl=nsl: e.tensor_tensor(h2[b2][:, nsl], tmp[:, nsl], hnew[b3][:, nsl], ALU.add),
                  reads=[K("tmp%d" % nb), K("hnew%d_%d" % (b3, nb))], writes=[K("h2%d_%d" % (b2, nb))])
        H2 = [K("h2%d_%d" % (b2, nb)) for nb in range(2)]
        if last:
            S.act(lambda e: e.activation(junk[:], h2[b2][:], AF.Square, accum_out=ss[:]), reads=H2,
                  writes=[K("junk"), K("ss")])
            S.act(lambda e: e.activation(ss[:], ss[:], AF.Sqrt, bias=epst[:], scale=1.0 / D), reads=[K("ss"), K("epst")],
                  writes=[K("ss")])
            S.dve(lambda e: e.reciprocal(ss[:], ss[:]), reads=[K("ss")], writes=[K("ss")])
            S.dve(lambda e: e.scalar_tensor_tensor(h2[b2][:], h2[b2][:], ss[:, 0:1], gfin[:], ALU.mult, ALU.mult),
                  reads=H2 + [K("ss"), K("gfin")], writes=H2)
        S.dma("pool", lambda e: e.dma_start(out=ov[blk], in_=h2[b2][:]), reads=H2, writes=[K("hout%d" % blk)])

    for it in range(nblk + 3):
        if it == min(6, nblk - 1):
            if "mid_hook" in t:
                t["mid_hook"](S)
            load_oall(1)
        if it < nblk:
            stageP1(it)
        if 0 <= it - 1 < nblk:
            stageO(it - 1)
        if it < nblk:
            stageP2(it)
        if 0 <= it - 2 < nblk:
            stageT(it - 2)
        if 0 <= it - 3 < nblk:
            stageG(it - 3)


def build_b2(last=False, **kw):
    nc = bass.Bass("TRN2", target_bir_lowering=False)
    with ExitStack() as es:
        C = Ctx(nc, es)
        t = dict(
            Og=C.din("Og", [4, NTOK, 128], BF16), SGA=C.din("SGA", [NTOK, 512], BF16),
            VN=C.din("VN", [NTOK, 256], BF16), XB=C.din("XB", [NTOK, 256], BF16),
            GBT=C.din("GBT", [256, NTOK], BF16), UGT=C.din("UGT", [256, NTOK], BF16),
            XBlast=C.din("XBlast", [4, 128, 256], BF16), halosel=C.din("halosel", [128, 4], F32),
            Afirst=C.din("Afirst", [4, 128, 128], BF16), Amat=C.din("Amat", [4, 2, 128, 128], BF16),
            trilm=C.din("trilm", [128, 128], F32), h=C.din("h", [NTOK, D], F32), p=C.din("p", [NTOK, 256], F32),
            w_out=C.din("w_out", [D, D], F32), ple_w=C.din("ple_w", [256, D], F32),
            ple_gate_w=C.din("ple_gate_w", [D, D], F32), pool_w=C.din("pool_w", [4, 64, 64], F32),
            pscale=C.din("pscale", [128, 2], F32), sgu_w=C.din("sgu_w", [4, 128, 128], F32),
            sgu_b=C.din("sgu_b", [1, 512], F32), ident=C.din("ident", [128, 128], BF16),
            hout=C.dout("hout", [NTOK, D], F32))
        if last:
            t["final_g"] = C.din("final_g", [1, D], F32)
        S = Sched(nc)
        emit_phase_b2(nc, S, C, t, last=last, **kw)
        S.emit()
    return nc


POOL_W = (2, 4, 8, 16)


def make_consts_b2():
    c = {}
    s = np.arange(128)[:, None]
    tt = np.arange(128)[None, :]
    Amat = np.zeros((4, 2, 128, 128), np.float32)
    Afirst = np.zeros((4, 128, 128), np.float32)
    for g, w in enumerate(POOL_W):
        d = tt - s
        Amat[g, 0] = np.where((d >= 0) & (d < w), 1.0 / w, 0.0) - (d == 0)
        dp = tt + 128 - s
        Amat[g, 1] = np.where(dp < w, 1.0 / w, 0.0)
        Afirst[g] = np.where((d >= 0) & (d < w), 1.0 / np.minimum(tt + 1, w), 0.0) - (d == 0)
    c["Amat"] = Amat.astype(ml_dtypes.bfloat16)
    c["Afirst_seq0"] = Afirst.astype(ml_dtypes.bfloat16)
    c["Afirst_other"] = Amat[:, 0].astype(ml_dtypes.bfloat16)
    c["trilm"] = np.tril(np.ones((128, 128), np.float32))
    return c


GROUPS = [[0, 1, 2, 3], [4, 5, 6, 7]]
BYP = mybir.AluOpType.bypass


def build_fused(depth=2, skip=()):
    nc = bass.Bass("TRN2", target_bir_lowering=False)
    ext_in = lambda name, shape, dt: nc.dram_tensor(name, shape, dt, kind="ExternalInput").ap()
    loc = lambda name, shape, dt: nc.dram_tensor(name, shape, dt)
    x = ext_in("x", [NTOK, D], F32)
    p = ext_in("p", [depth, NTOK, 256], F32)
    norm_g = ext_in("norm_g", [depth, 128, 8], F32)
    w_in = ext_in("w_in", [depth, D, DIN], F32)
    w_out = ext_in("w_out", [depth, D, D], F32)
    rb = ext_in("rb", [32, 2], F32)
    pool_w = ext_in("pool_w", [depth, 4, 64, 64], F32)
    pscale = ext_in("pscale", [depth, 128, 2], F32)
    sgu_w = ext_in("sgu_w", [depth, 4, 128, 128], F32)
    sgu_b = ext_in("sgu_b", [depth, 1, 512], F32)
    ple_w = ext_in("ple_w", [depth, 256, D], F32)
    ple_gate_w = ext_in("ple_gate_w", [depth, D, D], F32)
    final_g = ext_in("final_g", [1, D], F32)
    halosel = ext_in("halosel", [128, 4], F32)
    Afirst = ext_in("Afirst", [4, 128, 128], BF16)
    consts = dict(ident=ext_in("ident", [128, 128], BF16), identf=ext_in("identf", [128, 128], F32),
                  Erows=ext_in("Erows", [32, S_LEN], BF16), onehot=ext_in("onehot", [33, NTAB], F32),
                  masks=ext_in("masks", [3, 128, 1024], F32), Amat=ext_in("Amat", [4, 2, 128, 128], BF16),
                  trilm=ext_in("trilm", [128, 128], F32))
    out = nc.dram_tensor("out", [NTOK, D], F32, kind="ExternalOutput").ap()
    hbuf = loc("hbuf", [NTOK, D], F32).ap()
    Qs2 = loc("Qs2", [2, 4, 128, 8, 128], BF16)
    KT2 = loc("KT2", [2, 8, 64, 1024], BF16)
    Vd2 = loc("Vd2", [2, 8, 128, 8, 64], BF16)
    XBl = loc("XBl", [128, 256], BF16)
    QsG = loc("QsG", [2, 4, 4, 128, 8, 128], BF16)
    KTG = loc("KTG", [2, 4, 4, 2, 64, 1024], BF16)
    VdG = loc("VdG", [2, 4, 4, 2, 128, 8, 64], BF16)
    XBlG = loc("XBlG", [4 * 128, 256], BF16)
    Os2 = loc("Os2", [2, 4, 1024, 128], BF16)
    OsG = loc("OsG", [2, 4, 4, 1024, 128], BF16)
    Qg = loc("Qg", [4, NTOK, 128], BF16)
    KTg = loc("KTg", [4, 2, 64, NTOK], BF16)
    Vdg = loc("Vdg", [4, 2, 128, 16, 64], BF16)
    Og = loc("Og", [4, NTOK, 128], BF16)
    SGA = loc("SGA", [NTOK, 512], BF16).ap()
    VN = loc("VN", [NTOK, 256], BF16).ap()
    XB = loc("XB", [NTOK, 256], BF16).ap()
    GBT = loc("GBT", [256, NTOK], BF16).ap()
    UGT = loc("UGT", [256, NTOK], BF16).ap()
    tabd = loc("tabd", [2, 128, NTAB], BF16).ap()

    rank_cache = {}

    def rank(e):
        key = (len(rank_cache_phase), str(e.engine))
        if key not in rank_cache:
            rank_cache[key] = e.partition_id() % 4
        return rank_cache[key]

    rank_cache_phase = []

    with ExitStack() as gs:
        P = SemPool(nc, gs)
        for i in range(depth):
            last = i == depth - 1
            with ExitStack() as es:
                C = Ctx(nc, es)
                S = Sched(nc, P)
                ta = dict(h=x if i == 0 else hbuf, w_in=w_in[i], g=norm_g[i], ident=consts["ident"], Qs2=Qs2.ap(),
                          SGA=SGA, VN=VN, XB=XB, KT2=KT2.ap(), Vd2=Vd2.ap(), GBT=GBT, UGT=UGT)
                def ag_(S_, src2d, dst2d, key, reads=()):
                    S_.cc(lambda e: e.collective_compute("AllGather", BYP, replica_groups=GROUPS, ins=[src2d.opt()],
                                                         outs=[dst2d.opt()]), reads=list(reads), writes=[key])

                deferred = []

                def xchg_half(S_, j, rq=(), rk=(), rv=()):
                    ag_(S_, Vd2.ap()[j].rearrange("h p k d -> (h p) (k d)"),
                        VdG.ap()[j].rearrange("s h l p k d -> (s h l p) (k d)"), "VdG%d" % j, rv)
                    ag_(S_, KT2.ap()[j].rearrange("h d t -> (h d) t"), KTG.ap()[j].rearrange("s h l d t -> (s h l d) t"),
                        "KTG%d" % j, rk)
                    ag_(S_, Qs2.ap()[j].rearrange("d p k c -> (d p) (k c)"),
                        QsG.ap()[j].rearrange("s h p k c -> (s h p) (k c)"), "QsG%d" % j, rq)
                    def copies():
                        S_.dma("sp", lambda e: e.dma_start(
                            out=Vdg.ap().rearrange("s l p k d -> s (l p) k d")[:, :, j * 8:(j + 1) * 8, :].rearrange(
                                "s r k d -> s r (k d)"),
                            in_=VdG.ap()[j, :, bass.ds(rank(e), 1)].rearrange("s o l p k d -> s (o l p) (k d)")),
                            reads=["VdG%d" % j], writes=["Vdg_%d" % j])
                    deferred.append(copies)

                def after_tile(lt, S_, K_):
                    if lt in (1, 3):
                        j = lt // 2
                        xchg_half(S_, j,
                                  rq=[K_("Qtok%d" % r_) for r_ in range(j * 1024, (j + 1) * 1024, 128)],
                                  rk=[K_("KT%d_%d" % (l_, i_)) for l_ in (2 * j, 2 * j + 1) for i_ in range(4)],
                                  rv=[K_("Vd%d_%d" % (l_, h_)) for l_ in (2 * j, 2 * j + 1) for h_ in range(8)])

                if "xa" not in skip:
                    ta["after_tile"] = after_tile

                def after_xb(S_, K_):
                    S_.dma("sp", lambda e: e.dma_start(out=XBl.ap(), in_=XB[NTOK - 128:NTOK, :]),
                           reads=[K_("XB%d" % (NTOK - 128))], writes=["XBl"])
                    ag_(S_, XBl.ap(), XBlG.ap(), "XBlG", reads=["XBl"])

                if "xa" not in skip:
                    ta["after_xb"] = after_xb

                def after_p2(n2):
                    if n2 == 2 and deferred:
                        deferred.pop(0)()

                ta["after_p2"] = after_p2
                if "a" not in skip:
                    emit_phase_a(nc, S, C, ta, pfx="a%d_" % i)
                else:
                    xchg_half(S, 0)
                    xchg_half(S, 1)
                    after_xb(S, lambda k_: k_)
                for fn_ in deferred:
                    fn_()
                S.drain_cc = False
                rank_cache_phase.append(1)
                S.emit()
            wscope = ExitStack()
            wpre = dict(wob=wscope.enter_context(nc.sbuf_tensor("wob_%d" % i, [128, 8, 1024], BF16)),
                        wgb=wscope.enter_context(nc.sbuf_tensor("wgb_%d" % i, [128, 8, 1024], BF16)),
                        wpb=wscope.enter_context(nc.sbuf_tensor("wpb_%d" % i, [128, 2, 1024], BF16)))
            with ExitStack() as es:
                C = Ctx(nc, es)
                S = Sched(nc, P)
                def after_epi(S_, K_, hl_, T_, i=i, wpre=wpre):
                    if hl_ == 0 and T_ == 4:
                        for (src_, key_, nch_) in ((w_out[i], "wob", 8), (ple_gate_w[i], "wgb", 8), (ple_w[i], "wpb", 2)):
                            for c_ in range(nch_):
                                S_.dma("pool", lambda e, src_=src_, key_=key_, c_=c_: e.dma_start(
                                    out=wpre[key_][:, c_, :], in_=src_[c_ * 128:(c_ + 1) * 128, :]),
                                    reads=[K_("Oall0_16")], writes=["%s%d_%d" % (key_, i, c_)])

                tb = dict(rb=rb, Os2=Os2.ap(), tabd=tabd, Qg=Qg.ap(), KTg=KTg.ap(), Vdg=Vdg.ap(), after_epi=after_epi,
                          **consts)
                tb["Qdirect"] = lambda e, j: QsG.ap()[j, :, bass.ds(rank(e), 1)].rearrange("s o p k c -> (o p) s (k c)")
                tb["Kdirect"] = lambda e, hl, j: KTG.ap()[j, :, bass.ds(rank(e), 1), hl].rearrange("s o d t -> (o d) s t")

                def after_out(S_, K_, th, nodep=False):
                    S_.cc(lambda e: e.collective_compute(
                        "AllGather", BYP, replica_groups=GROUPS, ins=[Os2.ap()[th].rearrange("d t c -> (d t) c").opt()],
                        outs=[OsG.ap()[th].rearrange("s h t c -> (s h t) c").opt()]),
                        reads=([] if nodep else [K_("Os%d_%d" % (d_, th)) for d_ in range(4)]), writes=["OsG%d" % th])
                    S_.dma("sp", lambda e: e.dma_start(
                        out=Og.ap()[:, th * 1024:(th + 1) * 1024, :],
                        in_=OsG.ap()[th, :, bass.ds(rank(e), 1)].rearrange("s o t c -> s (o t) c")),
                        reads=["OsG%d" % th], writes=["Og_%d" % th])

                tb["after_out"] = after_out
                if "b1" not in skip:
                    emit_phase_b1(nc, S, C, tb, pfx="b%d_" % i)
                else:
                    after_out(S, lambda k_: k_, 0)
                rank_cache_phase.append(1)
                S.emit()
            with ExitStack() as es:
                C = Ctx(nc, es)
                S = Sched(nc, P)
                tc_ = dict(SGA=SGA, VN=VN, XB=XB, GBT=GBT, UGT=UGT, XBlast=XBlG.ap().rearrange("(s p) c -> s p c", p=128),
                           halosel=halosel, Afirst=Afirst, h=x if i == 0 else hbuf, p=p[i], Og=Og.ap(),
                           hout=out if last else hbuf, w_out=w_out[i], ple_w=ple_w[i], ple_gate_w=ple_gate_w[i],
                           pool_w=pool_w[i], pscale=pscale[i], sgu_w=sgu_w[i], sgu_b=sgu_b[i], final_g=final_g,
                           **consts)
                def after_loads(S_, keys):
                    S_.cc(lambda e: e.collective_compute(
                        "AllGather", BYP, replica_groups=GROUPS, ins=[Os2.ap()[1].rearrange("d t c -> (d t) c").opt()],
                        outs=[OsG.ap()[1].rearrange("s h t c -> (s h t) c").opt()]), reads=keys, writes=["OsG1"])

                tc_["after_loads"] = after_loads
                if "b2" in skip:
                    after_loads(S, [])
                tc_["mid_hook"] = lambda S_: S_.dma("sp", lambda e: e.dma_start(
                    out=Og.ap()[:, 1024:2048, :],
                    in_=OsG.ap()[1, :, bass.ds(rank(e), 1)].rearrange("s o t c -> s (o t) c")),
                    reads=["OsG1"], writes=["Og_1"])
                tc_["okeys"] = {1: ["Og_1"]}
                tc_["wpre"] = wpre
                rank_cache_phase.append(1)
                if "b2" not in skip:
                    emit_phase_b2(nc, S, C, tc_, pfx="c%d_" % i, last=last)
                S.emit()
            wscope.close()
    return nc


_PROGS = {}


def kernel(x, p, norm_g, w_in, w_out, rel_bias, pool_w, pool_scale, sgu_w, sgu_b, ple_w, ple_gate_w, final_g):
    f32 = lambda a: np.ascontiguousarray(np.asarray(a, dtype=np.float32))
    x, p, norm_g, w_in, w_out, rel_bias = map(f32, (x, p, norm_g, w_in, w_out, rel_bias))
    pool_w, pool_scale, sgu_w, sgu_b, ple_w, ple_gate_w, final_g = map(
        f32, (pool_w, pool_scale, sgu_w, sgu_b, ple_w, ple_gate_w, final_g))
    depth = w_in.shape[0]
    cs = make_consts()
    cb = make_consts_b2()
    if "fused" not in _PROGS:
        _PROGS["fused"] = build_fused(depth)
    nc = _PROGS["fused"]
    shared = dict(
        norm_g=np.ascontiguousarray(norm_g.reshape(depth, 8, 128).transpose(0, 2, 1)), w_in=w_in, w_out=w_out,
        pool_w=pool_w, pscale=np.ascontiguousarray(pool_scale.reshape(depth, 2, 128).transpose(0, 2, 1)),
        sgu_w=sgu_w, sgu_b=np.ascontiguousarray(sgu_b.reshape(depth, 1, 512)), ple_w=ple_w, ple_gate_w=ple_gate_w,
        final_g=np.ascontiguousarray(final_g.reshape(1, D)), ident=cs["ident"], identf=cs["identf"],
        Erows=cs["Erows"], onehot=cs["onehot"], masks=cs["masks"], Amat=cb["Amat"], trilm=cb["trilm"])
    in_maps = []
    for c in range(NCORES):
        b, r = c // 4, c % 4
        d = dict(shared)
        d["x"] = np.ascontiguousarray(x[b, r * NTOK:(r + 1) * NTOK])
        d["p"] = np.ascontiguousarray(p[:, b, r * NTOK:(r + 1) * NTOK])
        d["rb"] = np.ascontiguousarray(rel_bias[:, 2 * r:2 * r + 2])
        d["halosel"] = np.ascontiguousarray(np.broadcast_to((np.arange(4) == r - 1).astype(np.float32)[None, :], (128, 4)))
        d["Afirst"] = cb["Afirst_seq0"] if r == 0 else cb["Afirst_other"]
        in_maps.append(d)
    res = run_bass_kernel_spmd(nc, in_maps, core_ids=list(range(NCORES))).results
    out = np.zeros(x.shape, np.float32)
    for c in range(NCORES):
        b, r = c // 4, c % 4
        out[b, r * NTOK:(r + 1) * NTOK] = np.asarray(res[c]["out"])
    return out
```

```python
import numpy as np
import ml_dtypes
from contextlib import ExitStack

import concourse.bass as bass
import concourse.mybir as mybir
from concourse.bass_utils import run_bass_kernel_spmd

F32 = mybir.dt.float32
BF16 = mybir.dt.bfloat16
AF = mybir.ActivationFunctionType
ALU = mybir.AluOpType
AX = mybir.AxisListType

NCORES = 8
S_LEN = 8192
D = 1024
DIN = 3328
NTOK = 2048
NTAB = 1280
TOFF = 511
NEGM = -30000.0
EPS = 1e-6
TL = [sorted([r, 7 - r, 8 + r, 15 - r]) for r in range(4)]
OWNER = {}
for _r in range(4):
    for _lt, _T in enumerate(TL[_r]):
        OWNER[_T] = (_r, _lt)


STRICT = True


class Op:
    __slots__ = ("eng", "fn", "reads", "writes", "dma", "deps", "signal", "count", "sem",
                 "target", "prev_target", "idx", "inc", "carry")

    def __init__(self, eng, fn, reads, writes, dma):
        self.eng = eng
        self.fn = fn
        self.reads = tuple(reads)
        self.writes = tuple(writes)
        self.dma = dma
        self.deps = []
        self.signal = dma
        self.count = 0
        self.sem = None
        self.target = 0
        self.prev_target = 0
        self.inc = 16
        self.carry = []


class SemPool:
    ENGS = ("pe", "act", "dve", "pool", "sp")

    def __init__(self, nc, es, n_dma_sems=12):
        self.n_dma_sems = n_dma_sems
        self.esem = {e: es.enter_context(nc.semaphore("s_" + e)) for e in self.ENGS}
        self.dsem = {(e, j): es.enter_context(nc.semaphore("d_%s_%d" % (e, j)))
                     for e in ("sp", "pool", "act", "cc") for j in range(n_dma_sems if e != "cc" else 4)}
        self.ecount = {e: 0 for e in self.ENGS}
        self.dcount = {k: 0 for k in self.dsem}
        self.dnext = {e: 0 for e in self.ENGS + ("cc",)}
        self.carry = {}


class Sched:
    ENGS = ("pe", "act", "dve", "pool", "sp")

    def __init__(self, nc, pool=None, n_dma_sems=12):
        self.nc = nc
        self.ops = []
        self.n_dma_sems = n_dma_sems
        self.pool_ = pool
        self.drain_cc = True

    def cc(self, fn, reads=(), writes=()):
        op = self.add("pool", fn, reads, writes, dma=True)
        op.inc = 1
        return op

    def add(self, eng, fn, reads=(), writes=(), dma=False):
        ex = [k for k in list(reads) + list(writes) if k.startswith("PS:")]
        reads = list(reads) + [k for k in ex if k not in reads]
        writes = list(writes) + [k for k in ex if k not in writes]
        op = Op(eng, fn, reads, writes, dma)
        op.idx = len(self.ops)
        self.ops.append(op)
        return op

    def pe(self, fn, reads=(), writes=()):
        return self.add("pe", fn, reads, writes)

    def act(self, fn, reads=(), writes=()):
        return self.add("act", fn, reads, writes)

    def dve(self, fn, reads=(), writes=()):
        return self.add("dve", fn, reads, writes)

    def pool(self, fn, reads=(), writes=()):
        return self.add("pool", fn, reads, writes)

    def dma(self, q, fn, reads=(), writes=()):
        return self.add(q, fn, reads, writes, dma=True)

    def analyze(self):
        last_writer = {}
        readers = {}
        for op in self.ops:
            raw = set()
            other = set()
            for k in op.reads:
                w = last_writer.get(k)
                if w is not None:
                    raw.add(w)
            for k in op.writes:
                w = last_writer.get(k)
                if w is not None:
                    other.add(w)
                for r in readers.get(k, ()):
                    other.add(r)
            deps = []
            for d in raw | other:
                if d is op:
                    continue
                if (not d.dma) and (not op.dma) and d.eng == op.eng:
                    if op.eng == "pe" or (d not in raw and not STRICT):
                        continue
                deps.append(d)
            deps.sort(key=lambda o: o.idx)
            if self.pool_ is not None:
                for k in op.reads:
                    if k not in last_writer and k in self.pool_.carry:
                        op.carry.append(self.pool_.carry[k])
            op.deps = deps
            for d in deps:
                d.signal = True
            for k in op.reads:
                readers.setdefault(k, []).append(op)
            for k in op.writes:
                last_writer[k] = op
                readers[k] = []
        P = self.pool_
        cnt = dict(P.ecount) if P else {e: 0 for e in self.ENGS}
        dcnt = dict(P.dnext) if P else {e: 0 for e in self.ENGS}
        self.dma_uses = dict(P.dcount) if P else {}
        for op in self.ops:
            if op.dma:
                if op.inc == 1 and P:
                    j = dcnt["cc"] % 4
                    dcnt["cc"] += 1
                    key = ("cc", j)
                else:
                    j = dcnt[op.eng] % self.n_dma_sems
                    dcnt[op.eng] += 1
                    key = (op.eng, j)
                prev = self.dma_uses.get(key, 0)
                op.sem = key
                op.prev_target = prev
                op.target = prev + op.inc
                self.dma_uses[key] = op.target
            elif op.signal:
                cnt[op.eng] += 1
                op.count = cnt[op.eng]
        if P:
            P.ecount = cnt
            P.dnext = dcnt
            P.dcount = dict(self.dma_uses)
            for op in self.ops:
                if op.dma and op.inc == 1:
                    for k in op.writes:
                        P.carry[k] = (op.sem, op.target)

    def emit(self):
        nc = self.nc
        self.analyze()
        with ExitStack() as es:
            if self.pool_ is not None:
                esem, dsem = self.pool_.esem, self.pool_.dsem
            else:
                esem = {e: es.enter_context(nc.semaphore("s_" + e)) for e in self.ENGS}
                dsem = {}
                for key in self.dma_uses:
                    dsem[key] = es.enter_context(nc.semaphore("d_%s_%d" % key))
            block = es.enter_context(nc.Block())
            per_eng = {e: [o for o in self.ops if o.eng == e] for e in self.ENGS}

            def run(eng_name, h):
                waited = {}

                def w(sem_key, sem, val):
                    if val <= 0 or waited.get(sem_key, 0) >= val:
                        return
                    waited[sem_key] = val
                    h.wait_ge(sem, val)

                for op in per_eng[eng_name]:
                    for d in op.deps:
                        if d.dma:
                            w(d.sem, dsem[d.sem], d.target)
                        else:
                            w(d.eng, esem[d.eng], d.count)
                    for (ck, ct) in op.carry:
                        w(ck, dsem[ck], ct)
                    if op.dma and op.prev_target > 0:
                        w(op.sem, dsem[op.sem], op.prev_target)
                    inst = op.fn(h)
                    if op.dma:
                        inst.then_inc(dsem[op.sem], op.inc)
                    elif op.signal:
                        inst.then_inc(esem[op.eng], 1)
                for key, tgt in self.dma_uses.items():
                    if key[0] == eng_name or (key[0] == "cc" and eng_name == "pool" and self.drain_cc):
                        w(key, dsem[key], tgt)

            @block.tensor
            def _(h):
                run("pe", h)

            @block.scalar
            def _(h):
                run("act", h)

            @block.vector
            def _(h):
                run("dve", h)

            @block.gpsimd
            def _(h):
                run("pool", h)

            @block.sync
            def _(h):
                run("sp", h)


class Ctx:
    def __init__(self, nc, es):
        self.nc = nc
        self.es = es

    def sb(self, name, shape, dt):
        return self.es.enter_context(self.nc.sbuf_tensor(name, shape, dt))

    def ps(self, name, shape, dt):
        return self.es.enter_context(self.nc.psum_tensor(name, shape, dt))

    def din(self, name, shape, dt):
        return self.nc.dram_tensor(name, shape, dt, kind="ExternalInput").ap()

    def dout(self, name, shape, dt):
        return self.nc.dram_tensor(name, shape, dt, kind="ExternalOutput").ap()


C_Q, C_K, C_V, C_GA, C_XB, C_GB, C_UC, C_VC, C_GC = 0, 512, 1024, 1536, 2048, 2304, 2560, 2816, 3072


def emit_phase_a(nc, S, C, t, pfx="", ntiles=4, nsub=4, dofm=True, stage=99):
    h, w_in, g, ident = t["h"], t["w_in"], t["g"], t["ident"]
    Qtok = t.get("Qtok")
    SGA, VN, XB, GBT, UGT = (t[k] for k in ("SGA", "VN", "XB", "GBT", "UGT"))
    KT, Vd = t.get("KT"), t.get("Vd")
    K = lambda s: pfx + s
    PK = lambda s: "PS:" + pfx + s
    wb = C.sb(K("wb"), [128, 8, DIN], BF16)
    wst = [C.sb(K("wst%d" % i), [128, DIN - C_GA], F32) for i in range(3)]
    gt = C.sb(K("gt"), [128, 8], F32)
    idt = C.sb(K("idt"), [128, 128], BF16)
    epst = C.sb(K("epst"), [128, 1], F32)
    ht = [C.sb(K("ht%d" % i), [128, 4, D], F32) for i in range(2)]
    junk = C.sb(K("junk"), [128, D], F32)
    ss = [C.sb(K("ss%d" % i), [128, 4], F32) for i in range(2)]
    rstd = [C.sb(K("rstd%d" % i), [128, 4], F32) for i in range(2)]
    hn = C.sb(K("hn"), [128, 4, D], BF16)
    hnT = [C.sb(K("hnT%d" % i), [128, 8, 512], BF16) for i in range(4)]
    qst = [C.sb(K("qst%d" % i), [128, 512], BF16) for i in range(2)]
    gast = [C.sb(K("gast%d" % i), [128, 512], BF16) for i in range(2)]
    xbst = [C.sb(K("xbst%d" % i), [128, 256], BF16) for i in range(2)]
    vnst = [C.sb(K("vnst%d" % i), [128, 256], BF16) for i in range(2)]
    vst = [C.sb(K("vst%d" % i), [128, 4, 512], BF16) for i in range(2)]
    bst = C.sb(K("bst"), [128, 4, 6], F32)
    mv = C.sb(K("mv"), [128, 4, 2], F32)
    vrs = C.sb(K("vrs"), [128, 4], F32)
    fst = [C.sb(K("fst%d" % i), [128, 512], BF16) for i in range(3)]
    sgc = C.sb(K("sgc"), [128, 2, 512], F32)
    tp = C.ps(K("tp"), [128, 8, 128], BF16)
    pq = C.ps(K("pq"), [128, 512], F32)
    pv = C.ps(K("pv"), [128, 512], F32)
    pg = C.ps(K("pg"), [128, 512], F32)
    px = C.ps(K("px"), [128, 512], F32)
    pf = [C.ps(K("pf%d" % i), [128, 512], F32) for i in range(2)]

    S.dma("sp", lambda e: e.dma_start(out=idt[:], in_=ident), writes=[K("idt")])
    S.dma("sp", lambda e: e.dma_start(out=gt[:], in_=g), writes=[K("gt")])
    S.pool(lambda e: e.memset(epst[:], EPS), writes=[K("epst")])
    nws = 0
    for (c0, c1, tag) in ((0, C_GA, "q"), (C_GA, DIN, "r")):
        for c in range(8):
            b = nws % 3
            nws += 1
            S.dma("sp", lambda e, c=c, b=b, c0=c0, c1=c1: e.dma_start(out=wst[b][:, 0:c1 - c0],
                                                                      in_=w_in[c * 128:(c + 1) * 128, c0:c1]),
                  writes=[K("wst%d" % b)])
            if c % 2 == 0:
                S.act(lambda e, c=c, b=b, c0=c0, c1=c1: e.activation(wb[:, c, c0:c1], wst[b][:, 0:c1 - c0], AF.Copy,
                                                                     scale=gt[:, c:c + 1]),
                      reads=[K("wst%d" % b), K("gt")], writes=[K("wb%s%d" % (tag, c))])
            else:
                S.dve(lambda e, c=c, b=b, c0=c0, c1=c1: e.tensor_scalar(wb[:, c, c0:c1], wst[b][:, 0:c1 - c0],
                                                                        gt[:, c:c + 1], None, ALU.mult),
                      reads=[K("wst%d" % b), K("gt")], writes=[K("wb%s%d" % (tag, c))])
    WBQ = [K("wbq%d" % c) for c in range(8)]
    WBR = [K("wbr%d" % c) for c in range(8)]
    hv = h.rearrange("(l s p) f -> l p s f", s=4, p=128)
    state = {"nfm": 0}

    def norm_tile(lt):
        hb = lt % 2
        HT = K("ht%d" % hb)
        S.dma("sp", lambda e: e.dma_start(out=ht[hb][:], in_=hv[lt]), writes=[HT])
        for sub in range(4):
            S.act(lambda e, sub=sub: e.activation(junk[:], ht[hb][:, sub, :], AF.Square,
                                                  accum_out=ss[hb][:, sub:sub + 1]),
                  reads=[HT], writes=[K("junk"), K("ss%d" % hb)])
        S.act(lambda e: e.activation(rstd[hb][:], ss[hb][:], AF.Sqrt, bias=epst[:], scale=1.0 / D),
              reads=[K("ss%d" % hb), K("epst")], writes=[K("rstd%d" % hb)])
        S.dve(lambda e: e.reciprocal(rstd[hb][:], rstd[hb][:]), reads=[K("rstd%d" % hb)], writes=[K("rstd%d" % hb)])

    def tr_sub(lt, sub):
        hb = lt % 2
        HT = K("ht%d" % hb)
        S.dve(lambda e: e.tensor_scalar(hn[:, sub, :], ht[hb][:, sub, :], rstd[hb][:, sub:sub + 1], None, ALU.mult),
              reads=[HT, K("rstd%d" % hb)], writes=[K("hn%d" % sub)])
        for c in range(8):
            S.pe(lambda e, c=c: e.transpose(tp[:, c, :], hn[:, sub, c * 128:(c + 1) * 128], idt[:]),
                 reads=[K("hn%d" % sub), K("idt")], writes=[PK("tp")])
        S.act(lambda e: e.copy(hnT[lt][:, :, sub * 128:(sub + 1) * 128], tp[:]),
              reads=[PK("tp")], writes=[K("hnT%d_%d" % (lt, sub))])

    def proj_sub(lt, sub, part):
        hb = lt % 2
        sb2 = sub % 2
        HNT = K("hnT%d_%d" % (lt, sub))
        row = (lt * 4 + sub) * 128

        def proj(ps_ap, col, n, pskey):
            for c in range(8):
                S.pe(lambda e, c=c: e.matmul(ps_ap, hnT[lt][:, c, sub * 128:(sub + 1) * 128],
                                             wb[:, c, col:col + n], start=(c == 0), stop=(c == 7)),
                     reads=[HNT, (WBQ if col < C_GA else WBR)[c]], writes=[pskey])

        if part == 2:
            return proj_sub2(lt, sub, proj)
        proj(pq[:], C_Q, 512, PK("pq"))
        S.act(lambda e: e.copy(qst[sb2][:], pq[:]), reads=[PK("pq")], writes=[K("qst%d" % sb2)])
        if "Qs2" in t:
            S.dma("sp", lambda e: e.dma_start(
                out=t["Qs2"][row // 1024, :, :, (row % 1024) // 128, :].rearrange("d p c -> p d c"),
                in_=qst[sb2][:].rearrange("p (d c) -> p d c", c=128)),
                reads=[K("qst%d" % sb2)], writes=[K("Qtok%d" % row)])
        else:
            S.dma("sp", lambda e: e.dma_start(out=Qtok[row:row + 128, :], in_=qst[sb2][:]),
                  reads=[K("qst%d" % sb2)], writes=[K("Qtok%d" % row)])
        proj(pv[:], C_V, 512, PK("pv"))
        S.dve(lambda e: e.tensor_copy(vst[hb][:, sub, :], pv[:]), reads=[PK("pv")], writes=[K("vst%d_%d" % (hb, sub))])

    def proj_sub2(lt, sub, proj):
        sb2 = sub % 2
        row = (lt * 4 + sub) * 128
        proj(pg[:], C_GA, 512, PK("pg"))
        S.act(lambda e: e.activation(gast[sb2][:], pg[:], AF.Silu), reads=[PK("pg")], writes=[K("gast%d" % sb2)])
        S.dma("sp", lambda e: e.dma_start(out=SGA[row:row + 128, :], in_=gast[sb2][:]),
              reads=[K("gast%d" % sb2)], writes=[K("SGA%d" % row)])
        proj(px[:, 0:256], C_XB, 256, PK("px"))
        proj(px[:, 256:512], C_VC, 256, PK("px"))
        S.act(lambda e: e.copy(xbst[sb2][:], px[:, 0:256]), reads=[PK("px")], writes=[K("xbst%d" % sb2)])
        S.dma("sp", lambda e: e.dma_start(out=XB[row:row + 128, :], in_=xbst[sb2][:]),
              reads=[K("xbst%d" % sb2)], writes=[K("XB%d" % row)])
        S.dve(lambda e: e.bn_stats(bst[:, sub, :], px[:, 256:512]), reads=[PK("px")], writes=[K("bst")])
        S.dve(lambda e: e.bn_aggr(mv[:, sub, :], bst[:, sub, :]), reads=[K("bst")], writes=[K("mv")])
        S.act(lambda e: e.activation(vrs[:, sub:sub + 1], mv[:, sub, 1:2], AF.Sqrt, bias=epst[:], scale=1.0),
              reads=[K("mv"), K("epst")], writes=[K("vrs")])
        S.dve(lambda e: e.reciprocal(vrs[:, sub:sub + 1], vrs[:, sub:sub + 1]), reads=[K("vrs")], writes=[K("vrs")])
        S.dve(lambda e: e.tensor_scalar(vnst[sb2][:], px[:, 256:512], mv[:, sub, 0:1], vrs[:, sub:sub + 1],
                                        ALU.subtract, ALU.mult),
              reads=[PK("px"), K("mv"), K("vrs")], writes=[K("vnst%d" % sb2)])
        S.dma("sp", lambda e: e.dma_start(out=VN[row:row + 128, :], in_=vnst[sb2][:]),
              reads=[K("vnst%d" % sb2)], writes=[K("VN%d" % row)])

    def fm_tile(lt, part):
        hb = lt % 2
        for hh in range(8 if part == 1 else 0):
            S.dma("sp", lambda e, hh=hh: e.dma_start(
                out=(t["Vd2"][lt // 2, hh, :, (lt % 2) * 4:(lt % 2) * 4 + 4, :] if "Vd2" in t
                     else Vd[hh, :, lt * 4:(lt + 1) * 4, :]), in_=vst[hb][:, :, hh * 64:(hh + 1) * 64]),
                reads=[K("vst%d_%d" % (hb, s_)) for s_ in range(4)], writes=[K("Vd%d_%d" % (lt, hh))])
        HNTA = [K("hnT%d_%d" % (lt, s_)) for s_ in range(4)]
        tsl = slice(lt * 512, (lt + 1) * 512)

        def fproj(col):
            pb_ = state["nfm"] % 2
            fb_ = state["nfm"] % 3
            state["nfm"] += 1
            for c in range(8):
                S.pe(lambda e, c=c: e.matmul(pf[pb_][:], wb[:, c, col:col + 128], hnT[lt][:, c, :],
                                             start=(c == 0), stop=(c == 7)),
                     reads=HNTA + [(WBQ if col < C_GA else WBR)[c]], writes=[PK("pf%d" % pb_)])
            return pb_, fb_

        for i in range(4 if part == 1 else 0):
            pb_, fb_ = fproj(C_K + i * 128)
            S.dve(lambda e, pb_=pb_, fb_=fb_: e.tensor_copy(fst[fb_][:], pf[pb_][:]), reads=[PK("pf%d" % pb_)],
                  writes=[K("fst%d" % fb_)])
            S.dma("sp", lambda e, fb_=fb_, i=i: e.dma_start(
                out=(t["KT2"][lt // 2, 2 * i:2 * i + 2, :, (lt % 2) * 512:(lt % 2) * 512 + 512] if "KT2" in t
                     else KT[2 * i:2 * i + 2, :, tsl]).rearrange("h d t -> (h d) t"), in_=fst[fb_][:]),
                reads=[K("fst%d" % fb_)], writes=[K("KT%d_%d" % (lt, i))])
        if part == 1:
            return
        for i in range(2):
            pb_, fb_ = fproj(C_GB + i * 128)
            S.act(lambda e, pb_=pb_, fb_=fb_: e.activation(fst[fb_][:], pf[pb_][:], AF.Silu), reads=[PK("pf%d" % pb_)],
                  writes=[K("fst%d" % fb_)])
            S.dma("sp", lambda e, fb_=fb_, i=i: e.dma_start(out=GBT[i * 128:(i + 1) * 128, tsl], in_=fst[fb_][:]),
                  reads=[K("fst%d" % fb_)], writes=[K("GBT%d_%d" % (lt, i))])
        for i in range(2):
            pb_ = state["nfm"] % 2
            state["nfm"] += 1
            for c in range(8):
                S.pe(lambda e, c=c, i=i, pb_=pb_: e.matmul(pf[pb_][:], wb[:, c, C_GC + i * 128:C_GC + (i + 1) * 128],
                                                           hnT[lt][:, c, :], start=(c == 0), stop=(c == 7)),
                     reads=HNTA + [WBR[c]], writes=[PK("pf%d" % pb_)])
            S.act(lambda e, pb_=pb_, i=i: e.activation(sgc[:, i, :], pf[pb_][:], AF.Silu), reads=[PK("pf%d" % pb_)],
                  writes=[K("sgc%d" % i)])
        for i in range(2):
            pb_, fb_ = fproj(C_UC + i * 128)
            S.dve(lambda e, pb_=pb_, fb_=fb_, i=i: e.tensor_tensor(fst[fb_][:], pf[pb_][:], sgc[:, i, :], ALU.mult),
                  reads=[PK("pf%d" % pb_), K("sgc%d" % i)], writes=[K("fst%d" % fb_)])
            S.dma("sp", lambda e, fb_=fb_, i=i: e.dma_start(out=UGT[i * 128:(i + 1) * 128, tsl], in_=fst[fb_][:]),
                  reads=[K("fst%d" % fb_)], writes=[K("UGT%d_%d" % (lt, i))])

    norm_tile(0)
    tr_sub(0, 0)
    for lt in range(ntiles):
        if lt + 1 < ntiles:
            norm_tile(lt + 1)
        for sub in range(4):
            if sub < 3:
                tr_sub(lt, sub + 1)
            elif lt + 1 < ntiles:
                tr_sub(lt + 1, 0)
            proj_sub(lt, sub, 1)
        fm_tile(lt, 1)
        if "after_tile" in t:
            t["after_tile"](lt, S, K)
    for n2, lt in enumerate([ntiles - 1] + list(range(ntiles - 1))):
        for sub in range(4):
            proj_sub(lt, sub, 2)
        if lt == ntiles - 1 and "after_xb" in t:
            t["after_xb"](S, K)
        fm_tile(lt, 2)
        if "after_p2" in t:
            t["after_p2"](n2)


def build_a(**kw):
    nc = bass.Bass("TRN2", target_bir_lowering=False)
    with ExitStack() as es:
        C = Ctx(nc, es)
        t = dict(
            h=C.din("h", [NTOK, D], F32), w_in=C.din("w_in", [D, DIN], F32), g=C.din("g", [128, 8], F32),
            ident=C.din("ident", [128, 128], BF16),
            Qtok=C.dout("Qtok", [NTOK, 512], BF16), SGA=C.dout("SGA", [NTOK, 512], BF16),
            VN=C.dout("VN", [NTOK, 256], BF16), XB=C.dout("XB", [NTOK, 256], BF16),
            KT=C.dout("KT", [8, 64, NTOK], BF16), Vd=C.dout("Vd", [8, 128, 16, 64], BF16),
            GBT=C.dout("GBT", [256, NTOK], BF16), UGT=C.dout("UGT", [256, NTOK], BF16))
        S = Sched(nc)
        emit_phase_a(nc, S, C, t, **kw)
        S.emit()
    return nc


def emit_phase_b1(nc, S, C, t, pfx="", ntile=16, nheads=2):
    rb, Os = t["rb"], t.get("Os")
    if "Qsrc" in t:
        Qsrc, Ksrc, Vsrc = t["Qsrc"], t["Ksrc"], t["Vsrc"]
    else:
        Qg, KTg, Vdg = t["Qg"], t["KTg"], t["Vdg"]
        Qsrc = lambda e, s_: Qg[s_].rearrange("(k p) c -> p k c", p=128)
        Ksrc = lambda e, s_, hl: KTg[s_, hl]
        Vsrc = lambda e, s_, hl: Vdg[s_, hl]
    XK = list(t.get("xkeys", []))
    ident, identf, Erows, onehot, masks, tabd = (t[k] for k in ("ident", "identf", "Erows", "onehot", "masks", "tabd"))
    K = lambda s: pfx + s
    PK = lambda s: "PS:" + pfx + s
    idt = C.sb(K("idt"), [128, 128], BF16)
    idf = C.sb(K("idf"), [128, 128], F32)
    Kaug = [C.sb(K("Kaug%d" % i), [96, S_LEN], BF16) for i in range(2)]
    Vaug = [C.sb(K("Vaug%d" % i), [128, 64, 66], BF16) for i in range(2)]
    Qall = C.sb(K("Qall"), [128, 64, 128], BF16)
    Oall = C.sb(K("Oall"), [128, 64, 128], BF16)
    TB = C.sb(K("TB"), [128, 2, 6, 512], BF16)
    msk = C.sb(K("msk"), [128, 3, 1024], F32)
    ksf = C.sb(K("ksf"), [64, 32], F32)
    ksb = [C.sb(K("ksb%d" % i), [64, 32], BF16) for i in range(2)]
    rbx = C.sb(K("rbx"), [32, 2], F32)
    rbrep = [C.sb(K("rbrep%d" % i), [33, 128], F32) for i in range(2)]
    oh = C.sb(K("oh"), [33, NTAB], F32)
    tabf = C.sb(K("tabf"), [128, 2, NTAB], BF16)
    Qaug = [C.sb(K("Qaug%d" % i), [96, 512], BF16) for i in range(2)]
    gsb = C.sb(K("gsb"), [128, 4, 32], F32)
    top8 = C.sb(K("top8"), [128, 4, 8], F32)
    sel = C.sb(K("sel"), [128, 4, 32], F32)
    Mtok = C.sb(K("Mtok"), [128, 4, 32], BF16)
    PT = [C.sb(K("PT%d" % i), [128, 512], BF16) for i in range(4)]
    osb = [C.sb(K("osb%d" % i), [65, 512], F32) for i in range(2)]
    rden = C.sb(K("rden"), [128, 4], F32)
    SPS = [C.ps(K("sps%d" % i), [128, 512], F32) for i in range(4)]
    OT = [C.ps(K("ot%d" % i), [128, 512], F32) for i in range(2)]
    PQM = C.ps(K("pqm"), [128, 1024], BF16)
    PGO = C.ps(K("pgo"), [128, 512], F32)

    S.dma("sp", lambda e: e.dma_start(out=idt[:], in_=ident), writes=[K("idt")])
    S.dma("sp", lambda e: e.dma_start(out=idf[:], in_=identf), writes=[K("idf")])
    S.dma("sp", lambda e: e.dma_start(out=msk[:], in_=masks.rearrange("m p n -> p m n")), writes=[K("msk")])
    S.dma("sp", lambda e: e.dma_start(out=oh[:], in_=onehot), writes=[K("oh")])
    if "Qdirect" in t:
        for j in range(2):
            S.dma("act", lambda e, j=j: e.dma_start(
                out=Qall[:, :, :].rearrange("p (s j k) c -> p s j (k c)", s=4, j=2, k=8)[:, :, j, :],
                in_=t["Qdirect"](e, j)), reads=XK + ["QsG%d" % j], writes=[K("Qall%d_%d" % (s_, j)) for s_ in range(4)])
    else:
        for s_ in range(4):
            S.dma("sp", lambda e, s_=s_: e.dma_start(out=Qall[:, s_ * 16:(s_ + 1) * 16, :], in_=Qsrc(e, s_)),
                  reads=XK, writes=[K("Qall%d_0" % s_), K("Qall%d_1" % s_)])
    for hl in range(nheads):
        S.dma("sp", lambda e, hl=hl: e.dma_start(out=Kaug[hl][64:96, :], in_=Erows), writes=[K("KE%d" % hl)])
        for s in range(4):
            for hf in range(2):
                if "Kdirect" in t:
                    if s == 0:
                        S.dma("act", lambda e, hl=hl, hf=hf: e.dma_start(
                            out=Kaug[hl][0:64, :].rearrange("p (s j t) -> p s j t", s=4, j=2)[:, :, hf, :],
                            in_=t["Kdirect"](e, hl, hf)), reads=XK + ["KTG%d" % hf],
                            writes=[K("Kd%d_%d_%d" % (hl, s_, hf)) for s_ in range(4)])
                else:
                    S.dma("sp", lambda e, hl=hl, s=s, hf=hf: e.dma_start(
                        out=Kaug[hl][0:64, s * 2048 + hf * 1024:s * 2048 + (hf + 1) * 1024],
                        in_=Ksrc(e, s, hl)[:, hf * 1024:(hf + 1) * 1024]), reads=XK,
                        writes=[K("Kd%d_%d_%d" % (hl, s, hf))])
                S.dma("sp", lambda e, hl=hl, s=s, hf=hf: e.dma_start(
                    out=Vaug[hl][:, s * 16 + hf * 8:s * 16 + (hf + 1) * 8, 0:64],
                    in_=Vsrc(e, s, hl)[:, hf * 8:(hf + 1) * 8, :]), reads=XK, writes=[K("Vd%d_%d_%d" % (hl, s, hf))])
        S.pool(lambda e, hl=hl: e.memset(Vaug[hl][:, :, 64:65], 1.0), writes=[K("V1%d" % hl)])
    S.dma("sp", lambda e: e.dma_start(out=rbx[0:32, :], in_=rb), writes=[K("rbx")])
    for hl in range(nheads):
        S.pool(lambda e, hl=hl: e.memset(rbrep[hl][:, :], 1.0), writes=[K("rbrep%d" % hl)])
        S.dve(lambda e, hl=hl: e.tensor_scalar(rbrep[hl][0:32, :], rbrep[hl][0:32, :], rbx[0:32, hl:hl + 1], None,
                                               ALU.mult),
              reads=[K("rbrep%d" % hl), K("rbx")], writes=[K("rbrep%d" % hl)])
        for ci, (c0, c1) in enumerate(((0, 512), (512, 1024), (1024, NTAB))):
            S.pe(lambda e, hl=hl, c0=c0, c1=c1: e.matmul(SPS[0][:, 0:c1 - c0], rbrep[hl][:, :], oh[:, c0:c1],
                                                         start=True, stop=True),
                 reads=[K("rbrep%d" % hl), K("oh")], writes=[PK("sps0")])
            S.dve(lambda e, hl=hl, c0=c0, c1=c1: e.tensor_copy(tabf[:, hl, c0:c1], SPS[0][:, 0:c1 - c0]),
                  reads=[PK("sps0")], writes=[K("tabf%d_%d" % (hl, ci))])
        S.dma("sp", lambda e, hl=hl: e.dma_start(out=tabd[hl], in_=tabf[:, hl, :]),
              reads=[K("tabf%d_%d" % (hl, ci)) for ci in range(3)], writes=[K("tabd%d" % hl)])
        for i in range(6):
            Dv = 256 - 128 * i
            src = bass.AP(tensor=tabd.tensor, offset=hl * 128 * NTAB + TOFF + Dv, ap=[[NTAB - 1, 128], [1, 512]])
            S.dma("sp", lambda e, hl=hl, i=i, src=src: e.dma_start(out=TB[:, hl, i, :], in_=src),
                  reads=[K("tabd%d" % hl)], writes=[K("TB%d" % hl)])
    KALL = lambda hl: [K("KE%d" % hl)] + [K("Kd%d_%d_%d" % (hl, s, hf)) for s in range(4) for hf in range(2)]
    VALL = lambda hl: [K("V1%d" % hl)] + [K("Vd%d_%d_%d" % (hl, s, hf)) for s in range(4) for hf in range(2)]

    def build_steps(idx, hl, T):
        qb = idx % 2
        QA, QM = K("QA%d" % qb), K("QM%d" % qb)
        steps = []

        def s1():
            for sub in range(4):
                blk = T * 4 + sub
                S.pe(lambda e, sub=sub, blk=blk: e.transpose(PQM[0:64, sub * 128:(sub + 1) * 128],
                                                             Qall[:, blk, hl * 64:(hl + 1) * 64], idt[:]),
                     reads=[K("Qall%d_%d" % (blk // 16, (blk % 16) // 8)), K("idt")], writes=[PK("pqm")])
            S.act(lambda e: e.mul(Qaug[qb][0:64, :], PQM[0:64, 0:512], 0.125), reads=[PK("pqm")], writes=[QA])

        def s2():
            for sub in range(4):
                S.pe(lambda e, sub=sub: e.matmul(PGO[:, sub * 32:(sub + 1) * 32],
                                                 Qaug[qb][0:64, sub * 128:(sub + 1) * 128], ksb[hl][:, :],
                                                 start=True, stop=True),
                     reads=[QA, K("ksb%d" % hl)], writes=[PK("pgo")])

        def s3():
            for sub in range(4):
                own = 2 * T + sub // 2
                osl = slice(own * 32, (own + 1) * 32)
                S.dve(lambda e, sub=sub, osl=osl: e.tensor_tensor(gsb[:, sub, :], PGO[:, sub * 32:(sub + 1) * 32],
                                                                  msk[:, 0, osl], ALU.add),
                      reads=[PK("pgo"), K("msk")], writes=[K("gsb")])
                S.dve(lambda e, sub=sub: e.max(top8[:, sub, :], gsb[:, sub, :]), reads=[K("gsb")], writes=[K("top8")])
                S.dve(lambda e, sub=sub, osl=osl: e.scalar_tensor_tensor(sel[:, sub, :], gsb[:, sub, :],
                                                                         top8[:, sub, 2:3], msk[:, 1, osl],
                                                                         ALU.is_ge, ALU.mult),
                      reads=[K("gsb"), K("top8"), K("msk")], writes=[K("sel")])
                S.dve(lambda e, sub=sub, osl=osl: e.tensor_tensor(sel[:, sub, :], sel[:, sub, :], msk[:, 2, osl],
                                                                  ALU.add),
                      reads=[K("sel"), K("msk")], writes=[K("sel")])
                S.dve(lambda e, sub=sub: e.tensor_scalar(Mtok[:, sub, :], sel[:, sub, :], -1.0, -NEGM,
                                                         ALU.add, ALU.mult),
                      reads=[K("sel")], writes=[K("Mtok")])

        def s4():
            for sub in range(4):
                S.pe(lambda e, sub=sub: e.transpose(PQM[0:32, 512 + sub * 128:512 + (sub + 1) * 128],
                                                    Mtok[:, sub, :], idt[:]),
                     reads=[K("Mtok"), K("idt")], writes=[PK("pqm")])
            S.act(lambda e: e.copy(Qaug[qb][64:96, :], PQM[0:32, 512:1024]), reads=[PK("pqm")], writes=[QM])

        return [s1, s2, s3, s4]

    tiles = [(hl, T) for T in range(ntile) for hl in range(nheads)]
    if nheads < 2 or ntile < 16:
        S.pool(lambda e: e.memset(Oall[:], 0.0),
               writes=[K("Oall%d_%d" % (h_, b_)) for h_ in range(2) for b_ in range(64)])
    state = {'nkt_total': 0}
    for hl in range(nheads):
        S.dve(lambda e, hl=hl: e.tensor_reduce(ksf[:, :], Kaug[hl][0:64, :].rearrange("p (j k) -> p j k", k=256),
                                               AX.X, ALU.add),
              reads=KALL(hl), writes=[K("ksf")])
        S.dve(lambda e, hl=hl: e.tensor_copy(ksb[hl][:, :], ksf[:, :]), reads=[K("ksf")], writes=[K("ksb%d" % hl)])

    NB = len(SPS)
    LAG = NB - 1
    steps = []
    for idx, (hl, T) in enumerate(tiles):
        nkt = 4 * T + 4
        for kt in range(nkt):
            steps.append((idx, hl, T, kt, nkt))
    first_step = {}
    for gi, st in enumerate(steps):
        first_step.setdefault(st[0], gi)
    actions = {}

    def at(gi, fn):
        actions.setdefault(max(gi, 0), []).append(fn)

    for st in build_steps(0, *tiles[0]):
        at(0, st)
    for idx in range(1, len(tiles)):
        f0 = first_step[idx - 1]
        n_prev = first_step[idx] - f0
        ready_by = first_step[idx] - LAG
        bs = build_steps(idx, *tiles[idx])
        offs = (0, 2, 3, 6)
        for k_, st in enumerate(bs):
            at(min(f0 + offs[k_], ready_by), st)

    def col0(T, kt):
        return max(0, kt - 4 * T) * 128

    def emit_qk(gi):
        idx, hl, T, kt, nkt = steps[gi]
        sb = gi % NB
        qb = idx % 2
        QA, QM = K("QA%d" % qb), K("QM%d" % qb)
        special = kt >= 4 * T - 1
        c0 = col0(T, kt)
        S.pe(lambda e: e.matmul(SPS[sb][:, c0:512], Kaug[hl][0:96, kt * 128:(kt + 1) * 128], Qaug[qb][0:96, c0:512],
                                start=True, stop=not special),
             reads=KALL(hl) + [QA, QM], writes=[PK("sps%d" % sb)])
        if special:
            i = kt - (4 * T - 2)
            S.pe(lambda e: e.matmul(SPS[sb][:, c0:512], idt[:, :], TB[:, hl, i, c0:512], start=False, stop=True),
                 reads=[K("idt"), K("TB%d" % hl)], writes=[PK("sps%d" % sb)])
        S.act(lambda e: e.activation(PT[sb][:, c0:512], SPS[sb][:, c0:512], AF.Exp),
              reads=[PK("sps%d" % sb)], writes=[K("PT%d" % sb)])

    def emit_pv(gi):
        idx, hl, T, kt, nkt = steps[gi]
        sb = gi % NB
        ob = idx % 2
        c0 = col0(T, kt)
        S.pe(lambda e: e.matmul(OT[ob][0:65, c0:512], Vaug[hl][:, kt, 0:65], PT[sb][:, c0:512],
                                start=(kt == 0), stop=(kt == nkt - 1)),
             reads=VALL(hl) + [K("PT%d" % sb)], writes=[PK("ot%d" % ob)])
        if kt == nkt - 1:
            S.act(lambda e: e.copy(osb[ob][:, :], OT[ob][0:65, :]), reads=[PK("ot%d" % ob)], writes=[K("osb%d" % ob)])

            def epi2():
                for sub in range(4):
                    S.pe(lambda e, sub=sub: e.transpose(PGO[:, 128 + sub * 66:128 + sub * 66 + 65],
                                                        osb[ob][0:65, sub * 128:(sub + 1) * 128], idf[0:65, 0:65]),
                         reads=[K("osb%d" % ob), K("idf")], writes=[PK("pgo")])
                otp = PGO[:, 128:392].rearrange("p (s c) -> p s c", c=66)
                S.dve(lambda e: e.reciprocal(rden[:, :], otp[:, :, 64]), reads=[PK("pgo")], writes=[K("rden")])
                for sub in range(4):
                    blk = T * 4 + sub
                    S.dve(lambda e, sub=sub, blk=blk: e.tensor_scalar(Oall[:, blk, hl * 64:(hl + 1) * 64],
                                                                      otp[:, sub, 0:64], rden[:, sub:sub + 1], None,
                                                                      ALU.mult),
                          reads=[PK("pgo"), K("rden")], writes=[K("Oall%d_%d" % (hl, blk))])
                if "after_epi" in t:
                    t["after_epi"](S, K, hl, T)
                if "Os2" in t and hl == nheads - 1 and T % 2 == 1:
                    d_, th = T // 4, (T % 4) // 2
                    S.dma("sp", lambda e: e.dma_start(
                        out=t["Os2"][th, d_].rearrange("(k p) c -> p k c", p=128),
                        in_=Oall[:, d_ * 16 + th * 8:d_ * 16 + th * 8 + 8, :]),
                        reads=[K("Oall%d_%d" % (h_, b_)) for h_ in range(nheads)
                               for b_ in range(d_ * 16 + th * 8, d_ * 16 + th * 8 + 8)],
                        writes=[K("Os%d_%d" % (d_, th))])
                    if "after_out" in t and d_ == 3 and th == 0:
                        t["after_out"](S, K, th)
            at(gi + LAG + 3, epi2)

    nsteps = len(steps)
    for gi in range(nsteps + LAG + 4):
        for fn in actions.pop(gi, []):
            fn()
        if gi < nsteps:
            emit_qk(gi)
        if 0 <= gi - LAG < nsteps:
            emit_pv(gi - LAG)
    for gi in sorted(actions):
        for fn in actions[gi]:
            fn()
    OKEYS = [K("Oall%d_%d" % (hl, blk)) for hl in range(nheads) for blk in range(4 * ntile)]
    if "Os2" not in t:
        S.dma("sp", lambda e: e.dma_start(out=Os.rearrange("d (k p) c -> p (d k) c", p=128)[:, 0:4 * ntile, :],
                                          in_=Oall[:, 0:4 * ntile, :]), reads=OKEYS, writes=[K("Os")])


def build_b1(**kw):
    nc = bass.Bass("TRN2", target_bir_lowering=False)
    with ExitStack() as es:
        C = Ctx(nc, es)
        t = dict(
            Qg=C.din("Qg", [4, NTOK, 128], BF16), KTg=C.din("KTg", [4, 2, 64, NTOK], BF16),
            Vdg=C.din("Vdg", [4, 2, 128, 16, 64], BF16), rb=C.din("rb", [32, 2], F32),
            ident=C.din("ident", [128, 128], BF16), identf=C.din("identf", [128, 128], F32),
            Erows=C.din("Erows", [32, S_LEN], BF16), onehot=C.din("onehot", [33, NTAB], F32),
            masks=C.din("masks", [3, 128, 1024], F32),
            tabd=nc.dram_tensor("tabd", [2, 128, NTAB], BF16, kind="Internal").ap(),
            Os=C.dout("Os", [4, NTOK, 128], BF16))
        S = Sched(nc)
        emit_phase_b1(nc, S, C, t, **kw)
        S.emit()
    return nc


def t5_bucket_np(n):
    n = np.asarray(n)
    nf = np.maximum(n, 1).astype(np.float32)
    large = 16 + (np.log(nf / 16) / np.float32(np.log(128 / 16)) * 16).astype(np.int32)
    large = np.minimum(large, 31)
    return np.where(n < 16, n, large)


def make_consts():
    c = {}
    c["ident"] = np.eye(128, dtype=np.float32).astype(ml_dtypes.bfloat16)
    c["identf"] = np.eye(128, dtype=np.float32)
    E = np.zeros((32, S_LEN), np.float32)
    for j in range(32):
        E[j, j * 256:(j + 1) * 256] = 1.0
    c["Erows"] = E.astype(ml_dtypes.bfloat16)
    n = np.arange(NTAB) - TOFF
    bk = t5_bucket_np(np.maximum(n, 0))
    oh = np.zeros((33, NTAB), np.float32)
    oh[bk, np.arange(NTAB)] = 1.0
    oh[31, :] -= 1.0
    oh[:32, n < 0] = 0.0
    oh[32, n < 0] = NEGM
    c["onehot"] = oh
    j = np.arange(32)[None, :]
    o = np.arange(32)[:, None]
    m = np.zeros((3, 32, 32), np.float32)
    m[0] = np.where(j >= o, -1e30, 0.0)
    m[1] = (j < o)
    m[2] = (j == o)
    c["masks"] = np.ascontiguousarray(np.broadcast_to(m.reshape(3, 1, 1024), (3, 128, 1024)))
    return c


def emit_phase_b2(nc, S, C, t, pfx="", last=False, nblk=16):
    K = lambda s: pfx + s
    PK = lambda s: "PS:" + pfx + s
    Og = t.get("Og")
    SGA, VN, XB, GBT, UGT = (t[k] for k in ("SGA", "VN", "XB", "GBT", "UGT"))
    XBlast, halosel, Afirst, Amat, trilm = (t[k] for k in ("XBlast", "halosel", "Afirst", "Amat", "trilm"))
    h, p, hout = t["h"], t["p"], t["hout"]
    idt = C.sb(K("idt"), [128, 128], BF16)
    Oall = C.sb(K("Oall"), [128, 16, 512], BF16)
    SGAs = C.sb(K("SGAs"), [128, 16, 512], BF16)
    VNs = C.sb(K("VNs"), [128, 16, 256], BF16)
    XBs = C.sb(K("XBs"), [128, 16, 256], BF16)
    GBs = C.sb(K("GBs"), [128, 2, NTOK], BF16)
    UGs = C.sb(K("UGs"), [128, 2, NTOK], BF16)
    XBl = C.sb(K("XBl"), [128, 4, 256], BF16)
    hs = C.sb(K("hs"), [128, 4], F32)
    xacc = C.sb(K("xacc"), [128, 256], F32)
    xbp0 = C.sb(K("xbp0"), [128, 256], BF16)
    Af = C.sb(K("Af"), [128, 4, 128], BF16)
    Am = C.sb(K("Am"), [128, 4, 2, 128], BF16)
    wpre = t.get("wpre")
    if wpre is None:
        wst = [C.sb(K("wst%d" % i), [128, 1024], F32) for i in range(4)]
        wob = C.sb(K("wob"), [128, 8, 1024], BF16)
        wgb = C.sb(K("wgb"), [128, 8, 1024], BF16)
        wpb = C.sb(K("wpb"), [128, 2, 1024], BF16)
    else:
        wob, wgb, wpb = wpre["wob"], wpre["wgb"], wpre["wpb"]
    wbdf = C.sb(K("wbdf"), [128, 2, 128], F32)
    wbd = C.sb(K("wbd"), [128, 2, 128], BF16)
    tril = C.sb(K("tril"), [128, 128], F32)
    swf = C.sb(K("swf"), [128, 4, 128], F32)
    swb = C.sb(K("swb"), [128, 4, 128], BF16)
    WsT = C.sb(K("WsT"), [128, 4, 128], BF16)
    bsf = C.sb(K("bsf"), [1, 512], F32)
    bsb = C.sb(K("bsb"), [1, 512], BF16)
    ones = C.sb(K("ones"), [1, 64], BF16)
    psc = C.sb(K("psc"), [128, 2], F32)
    yab = [C.sb(K("yab%d" % i), [128, 512], BF16) for i in range(2)]
    ycT = [C.sb(K("ycT%d" % i), [128, 8, 128], BF16) for i in range(2)]
    plsb = C.sb(K("plsb"), [128, 2, 128], BF16)
    ht = [C.sb(K("ht%d" % i), [128, D], F32) for i in range(3)]
    pt = [C.sb(K("pt%d" % i), [128, 256], F32) for i in range(2)]
    pb = [C.sb(K("pb%d" % i), [128, 256], BF16) for i in range(2)]
    pT = [C.sb(K("pT%d" % i), [128, 2, 128], BF16) for i in range(4)]
    hnew = [C.sb(K("hnew%d" % i), [128, D], F32) for i in range(3)]
    hb = [C.sb(K("hb%d" % i), [128, D], BF16) for i in range(2)]
    hT = [C.sb(K("hT%d" % i), [128, 8, 128], BF16) for i in range(2)]
    sig = C.sb(K("sig"), [128, D], F32)
    tmp = C.sb(K("tmp"), [128, D], F32)
    h2 = [C.sb(K("h2%d" % i), [128, D], F32) for i in range(2)]
    PT1 = C.ps(K("ps_t1"), [128, 1024], BF16)
    PT2 = C.ps(K("ps_t2"), [128, 8, 128], BF16)
    PPS = C.ps(K("pps"), [128, 512], F32)
    BO = [C.ps(K("bo%d" % i), [128, 512], F32) for i in range(2)]
    BG = [C.ps(K("bg%d" % i), [128, 512], F32) for i in range(2)]
    BP = C.ps(K("bp"), [128, 512], F32)
    if last:
        gfin = C.sb(K("gfin"), [128, D], F32)
        junk = C.sb(K("junk"), [128, D], F32)
        ss = C.sb(K("ss"), [128, 1], F32)
        epst = C.sb(K("epst"), [128, 1], F32)
        S.dma("sp", lambda e: e.dma_start(out=gfin[:], in_=t["final_g"].partition_broadcast(128)), writes=[K("gfin")])
        S.pool(lambda e: e.memset(epst[:], EPS), writes=[K("epst")])

    ld = lambda dst, src, key, q="sp": S.dma(q, lambda e: e.dma_start(out=dst, in_=src), writes=[key])
    ld(idt[:], t["ident"], K("idt"))
    ld(hs[:], halosel, K("hs"))
    S.dma("sp", lambda e: e.dma_start(out=XBl[:], in_=XBlast.rearrange("s p c -> p s c")), reads=list(t.get("xkeys", [])),
          writes=[K("XBl")])
    ld(Af[:], Afirst.rearrange("g s t -> s g t"), K("Af"))
    ld(Am[:], Amat.rearrange("g a s t -> s g a t"), K("Am"))
    ld(tril[:], trilm, K("tril"))
    ld(swf[:], t["sgu_w"].rearrange("h t s -> t h s"), K("swf"))
    ld(bsf[:], t["sgu_b"], K("bsf"))
    ld(psc[:], t["pscale"], K("psc"))
    S.pool(lambda e: e.memset(wbdf[:], 0.0), writes=[K("wbdf")])
    for g in range(4):
        c, gi = g // 2, g % 2
        S.dma("sp", lambda e, g=g, c=c, gi=gi: e.dma_start(out=wbdf[gi * 64:(gi + 1) * 64, c, gi * 64:(gi + 1) * 64],
                                                           in_=t["pool_w"][g]),
              reads=[K("wbdf")], writes=[K("wbdf%d" % g)])
    S.dve(lambda e: e.tensor_copy(wbd[:], wbdf[:]), reads=[K("wbdf")] + [K("wbdf%d" % g) for g in range(4)],
          writes=[K("wbd")])
    S.pool(lambda e: e.memset(ones[:], 1.0), writes=[K("ones")])
    S.dve(lambda e: e.tensor_copy(bsb[:], bsf[:]), reads=[K("bsf")], writes=[K("bsb")])
    for hh in range(4):
        S.dve(lambda e, hh=hh: e.tensor_tensor(swb[:, hh, :], swf[:, hh, :], tril[:], ALU.mult),
              reads=[K("swf"), K("tril")], writes=[K("swb%d" % hh)])
        S.pe(lambda e, hh=hh: e.transpose(PT1[:, hh * 128:(hh + 1) * 128], swb[:, hh, :], idt[:]),
             reads=[K("swb%d" % hh), K("idt")], writes=[PK("pt1")])
    S.act(lambda e: e.copy(WsT[:], PT1[:, 0:512].rearrange("p (h t) -> p h t", t=128)), reads=[PK("pt1")],
          writes=[K("WsT")])
    XK = list(t.get("xkeys", []))
    Osrc = t["Osrc"] if "Osrc" in t else (lambda e, hp: Og[hp].rearrange("(k p) c -> p k c", p=128))
    okeys = t.get("okeys", {})

    def load_oall(hf):
        for s_ in range(4):
            S.dma("sp", lambda e, s_=s_: e.dma_start(out=Oall[:, hf * 8:(hf + 1) * 8, s_ * 128:(s_ + 1) * 128],
                                                     in_=Osrc(e, s_)[:, hf * 8:(hf + 1) * 8, :]),
                  reads=XK + list(okeys.get(hf, [])), writes=[K("Oall%d_%d" % (s_, hf))])

    load_oall(0)
    ld(SGAs[:], SGA.rearrange("(k p) c -> p k c", p=128), K("SGAs"))
    ld(XBs[:], XB.rearrange("(k p) c -> p k c", p=128), K("XBs"))
    ld(VNs[:], VN.rearrange("(k p) c -> p k c", p=128), K("VNs"))
    ld(GBs[:], GBT.rearrange("(c p) t -> p c t", p=128), K("GBs"))
    ld(UGs[:], UGT.rearrange("(c p) t -> p c t", p=128), K("UGs"))
    if "after_loads" in t:
        t["after_loads"](S, [K("SGAs"), K("XBs"), K("VNs"), K("GBs"), K("UGs")] + [K("Oall%d_0" % s_) for s_ in range(4)])
    nw = 0
    for (src, dstt, nch, key) in ((t["w_out"], wob, 8, "wob"), (t["ple_gate_w"], wgb, 8, "wgb"), (t["ple_w"], wpb, 2, "wpb")):
        for c in range(nch if wpre is None else 0):
            b = nw % 4
            nw += 1
            S.dma("sp", lambda e, src=src, c=c, b=b: e.dma_start(out=wst[b][:], in_=src[c * 128:(c + 1) * 128, :]),
                  writes=[K("wst%d" % b)])
            if c % 2 == 0:
                S.act(lambda e, dstt=dstt, c=c, b=b: e.copy(dstt[:, c, :], wst[b][:]), reads=[K("wst%d" % b)],
                      writes=[K("%s%d" % (key, c))])
            else:
                S.dve(lambda e, dstt=dstt, c=c, b=b: e.tensor_copy(dstt[:, c, :], wst[b][:]), reads=[K("wst%d" % b)],
                      writes=[K("%s%d" % (key, c))])
    WOB = [K("wob%d" % c) for c in range(8)]
    WGB = [K("wgb%d" % c) for c in range(8)]
    WPB = [K("wpb%d" % c) for c in range(2)]
    S.dve(lambda e: e.tensor_scalar(xacc[:], XBl[:, 0, :], hs[:, 0:1], None, ALU.mult), reads=[K("XBl"), K("hs")],
          writes=[K("xacc")])
    for s in range(1, 4):
        S.dve(lambda e, s=s: e.scalar_tensor_tensor(xacc[:], XBl[:, s, :], hs[:, s:s + 1], xacc[:], ALU.mult, ALU.add),
              reads=[K("XBl"), K("hs"), K("xacc")], writes=[K("xacc")])
    S.dve(lambda e: e.tensor_copy(xbp0[:], xacc[:]), reads=[K("xacc")], writes=[K("xbp0")])

    hv = h.rearrange("(k p) f -> k p f", p=128)
    pv = p.rearrange("(k p) f -> k p f", p=128)
    ov = hout.rearrange("(k p) f -> k p f", p=128)

    def stageP1(blk):
        b2, b3, b4 = blk % 2, blk % 3, blk % 4
        HT, PTK = K("ht%d" % b3), K("ptk%d" % b2)
        S.dma("sp", lambda e: e.dma_start(out=ht[b3][:], in_=hv[blk]), writes=[HT])
        S.dma("sp", lambda e: e.dma_start(out=pt[b2][:], in_=pv[blk]), writes=[PTK])
        S.dve(lambda e: e.tensor_tensor(yab[b2][:], Oall[:, blk, :], SGAs[:, blk, :], ALU.mult),
              reads=[K("Oall%d_%d" % (s_, blk // 8)) for s_ in range(4)] + [K("SGAs")], writes=[K("yab%d" % b2)])
        S.act(lambda e: e.copy(pb[b2][:], pt[b2][:]), reads=[PTK], writes=[K("pb%d" % b2)])
        for c in range(4):
            S.pe(lambda e, c=c: e.transpose(PT1[:, c * 128:(c + 1) * 128], yab[b2][:, c * 128:(c + 1) * 128], idt[:]),
                 reads=[K("yab%d" % b2), K("idt")], writes=[PK("pt1")])
        for c in range(2):
            S.pe(lambda e, c=c: e.transpose(PT1[:, 512 + c * 128:512 + (c + 1) * 128], pb[b2][:, c * 128:(c + 1) * 128],
                                            idt[:]),
                 reads=[K("pb%d" % b2), K("idt")], writes=[PK("pt1")])
        S.act(lambda e: e.copy(ycT[b2][:, 0:4, :], PT1[:, 0:512].rearrange("p (c t) -> p c t", t=128)),
              reads=[PK("pt1")], writes=[K("ycT%da" % b2)])
        S.act(lambda e: e.copy(pT[b4][:], PT1[:, 512:768].rearrange("p (c t) -> p c t", t=128)), reads=[PK("pt1")],
              writes=[K("pT%d" % b4)])
        for c in range(2):
            for gi in range(2):
                g = 2 * c + gi
                osl = slice(gi * 64, (gi + 1) * 64)
                csl = slice(c * 128, (c + 1) * 128)
                a_d = Af[:, g, :] if blk == 0 else Am[:, g, 0, :]
                S.pe(lambda e, g=g, osl=osl, a_d=a_d, csl=csl: e.matmul(PPS[osl, csl], XBs[:, blk, g * 64:(g + 1) * 64], a_d,
                                                                       start=True, stop=False),
                     reads=[K("XBs"), K("Af"), K("Am")], writes=[PK("pps")])
                if blk == 0:
                    S.pe(lambda e, g=g, osl=osl, csl=csl: e.matmul(PPS[osl, csl], xbp0[64:128, g * 64:(g + 1) * 64],
                                                                   Am[64:128, g, 1, :], start=False, stop=True),
                         reads=[K("xbp0"), K("Am")], writes=[PK("pps")])
                else:
                    S.pe(lambda e, g=g, osl=osl, csl=csl: e.matmul(PPS[osl, csl], XBs[64:128, blk - 1, g * 64:(g + 1) * 64],
                                                                   Am[64:128, g, 1, :], start=False, stop=True),
                         reads=[K("XBs"), K("Am")], writes=[PK("pps")])
        for c in range(2):
            for hi in range(2):
                hh = 2 * c + hi
                osl = slice(hi * 64, (hi + 1) * 64)
                csl = slice(256 + c * 128, 256 + (c + 1) * 128)
                S.pe(lambda e, hh=hh, osl=osl, csl=csl: e.matmul(PPS[osl, csl], VNs[:, blk, hh * 64:(hh + 1) * 64],
                                                                 WsT[:, hh, :], start=True, stop=False),
                     reads=[K("VNs"), K("WsT")], writes=[PK("pps")])
                S.pe(lambda e, hh=hh, osl=osl, csl=csl: e.matmul(PPS[osl, csl], ones[0:1, :],
                                                                 bsb[0:1, hh * 128:(hh + 1) * 128], start=False, stop=True),
                     reads=[K("ones"), K("bsb")], writes=[PK("pps")])
        tsl = slice(blk * 128, (blk + 1) * 128)
        S.dve(lambda e: e.tensor_copy(plsb[:], PPS[:, 0:256].rearrange("p (c t) -> p c t", t=128)), reads=[PK("pps")],
              writes=[K("plsb")])
        for c in range(2):
            S.dve(lambda e, c=c: e.tensor_tensor(ycT[b2][:, 6 + c, :], PPS[:, 256 + c * 128:256 + (c + 1) * 128],
                                                 UGs[:, c, tsl], ALU.mult),
                  reads=[PK("pps"), K("UGs")], writes=[K("ycT%dc%d" % (b2, c))])

    def stageP2(blk):
        b2 = blk % 2
        tsl = slice(blk * 128, (blk + 1) * 128)
        for c in range(2):
            S.pe(lambda e, c=c: e.matmul(PPS[:, c * 128:(c + 1) * 128], wbd[:, c, :], plsb[:, c, :], start=True, stop=True),
                 reads=[K("wbd"), K("plsb")], writes=[PK("pps")])
        for c in range(2):
            S.dve(lambda e, c=c: e.scalar_tensor_tensor(ycT[b2][:, 4 + c, :], PPS[:, c * 128:(c + 1) * 128], psc[:, c:c + 1],
                                                        GBs[:, c, tsl], ALU.mult, ALU.mult),
                  reads=[PK("pps"), K("psc"), K("GBs")], writes=[K("ycT%db%d" % (b2, c))])

    def stageO(blk):
        b2, b3 = blk % 2, blk % 3
        HT = K("ht%d" % b3)
        YC = K("ycT%d" % b2)
        YCALL = [YC + "a", YC + "b0", YC + "b1", YC + "c0", YC + "c1"]
        for nb in range(2):
            nsl = slice(nb * 512, (nb + 1) * 512)
            for c in range(8):
                S.pe(lambda e, nb=nb, c=c, nsl=nsl: e.matmul(BO[nb][:, :], ycT[b2][:, c, :], wob[:, c, nsl],
                                                             start=(c == 0), stop=(c == 7)),
                     reads=YCALL + [WOB[c]], writes=[PK("bo%d" % nb)])
            S.dve(lambda e, nb=nb, nsl=nsl: e.tensor_tensor(hnew[b3][:, nsl], BO[nb][:, :], ht[b3][:, nsl], ALU.add),
                  reads=[PK("bo%d" % nb), HT], writes=[K("hnew%d_%d" % (b3, nb))])
            S.act(lambda e, nb=nb, nsl=nsl: e.copy(hb[b2][:, nsl], hnew[b3][:, nsl]),
                  reads=[K("hnew%d_%d" % (b3, nb))], writes=[K("hb%d_%d" % (b2, nb))])

    def stageT(blk):
        b2 = blk % 2
        for c in range(8):
            S.pe(lambda e, c=c: e.transpose(PT2[:, c, :], hb[b2][:, c * 128:(c + 1) * 128], idt[:]),
                 reads=[K("hb%d_%d" % (b2, c // 4)), K("idt")], writes=[PK("pt2")])
        S.act(lambda e: e.copy(hT[b2][:], PT2[:]), reads=[PK("pt2")], writes=[K("hT%d" % b2)])

    def stageG(blk):
        b2, b3, b4 = blk % 2, blk % 3, blk % 4
        for nb in range(2):
            nsl = slice(nb * 512, (nb + 1) * 512)
            for c in range(8):
                S.pe(lambda e, nb=nb, c=c, nsl=nsl: e.matmul(BG[nb][:, :], hT[b2][:, c, :], wgb[:, c, nsl],
                                                             start=(c == 0), stop=(c == 7)),
                     reads=[K("hT%d" % b2), WGB[c]], writes=[PK("bg%d" % nb)])
            S.act(lambda e, nb=nb, nsl=nsl: e.activation(sig[:, nsl], BG[nb][:, :], AF.Sigmoid),
                  reads=[PK("bg%d" % nb)], writes=[K("sig%d" % nb)])
            for c in range(2):
                S.pe(lambda e, nb=nb, c=c, nsl=nsl: e.matmul(BP[:, :], pT[b4][:, c, :], wpb[:, c, nsl],
                                                             start=(c == 0), stop=(c == 1)),
                     reads=[K("pT%d" % b4), WPB[c]], writes=[PK("bp")])
            S.dve(lambda e, nb=nb, nsl=nsl: e.tensor_tensor(tmp[:, nsl], BP[:, :], sig[:, nsl], ALU.mult),
                  reads=[PK("bp"), K("sig%d" % nb)], writes=[K("tmp%d" % nb)])
            S.dve(lambda e, nb=nb, nsl=nsl: e.tensor_tensor(h2[b2][:, nsl], tmp[:, nsl], hnew[b3][:, nsl], ALU.add),
                  reads=[K("tmp%d" % nb), K("hnew%d_%d" % (b3, nb))], writes=[K("h2%d_%d" % (b2, nb))])
        H2 = [K("h2%d_%d" % (b2, nb)) for nb in range(2)]
        if last:
            S.act(lambda e: e.activation(junk[:], h2[b2][:], AF.Square, accum_out=ss[:]), reads=H2,
                  writes=[K("junk"), K("ss")])
            S.act(lambda e: e.activation(ss[:], ss[:], AF.Sqrt, bias=epst[:], scale=1.0 / D), reads=[K("ss"), K("epst")],
                  writes=[K("ss")])
            S.dve(lambda e: e.reciprocal(ss[:], ss[:]), reads=[K("ss")], writes=[K("ss")])
            S.dve(lambda e: e.scalar_tensor_tensor(h2[b2][:], h2[b2][:], ss[:, 0:1], gfin[:], ALU.mult, ALU.mult),
                  reads=H2 + [K("ss"), K("gfin")], writes=H2)
        S.dma("pool", lambda e: e.dma_start(out=ov[blk], in_=h2[b2][:]), reads=H2, writes=[K("hout%d" % blk)])

    for it in range(nblk + 3):
        if it == min(6, nblk - 1):
            if "mid_hook" in t:
                t["mid_hook"](S)
            load_oall(1)
        if it < nblk:
            stageP1(it)
        if 0 <= it - 1 < nblk:
            stageO(it - 1)
        if it < nblk:
            stageP2(it)
        if 0 <= it - 2 < nblk:
            stageT(it - 2)
        if 0 <= it - 3 < nblk:
            stageG(it - 3)


def build_b2(last=False, **kw):
    nc = bass.Bass("TRN2", target_bir_lowering=False)
    with ExitStack() as es:
        C = Ctx(nc, es)
        t = dict(
            Og=C.din("Og", [4, NTOK, 128], BF16), SGA=C.din("SGA", [NTOK, 512], BF16),
            VN=C.din("VN", [NTOK, 256], BF16), XB=C.din("XB", [NTOK, 256], BF16),
            GBT=C.din("GBT", [256, NTOK], BF16), UGT=C.din("UGT", [256, NTOK], BF16),
            XBlast=C.din("XBlast", [4, 128, 256], BF16), halosel=C.din("halosel", [128, 4], F32),
            Afirst=C.din("Afirst", [4, 128, 128], BF16), Amat=C.din("Amat", [4, 2, 128, 128], BF16),
            trilm=C.din("trilm", [128, 128], F32), h=C.din("h", [NTOK, D], F32), p=C.din("p", [NTOK, 256], F32),
            w_out=C.din("w_out", [D, D], F32), ple_w=C.din("ple_w", [256, D], F32),
            ple_gate_w=C.din("ple_gate_w", [D, D], F32), pool_w=C.din("pool_w", [4, 64, 64], F32),
            pscale=C.din("pscale", [128, 2], F32), sgu_w=C.din("sgu_w", [4, 128, 128], F32),
            sgu_b=C.din("sgu_b", [1, 512], F32), ident=C.din("ident", [128, 128], BF16),
            hout=C.dout("hout", [NTOK, D], F32))
        if last:
            t["final_g"] = C.din("final_g", [1, D], F32)
        S = Sched(nc)
        emit_phase_b2(nc, S, C, t, last=last, **kw)
        S.emit()
    return nc


POOL_W = (2, 4, 8, 16)


def make_consts_b2():
    c = {}
    s = np.arange(128)[:, None]
    tt = np.arange(128)[None, :]
    Amat = np.zeros((4, 2, 128, 128), np.float32)
    Afirst = np.zeros((4, 128, 128), np.float32)
    for g, w in enumerate(POOL_W):
        d = tt - s
        Amat[g, 0] = np.where((d >= 0) & (d < w), 1.0 / w, 0.0) - (d == 0)
        dp = tt + 128 - s
        Amat[g, 1] = np.where(dp < w, 1.0 / w, 0.0)
        Afirst[g] = np.where((d >= 0) & (d < w), 1.0 / np.minimum(tt + 1, w), 0.0) - (d == 0)
    c["Amat"] = Amat.astype(ml_dtypes.bfloat16)
    c["Afirst_seq0"] = Afirst.astype(ml_dtypes.bfloat16)
    c["Afirst_other"] = Amat[:, 0].astype(ml_dtypes.bfloat16)
    c["trilm"] = np.tril(np.ones((128, 128), np.float32))
    return c


GROUPS = [[0, 1, 2, 3], [4, 5, 6, 7]]
BYP = mybir.AluOpType.bypass


def build_fused(depth=2, skip=()):
    nc = bass.Bass("TRN2", target_bir_lowering=False)
    ext_in = lambda name, shape, dt: nc.dram_tensor(name, shape, dt, kind="ExternalInput").ap()
    loc = lambda name, shape, dt: nc.dram_tensor(name, shape, dt)
    x = ext_in("x", [NTOK, D], F32)
    p = ext_in("p", [depth, NTOK, 256], F32)
    norm_g = ext_in("norm_g", [depth, 128, 8], F32)
    w_in = ext_in("w_in", [depth, D, DIN], F32)
    w_out = ext_in("w_out", [depth, D, D], F32)
    rb = ext_in("rb", [32, 2], F32)
    pool_w = ext_in("pool_w", [depth, 4, 64, 64], F32)
    pscale = ext_in("pscale", [depth, 128, 2], F32)
    sgu_w = ext_in("sgu_w", [depth, 4, 128, 128], F32)
    sgu_b = ext_in("sgu_b", [depth, 1, 512], F32)
    ple_w = ext_in("ple_w", [depth, 256, D], F32)
    ple_gate_w = ext_in("ple_gate_w", [depth, D, D], F32)
    final_g = ext_in("final_g", [1, D], F32)
    halosel = ext_in("halosel", [128, 4], F32)
    Afirst = ext_in("Afirst", [4, 128, 128], BF16)
    consts = dict(ident=ext_in("ident", [128, 128], BF16), identf=ext_in("identf", [128, 128], F32),
                  Erows=ext_in("Erows", [32, S_LEN], BF16), onehot=ext_in("onehot", [33, NTAB], F32),
                  masks=ext_in("masks", [3, 128, 1024], F32), Amat=ext_in("Amat", [4, 2, 128, 128], BF16),
                  trilm=ext_in("trilm", [128, 128], F32))
    out = nc.dram_tensor("out", [NTOK, D], F32, kind="ExternalOutput").ap()
    hbuf = loc("hbuf", [NTOK, D], F32).ap()
    Qs2 = loc("Qs2", [2, 4, 128, 8, 128], BF16)
    KT2 = loc("KT2", [2, 8, 64, 1024], BF16)
    Vd2 = loc("Vd2", [2, 8, 128, 8, 64], BF16)
    XBl = loc("XBl", [128, 256], BF16)
    QsG = loc("QsG", [2, 4, 4, 128, 8, 128], BF16)
    KTG = loc("KTG", [2, 4, 4, 2, 64, 1024], BF16)
    VdG = loc("VdG", [2, 4, 4, 2, 128, 8, 64], BF16)
    XBlG = loc("XBlG", [4 * 128, 256], BF16)
    Os2 = loc("Os2", [2, 4, 1024, 128], BF16)
    OsG = loc("OsG", [2, 4, 4, 1024, 128], BF16)
    Qg = loc("Qg", [4, NTOK, 128], BF16)
    KTg = loc("KTg", [4, 2, 64, NTOK], BF16)
    Vdg = loc("Vdg", [4, 2, 128, 16, 64], BF16)
    Og = loc("Og", [4, NTOK, 128], BF16)
    SGA = loc("SGA", [NTOK, 512], BF16).ap()
    VN = loc("VN", [NTOK, 256], BF16).ap()
    XB = loc("XB", [NTOK, 256], BF16).ap()
    GBT = loc("GBT", [256, NTOK], BF16).ap()
    UGT = loc("UGT", [256, NTOK], BF16).ap()
    tabd = loc("tabd", [2, 128, NTAB], BF16).ap()

    rank_cache = {}

    def rank(e):
        key = (len(rank_cache_phase), str(e.engine))
        if key not in rank_cache:
            rank_cache[key] = e.partition_id() % 4
        return rank_cache[key]

    rank_cache_phase = []

    with ExitStack() as gs:
        P = SemPool(nc, gs)
        for i in range(depth):
            last = i == depth - 1
            with ExitStack() as es:
                C = Ctx(nc, es)
                S = Sched(nc, P)
                ta = dict(h=x if i == 0 else hbuf, w_in=w_in[i], g=norm_g[i], ident=consts["ident"], Qs2=Qs2.ap(),
                          SGA=SGA, VN=VN, XB=XB, KT2=KT2.ap(), Vd2=Vd2.ap(), GBT=GBT, UGT=UGT)
                def ag_(S_, src2d, dst2d, key, reads=()):
                    S_.cc(lambda e: e.collective_compute("AllGather", BYP, replica_groups=GROUPS, ins=[src2d.opt()],
                                                         outs=[dst2d.opt()]), reads=list(reads), writes=[key])

                deferred = []

                def xchg_half(S_, j, rq=(), rk=(), rv=()):
                    ag_(S_, Vd2.ap()[j].rearrange("h p k d -> (h p) (k d)"),
                        VdG.ap()[j].rearrange("s h l p k d -> (s h l p) (k d)"), "VdG%d" % j, rv)
                    ag_(S_, KT2.ap()[j].rearrange("h d t -> (h d) t"), KTG.ap()[j].rearrange("s h l d t -> (s h l d) t"),
                        "KTG%d" % j, rk)
                    ag_(S_, Qs2.ap()[j].rearrange("d p k c -> (d p) (k c)"),
                        QsG.ap()[j].rearrange("s h p k c -> (s h p) (k c)"), "QsG%d" % j, rq)
                    def copies():
                        S_.dma("sp", lambda e: e.dma_start(
                            out=Vdg.ap().rearrange("s l p k d -> s (l p) k d")[:, :, j * 8:(j + 1) * 8, :].rearrange(
                                "s r k d -> s r (k d)"),
                            in_=VdG.ap()[j, :, bass.ds(rank(e), 1)].rearrange("s o l p k d -> s (o l p) (k d)")),
                            reads=["VdG%d" % j], writes=["Vdg_%d" % j])
                    deferred.append(copies)

                def after_tile(lt, S_, K_):
                    if lt in (1, 3):
                        j = lt // 2
                        xchg_half(S_, j,
                                  rq=[K_("Qtok%d" % r_) for r_ in range(j * 1024, (j + 1) * 1024, 128)],
                                  rk=[K_("KT%d_%d" % (l_, i_)) for l_ in (2 * j, 2 * j + 1) for i_ in range(4)],
                                  rv=[K_("Vd%d_%d" % (l_, h_)) for l_ in (2 * j, 2 * j + 1) for h_ in range(8)])

                if "xa" not in skip:
                    ta["after_tile"] = after_tile

                def after_xb(S_, K_):
                    S_.dma("sp", lambda e: e.dma_start(out=XBl.ap(), in_=XB[NTOK - 128:NTOK, :]),
                           reads=[K_("XB%d" % (NTOK - 128))], writes=["XBl"])
                    ag_(S_, XBl.ap(), XBlG.ap(), "XBlG", reads=["XBl"])

                if "xa" not in skip:
                    ta["after_xb"] = after_xb

                def after_p2(n2):
                    if n2 == 2 and deferred:
                        deferred.pop(0)()

                ta["after_p2"] = after_p2
                if "a" not in skip:
                    emit_phase_a(nc, S, C, ta, pfx="a%d_" % i)
                else:
                    xchg_half(S, 0)
                    xchg_half(S, 1)
                    after_xb(S, lambda k_: k_)
                for fn_ in deferred:
                    fn_()
                S.drain_cc = False
                rank_cache_phase.append(1)
                S.emit()
            wscope = ExitStack()
            wpre = dict(wob=wscope.enter_context(nc.sbuf_tensor("wob_%d" % i, [128, 8, 1024], BF16)),
                        wgb=wscope.enter_context(nc.sbuf_tensor("wgb_%d" % i, [128, 8, 1024], BF16)),
                        wpb=wscope.enter_context(nc.sbuf_tensor("wpb_%d" % i, [128, 2, 1024], BF16)))
            with ExitStack() as es:
                C = Ctx(nc, es)
                S = Sched(nc, P)
                def after_epi(S_, K_, hl_, T_, i=i, wpre=wpre):
                    if hl_ == 0 and T_ == 4:
                        for (src_, key_, nch_) in ((w_out[i], "wob", 8), (ple_gate_w[i], "wgb", 8), (ple_w[i], "wpb", 2)):
                            for c_ in range(nch_):
                                S_.dma("pool", lambda e, src_=src_, key_=key_, c_=c_: e.dma_start(
                                    out=wpre[key_][:, c_, :], in_=src_[c_ * 128:(c_ + 1) * 128, :]),
                                    reads=[K_("Oall0_16")], writes=["%s%d_%d" % (key_, i, c_)])

                tb = dict(rb=rb, Os2=Os2.ap(), tabd=tabd, Qg=Qg.ap(), KTg=KTg.ap(), Vdg=Vdg.ap(), after_epi=after_epi,
                          **consts)
                tb["Qdirect"] = lambda e, j: QsG.ap()[j, :, bass.ds(rank(e), 1)].rearrange("s o p k c -> (o p) s (k c)")
                tb["Kdirect"] = lambda e, hl, j: KTG.ap()[j, :, bass.ds(rank(e), 1), hl].rearrange("s o d t -> (o d) s t")

                def after_out(S_, K_, th, nodep=False):
                    S_.cc(lambda e: e.collective_compute(
                        "AllGather", BYP, replica_groups=GROUPS, ins=[Os2.ap()[th].rearrange("d t c -> (d t) c").opt()],
                        outs=[OsG.ap()[th].rearrange("s h t c -> (s h t) c").opt()]),
                        reads=([] if nodep else [K_("Os%d_%d" % (d_, th)) for d_ in range(4)]), writes=["OsG%d" % th])
                    S_.dma("sp", lambda e: e.dma_start(
                        out=Og.ap()[:, th * 1024:(th + 1) * 1024, :],
                        in_=OsG.ap()[th, :, bass.ds(rank(e), 1)].rearrange("s o t c -> s (o t) c")),
                        reads=["OsG%d" % th], writes=["Og_%d" % th])

                tb["after_out"] = after_out
                if "b1" not in skip:
                    emit_phase_b1(nc, S, C, tb, pfx="b%d_" % i)
                else:
                    after_out(S, lambda k_: k_, 0)
                rank_cache_phase.append(1)
                S.emit()
            with ExitStack() as es:
                C = Ctx(nc, es)
                S = Sched(nc, P)
                tc_ = dict(SGA=SGA, VN=VN, XB=XB, GBT=GBT, UGT=UGT, XBlast=XBlG.ap().rearrange("(s p) c -> s p c", p=128),
                           halosel=halosel, Afirst=Afirst, h=x if i == 0 else hbuf, p=p[i], Og=Og.ap(),
                           hout=out if last else hbuf, w_out=w_out[i], ple_w=ple_w[i], ple_gate_w=ple_gate_w[i],
                           pool_w=pool_w[i], pscale=pscale[i], sgu_w=sgu_w[i], sgu_b=sgu_b[i], final_g=final_g,
                           **consts)
                def after_loads(S_, keys):
                    S_.cc(lambda e: e.collective_compute(
                        "AllGather", BYP, replica_groups=GROUPS, ins=[Os2.ap()[1].rearrange("d t c -> (d t) c").opt()],
                        outs=[OsG.ap()[1].rearrange("s h t c -> (s h t) c").opt()]), reads=keys, writes=["OsG1"])

                tc_["after_loads"] = after_loads
                if "b2" in skip:
                    after_loads(S, [])
                tc_["mid_hook"] = lambda S_: S_.dma("sp", lambda e: e.dma_start(
                    out=Og.ap()[:, 1024:2048, :],
                    in_=OsG.ap()[1, :, bass.ds(rank(e), 1)].rearrange("s o t c -> s (o t) c")),
                    reads=["OsG1"], writes=["Og_1"])
                tc_["okeys"] = {1: ["Og_1"]}
                tc_["wpre"] = wpre
                rank_cache_phase.append(1)
                if "b2" not in skip:
                    emit_phase_b2(nc, S, C, tc_, pfx="c%d_" % i, last=last)
                S.emit()
            wscope.close()
    return nc


_PROGS = {}


def kernel(x, p, norm_g, w_in, w_out, rel_bias, pool_w, pool_scale, sgu_w, sgu_b, ple_w, ple_gate_w, final_g):
    f32 = lambda a: np.ascontiguousarray(np.asarray(a, dtype=np.float32))
    x, p, norm_g, w_in, w_out, rel_bias = map(f32, (x, p, norm_g, w_in, w_out, rel_bias))
    pool_w, pool_scale, sgu_w, sgu_b, ple_w, ple_gate_w, final_g = map(
        f32, (pool_w, pool_scale, sgu_w, sgu_b, ple_w, ple_gate_w, final_g))
    depth = w_in.shape[0]
    cs = make_consts()
    cb = make_consts_b2()
    if "fused" not in _PROGS:
        _PROGS["fused"] = build_fused(depth)
    nc = _PROGS["fused"]
    shared = dict(
        norm_g=np.ascontiguousarray(norm_g.reshape(depth, 8, 128).transpose(0, 2, 1)), w_in=w_in, w_out=w_out,
        pool_w=pool_w, pscale=np.ascontiguousarray(pool_scale.reshape(depth, 2, 128).transpose(0, 2, 1)),
        sgu_w=sgu_w, sgu_b=np.ascontiguousarray(sgu_b.reshape(depth, 1, 512)), ple_w=ple_w, ple_gate_w=ple_gate_w,
        final_g=np.ascontiguousarray(final_g.reshape(1, D)), ident=cs["ident"], identf=cs["identf"],
        Erows=cs["Erows"], onehot=cs["onehot"], masks=cs["masks"], Amat=cb["Amat"], trilm=cb["trilm"])
    in_maps = []
    for c in range(NCORES):
        b, r = c // 4, c % 4
        d = dict(shared)
        d["x"] = np.ascontiguousarray(x[b, r * NTOK:(r + 1) * NTOK])
        d["p"] = np.ascontiguousarray(p[:, b, r * NTOK:(r + 1) * NTOK])
        d["rb"] = np.ascontiguousarray(rel_bias[:, 2 * r:2 * r + 2])
        d["halosel"] = np.ascontiguousarray(np.broadcast_to((np.arange(4) == r - 1).astype(np.float32)[None, :], (128, 4)))
        d["Afirst"] = cb["Afirst_seq0"] if r == 0 else cb["Afirst_other"]
        in_maps.append(d)
    res = run_bass_kernel_spmd(nc, in_maps, core_ids=list(range(NCORES))).results
    out = np.zeros(x.shape, np.float32)
    for c in range(NCORES):
        b, r = c // 4, c % 4
        out[b, r * NTOK:(r + 1) * NTOK] = np.asarray(res[c]["out"])
    return out
```

```python
import numpy as np
import ml_dtypes
from contextlib import ExitStack

import concourse.bass as bass
import concourse.mybir as mybir
from concourse.bass_utils import run_bass_kernel_spmd

F32 = mybir.dt.float32
BF16 = mybir.dt.bfloat16
AF = mybir.ActivationFunctionType
ALU = mybir.AluOpType
AX = mybir.AxisListType

NCORES = 8
S_LEN = 8192
D = 1024
DIN = 3328
NTOK = 2048
NTAB = 1280
TOFF = 511
NEGM = -30000.0
EPS = 1e-6
TL = [sorted([r, 7 - r, 8 + r, 15 - r]) for r in range(4)]
OWNER = {}
for _r in range(4):
    for _lt, _T in enumerate(TL[_r]):
        OWNER[_T] = (_r, _lt)


STRICT = True


class Op:
    __slots__ = ("eng", "fn", "reads", "writes", "dma", "deps", "signal", "count", "sem",
                 "target", "prev_target", "idx", "inc", "carry")

    def __init__(self, eng, fn, reads, writes, dma):
        self.eng = eng
        self.fn = fn
        self.reads = tuple(reads)
        self.writes = tuple(writes)
        self.dma = dma
        self.deps = []
        self.signal = dma
        self.count = 0
        self.sem = None
        self.target = 0
        self.prev_target = 0
        self.inc = 16
        self.carry = []


class SemPool:
    ENGS = ("pe", "act", "dve", "pool", "sp")

    def __init__(self, nc, es, n_dma_sems=12):
        self.n_dma_sems = n_dma_sems
        self.esem = {e: es.enter_context(nc.semaphore("s_" + e)) for e in self.ENGS}
        self.dsem = {(e, j): es.enter_context(nc.semaphore("d_%s_%d" % (e, j)))
                     for e in ("sp", "pool", "act", "cc") for j in range(n_dma_sems if e != "cc" else 4)}
        self.ecount = {e: 0 for e in self.ENGS}
        self.dcount = {k: 0 for k in self.dsem}
        self.dnext = {e: 0 for e in self.ENGS + ("cc",)}
        self.carry = {}


class Sched:
    ENGS = ("pe", "act", "dve", "pool", "sp")

    def __init__(self, nc, pool=None, n_dma_sems=12):
        self.nc = nc
        self.ops = []
        self.n_dma_sems = n_dma_sems
        self.pool_ = pool
        self.drain_cc = True

    def cc(self, fn, reads=(), writes=()):
        op = self.add("pool", fn, reads, writes, dma=True)
        op.inc = 1
        return op

    def add(self, eng, fn, reads=(), writes=(), dma=False):
        ex = [k for k in list(reads) + list(writes) if k.startswith("PS:")]
        reads = list(reads) + [k for k in ex if k not in reads]
        writes = list(writes) + [k for k in ex if k not in writes]
        op = Op(eng, fn, reads, writes, dma)
        op.idx = len(self.ops)
        self.ops.append(op)
        return op

    def pe(self, fn, reads=(), writes=()):
        return self.add("pe", fn, reads, writes)

    def act(self, fn, reads=(), writes=()):
        return self.add("act", fn, reads, writes)

    def dve(self, fn, reads=(), writes=()):
        return self.add("dve", fn, reads, writes)

    def pool(self, fn, reads=(), writes=()):
        return self.add("pool", fn, reads, writes)

    def dma(self, q, fn, reads=(), writes=()):
        return self.add(q, fn, reads, writes, dma=True)

    def analyze(self):
        last_writer = {}
        readers = {}
        for op in self.ops:
            raw = set()
            other = set()
            for k in op.reads:
                w = last_writer.get(k)
                if w is not None:
                    raw.add(w)
            for k in op.writes:
                w = last_writer.get(k)
                if w is not None:
                    other.add(w)
                for r in readers.get(k, ()):
                    other.add(r)
            deps = []
            for d in raw | other:
                if d is op:
                    continue
                if (not d.dma) and (not op.dma) and d.eng == op.eng:
                    if op.eng == "pe" or (d not in raw and not STRICT):
                        continue
                deps.append(d)
            deps.sort(key=lambda o: o.idx)
            if self.pool_ is not None:
                for k in op.reads:
                    if k not in last_writer and k in self.pool_.carry:
                        op.carry.append(self.pool_.carry[k])
            op.deps = deps
            for d in deps:
                d.signal = True
            for k in op.reads:
                readers.setdefault(k, []).append(op)
            for k in op.writes:
                last_writer[k] = op
                readers[k] = []
        P = self.pool_
        cnt = dict(P.ecount) if P else {e: 0 for e in self.ENGS}
        dcnt = dict(P.dnext) if P else {e: 0 for e in self.ENGS}
        self.dma_uses = dict(P.dcount) if P else {}
        for op in self.ops:
            if op.dma:
                if op.inc == 1 and P:
                    j = dcnt["cc"] % 4
                    dcnt["cc"] += 1
                    key = ("cc", j)
                else:
                    j = dcnt[op.eng] % self.n_dma_sems
                    dcnt[op.eng] += 1
                    key = (op.eng, j)
                prev = self.dma_uses.get(key, 0)
                op.sem = key
                op.prev_target = prev
                op.target = prev + op.inc
                self.dma_uses[key] = op.target
            elif op.signal:
                cnt[op.eng] += 1
                op.count = cnt[op.eng]
        if P:
            P.ecount = cnt
            P.dnext = dcnt
            P.dcount = dict(self.dma_uses)
            for op in self.ops:
                if op.dma and op.inc == 1:
                    for k in op.writes:
                        P.carry[k] = (op.sem, op.target)

    def emit(self):
        nc = self.nc
        self.analyze()
        with ExitStack() as es:
            if self.pool_ is not None:
                esem, dsem = self.pool_.esem, self.pool_.dsem
            else:
                esem = {e: es.enter_context(nc.semaphore("s_" + e)) for e in self.ENGS}
                dsem = {}
                for key in self.dma_uses:
                    dsem[key] = es.enter_context(nc.semaphore("d_%s_%d" % key))
            block = es.enter_context(nc.Block())
            per_eng = {e: [o for o in self.ops if o.eng == e] for e in self.ENGS}

            def run(eng_name, h):
                waited = {}

                def w(sem_key, sem, val):
                    if val <= 0 or waited.get(sem_key, 0) >= val:
                        return
                    waited[sem_key] = val
                    h.wait_ge(sem, val)

                for op in per_eng[eng_name]:
                    for d in op.deps:
                        if d.dma:
                            w(d.sem, dsem[d.sem], d.target)
                        else:
                            w(d.eng, esem[d.eng], d.count)
                    for (ck, ct) in op.carry:
                        w(ck, dsem[ck], ct)
                    if op.dma and op.prev_target > 0:
                        w(op.sem, dsem[op.sem], op.prev_target)
                    inst = op.fn(h)
                    if op.dma:
                        inst.then_inc(dsem[op.sem], op.inc)
                    elif op.signal:
                        inst.then_inc(esem[op.eng], 1)
                for key, tgt in self.dma_uses.items():
                    if key[0] == eng_name or (key[0] == "cc" and eng_name == "pool" and self.drain_cc):
                        w(key, dsem[key], tgt)

            @block.tensor
            def _(h):
                run("pe", h)

            @block.scalar
            def _(h):
                run("act", h)

            @block.vector
            def _(h):
                run("dve", h)

            @block.gpsimd
            def _(h):
                run("pool", h)

            @block.sync
            def _(h):
                run("sp", h)


class Ctx:
    def __init__(self, nc, es):
        self.nc = nc
        self.es = es

    def sb(self, name, shape, dt):
        return self.es.enter_context(self.nc.sbuf_tensor(name, shape, dt))

    def ps(self, name, shape, dt):
        return self.es.enter_context(self.nc.psum_tensor(name, shape, dt))

    def din(self, name, shape, dt):
        return self.nc.dram_tensor(name, shape, dt, kind="ExternalInput").ap()

    def dout(self, name, shape, dt):
        return self.nc.dram_tensor(name, shape, dt, kind="ExternalOutput").ap()


C_Q, C_K, C_V, C_GA, C_XB, C_GB, C_UC, C_VC, C_GC = 0, 512, 1024, 1536, 2048, 2304, 2560, 2816, 3072


def emit_phase_a(nc, S, C, t, pfx="", ntiles=4, nsub=4, dofm=True, stage=99):
    h, w_in, g, ident = t["h"], t["w_in"], t["g"], t["ident"]
    Qtok = t.get("Qtok")
    SGA, VN, XB, GBT, UGT = (t[k] for k in ("SGA", "VN", "XB", "GBT", "UGT"))
    KT, Vd = t.get("KT"), t.get("Vd")
    K = lambda s: pfx + s
    PK = lambda s: "PS:" + pfx + s
    wb = C.sb(K("wb"), [128, 8, DIN], BF16)
    wst = [C.sb(K("wst%d" % i), [128, DIN - C_GA], F32) for i in range(3)]
    gt = C.sb(K("gt"), [128, 8], F32)
    idt = C.sb(K("idt"), [128, 128], BF16)
    epst = C.sb(K("epst"), [128, 1], F32)
    ht = [C.sb(K("ht%d" % i), [128, 4, D], F32) for i in range(2)]
    junk = C.sb(K("junk"), [128, D], F32)
    ss = [C.sb(K("ss%d" % i), [128, 4], F32) for i in range(2)]
    rstd = [C.sb(K("rstd%d" % i), [128, 4], F32) for i in range(2)]
    hn = C.sb(K("hn"), [128, 4, D], BF16)
    hnT = [C.sb(K("hnT%d" % i), [128, 8, 512], BF16) for i in range(4)]
    qst = [C.sb(K("qst%d" % i), [128, 512], BF16) for i in range(2)]
    gast = [C.sb(K("gast%d" % i), [128, 512], BF16) for i in range(2)]
    xbst = [C.sb(K("xbst%d" % i), [128, 256], BF16) for i in range(2)]
    vnst = [C.sb(K("vnst%d" % i), [128, 256], BF16) for i in range(2)]
    vst = [C.sb(K("vst%d" % i), [128, 4, 512], BF16) for i in range(2)]
    bst = C.sb(K("bst"), [128, 4, 6], F32)
    mv = C.sb(K("mv"), [128, 4, 2], F32)
    vrs = C.sb(K("vrs"), [128, 4], F32)
    fst = [C.sb(K("fst%d" % i), [128, 512], BF16) for i in range(3)]
    sgc = C.sb(K("sgc"), [128, 2, 512], F32)
    tp = C.ps(K("tp"), [128, 8, 128], BF16)
    pq = C.ps(K("pq"), [128, 512], F32)
    pv = C.ps(K("pv"), [128, 512], F32)
    pg = C.ps(K("pg"), [128, 512], F32)
    px = C.ps(K("px"), [128, 512], F32)
    pf = [C.ps(K("pf%d" % i), [128, 512], F32) for i in range(2)]

    S.dma("sp", lambda e: e.dma_start(out=idt[:], in_=ident), writes=[K("idt")])
    S.dma("sp", lambda e: e.dma_start(out=gt[:], in_=g), writes=[K("gt")])
    S.pool(lambda e: e.memset(epst[:], EPS), writes=[K("epst")])
    nws = 0
    for (c0, c1, tag) in ((0, C_GA, "q"), (C_GA, DIN, "r")):
        for c in range(8):
            b = nws % 3
            nws += 1
            S.dma("sp", lambda e, c=c, b=b, c0=c0, c1=c1: e.dma_start(out=wst[b][:, 0:c1 - c0],
                                                                      in_=w_in[c * 128:(c + 1) * 128, c0:c1]),
                  writes=[K("wst%d" % b)])
            if c % 2 == 0:
                S.act(lambda e, c=c, b=b, c0=c0, c1=c1: e.activation(wb[:, c, c0:c1], wst[b][:, 0:c1 - c0], AF.Copy,
                                                                     scale=gt[:, c:c + 1]),
                      reads=[K("wst%d" % b), K("gt")], writes=[K("wb%s%d" % (tag, c))])
            else:
                S.dve(lambda e, c=c, b=b, c0=c0, c1=c1: e.tensor_scalar(wb[:, c, c0:c1], wst[b][:, 0:c1 - c0],
                                                                        gt[:, c:c + 1], None, ALU.mult),
                      reads=[K("wst%d" % b), K("gt")], writes=[K("wb%s%d" % (tag, c))])
    WBQ = [K("wbq%d" % c) for c in range(8)]
    WBR = [K("wbr%d" % c) for c in range(8)]
    hv = h.rearrange("(l s p) f -> l p s f", s=4, p=128)
    state = {"nfm": 0}

    def norm_tile(lt):
        hb = lt % 2
        HT = K("ht%d" % hb)
        S.dma("sp", lambda e: e.dma_start(out=ht[hb][:], in_=hv[lt]), writes=[HT])
        for sub in range(4):
            S.act(lambda e, sub=sub: e.activation(junk[:], ht[hb][:, sub, :], AF.Square,
                                                  accum_out=ss[hb][:, sub:sub + 1]),
                  reads=[HT], writes=[K("junk"), K("ss%d" % hb)])
        S.act(lambda e: e.activation(rstd[hb][:], ss[hb][:], AF.Sqrt, bias=epst[:], scale=1.0 / D),
              reads=[K("ss%d" % hb), K("epst")], writes=[K("rstd%d" % hb)])
        S.dve(lambda e: e.reciprocal(rstd[hb][:], rstd[hb][:]), reads=[K("rstd%d" % hb)], writes=[K("rstd%d" % hb)])

    def tr_sub(lt, sub):
        hb = lt % 2
        HT = K("ht%d" % hb)
        S.dve(lambda e: e.tensor_scalar(hn[:, sub, :], ht[hb][:, sub, :], rstd[hb][:, sub:sub + 1], None, ALU.mult),
              reads=[HT, K("rstd%d" % hb)], writes=[K("hn%d" % sub)])
        for c in range(8):
            S.pe(lambda e, c=c: e.transpose(tp[:, c, :], hn[:, sub, c * 128:(c + 1) * 128], idt[:]),
                 reads=[K("hn%d" % sub), K("idt")], writes=[PK("tp")])
        S.act(lambda e: e.copy(hnT[lt][:, :, sub * 128:(sub + 1) * 128], tp[:]),
              reads=[PK("tp")], writes=[K("hnT%d_%d" % (lt, sub))])

    def proj_sub(lt, sub, part):
        hb = lt % 2
        sb2 = sub % 2
        HNT = K("hnT%d_%d" % (lt, sub))
        row = (lt * 4 + sub) * 128

        def proj(ps_ap, col, n, pskey):
            for c in range(8):
                S.pe(lambda e, c=c: e.matmul(ps_ap, hnT[lt][:, c, sub * 128:(sub + 1) * 128],
                                             wb[:, c, col:col + n], start=(c == 0), stop=(c == 7)),
                     reads=[HNT, (WBQ if col < C_GA else WBR)[c]], writes=[pskey])

        if part == 2:
            return proj_sub2(lt, sub, proj)
        proj(pq[:], C_Q, 512, PK("pq"))
        S.act(lambda e: e.copy(qst[sb2][:], pq[:]), reads=[PK("pq")], writes=[K("qst%d" % sb2)])
        if "Qs2" in t:
            S.dma("act", lambda e: e.dma_start(
                out=t["Qs2"][row // 1024, :, :, (row % 1024) // 128, :].rearrange("d p c -> p d c"),
                in_=qst[sb2][:].rearrange("p (d c) -> p d c", c=128)),
                reads=[K("qst%d" % sb2)], writes=[K("Qtok%d" % row)])
        else:
            S.dma("sp", lambda e: e.dma_start(out=Qtok[row:row + 128, :], in_=qst[sb2][:]),
                  reads=[K("qst%d" % sb2)], writes=[K("Qtok%d" % row)])
        proj(pv[:], C_V, 512, PK("pv"))
        S.dve(lambda e: e.tensor_copy(vst[hb][:, sub, :], pv[:]), reads=[PK("pv")], writes=[K("vst%d_%d" % (hb, sub))])

    def proj_sub2(lt, sub, proj):
        sb2 = sub % 2
        row = (lt * 4 + sub) * 128
        proj(pg[:], C_GA, 512, PK("pg"))
        S.act(lambda e: e.activation(gast[sb2][:], pg[:], AF.Silu), reads=[PK("pg")], writes=[K("gast%d" % sb2)])
        S.dma("act", lambda e: e.dma_start(out=SGA[row:row + 128, :], in_=gast[sb2][:]),
              reads=[K("gast%d" % sb2)], writes=[K("SGA%d" % row)])
        proj(px[:, 0:256], C_XB, 256, PK("px"))
        proj(px[:, 256:512], C_VC, 256, PK("px"))
        S.act(lambda e: e.copy(xbst[sb2][:], px[:, 0:256]), reads=[PK("px")], writes=[K("xbst%d" % sb2)])
        S.dma("act", lambda e: e.dma_start(out=XB[row:row + 128, :], in_=xbst[sb2][:]),
              reads=[K("xbst%d" % sb2)], writes=[K("XB%d" % row)])
        S.dve(lambda e: e.bn_stats(bst[:, sub, :], px[:, 256:512]), reads=[PK("px")], writes=[K("bst")])
        S.dve(lambda e: e.bn_aggr(mv[:, sub, :], bst[:, sub, :]), reads=[K("bst")], writes=[K("mv")])
        S.act(lambda e: e.activation(vrs[:, sub:sub + 1], mv[:, sub, 1:2], AF.Sqrt, bias=epst[:], scale=1.0),
              reads=[K("mv"), K("epst")], writes=[K("vrs")])
        S.dve(lambda e: e.reciprocal(vrs[:, sub:sub + 1], vrs[:, sub:sub + 1]), reads=[K("vrs")], writes=[K("vrs")])
        S.dve(lambda e: e.tensor_scalar(vnst[sb2][:], px[:, 256:512], mv[:, sub, 0:1], vrs[:, sub:sub + 1],
                                        ALU.subtract, ALU.mult),
              reads=[PK("px"), K("mv"), K("vrs")], writes=[K("vnst%d" % sb2)])
        S.dma("sp", lambda e: e.dma_start(out=VN[row:row + 128, :], in_=vnst[sb2][:]),
              reads=[K("vnst%d" % sb2)], writes=[K("VN%d" % row)])

    def fm_tile(lt, part):
        hb = lt % 2
        for hh in range(8 if part == 1 else 0):
            S.dma("sp", lambda e, hh=hh: e.dma_start(
                out=(t["Vd2"][lt // 2, hh, :, (lt % 2) * 4:(lt % 2) * 4 + 4, :] if "Vd2" in t
                     else Vd[hh, :, lt * 4:(lt + 1) * 4, :]), in_=vst[hb][:, :, hh * 64:(hh + 1) * 64]),
                reads=[K("vst%d_%d" % (hb, s_)) for s_ in range(4)], writes=[K("Vd%d_%d" % (lt, hh))])
        HNTA = [K("hnT%d_%d" % (lt, s_)) for s_ in range(4)]
        tsl = slice(lt * 512, (lt + 1) * 512)

        def fproj(col):
            pb_ = state["nfm"] % 2
            fb_ = state["nfm"] % 3
            state["nfm"] += 1
            for c in range(8):
                S.pe(lambda e, c=c: e.matmul(pf[pb_][:], wb[:, c, col:col + 128], hnT[lt][:, c, :],
                                             start=(c == 0), stop=(c == 7)),
                     reads=HNTA + [(WBQ if col < C_GA else WBR)[c]], writes=[PK("pf%d" % pb_)])
            return pb_, fb_

        for i in range(4 if part == 1 else 0):
            pb_, fb_ = fproj(C_K + i * 128)
            S.dve(lambda e, pb_=pb_, fb_=fb_: e.tensor_copy(fst[fb_][:], pf[pb_][:]), reads=[PK("pf%d" % pb_)],
                  writes=[K("fst%d" % fb_)])
            S.dma("sp", lambda e, fb_=fb_, i=i: e.dma_start(
                out=(t["KT2"][lt // 2, 2 * i:2 * i + 2, :, (lt % 2) * 512:(lt % 2) * 512 + 512] if "KT2" in t
                     else KT[2 * i:2 * i + 2, :, tsl]).rearrange("h d t -> (h d) t"), in_=fst[fb_][:]),
                reads=[K("fst%d" % fb_)], writes=[K("KT%d_%d" % (lt, i))])
        if part == 1:
            return
        for i in range(2):
            pb_, fb_ = fproj(C_GB + i * 128)
            S.act(lambda e, pb_=pb_, fb_=fb_: e.activation(fst[fb_][:], pf[pb_][:], AF.Silu), reads=[PK("pf%d" % pb_)],
                  writes=[K("fst%d" % fb_)])
            S.dma("act", lambda e, fb_=fb_, i=i: e.dma_start(out=GBT[i * 128:(i + 1) * 128, tsl], in_=fst[fb_][:]),
                  reads=[K("fst%d" % fb_)], writes=[K("GBT%d_%d" % (lt, i))])
        for i in range(2):
            pb_ = state["nfm"] % 2
            state["nfm"] += 1
            for c in range(8):
                S.pe(lambda e, c=c, i=i, pb_=pb_: e.matmul(pf[pb_][:], wb[:, c, C_GC + i * 128:C_GC + (i + 1) * 128],
                                                           hnT[lt][:, c, :], start=(c == 0), stop=(c == 7)),
                     reads=HNTA + [WBR[c]], writes=[PK("pf%d" % pb_)])
            S.act(lambda e, pb_=pb_, i=i: e.activation(sgc[:, i, :], pf[pb_][:], AF.Silu), reads=[PK("pf%d" % pb_)],
                  writes=[K("sgc%d" % i)])
        for i in range(2):
            pb_, fb_ = fproj(C_UC + i * 128)
            S.dve(lambda e, pb_=pb_, fb_=fb_, i=i: e.tensor_tensor(fst[fb_][:], pf[pb_][:], sgc[:, i, :], ALU.mult),
                  reads=[PK("pf%d" % pb_), K("sgc%d" % i)], writes=[K("fst%d" % fb_)])
            S.dma("sp", lambda e, fb_=fb_, i=i: e.dma_start(out=UGT[i * 128:(i + 1) * 128, tsl], in_=fst[fb_][:]),
                  reads=[K("fst%d" % fb_)], writes=[K("UGT%d_%d" % (lt, i))])

    norm_tile(0)
    tr_sub(0, 0)
    for lt in range(ntiles):
        if lt + 1 < ntiles:
            norm_tile(lt + 1)
        for sub in range(4):
            if sub < 3:
                tr_sub(lt, sub + 1)
            elif lt + 1 < ntiles:
                tr_sub(lt + 1, 0)
            proj_sub(lt, sub, 1)
        fm_tile(lt, 1)
        if "after_tile" in t:
            t["after_tile"](lt, S, K)
    for n2, lt in enumerate([ntiles - 1] + list(range(ntiles - 1))):
        for sub in range(4):
            proj_sub(lt, sub, 2)
        if lt == ntiles - 1 and "after_xb" in t:
            t["after_xb"](S, K)
        fm_tile(lt, 2)
        if "after_p2" in t:
            t["after_p2"](n2)


def build_a(**kw):
    nc = bass.Bass("TRN2", target_bir_lowering=False)
    with ExitStack() as es:
        C = Ctx(nc, es)
        t = dict(
            h=C.din("h", [NTOK, D], F32), w_in=C.din("w_in", [D, DIN], F32), g=C.din("g", [128, 8], F32),
            ident=C.din("ident", [128, 128], BF16),
            Qtok=C.dout("Qtok", [NTOK, 512], BF16), SGA=C.dout("SGA", [NTOK, 512], BF16),
            VN=C.dout("VN", [NTOK, 256], BF16), XB=C.dout("XB", [NTOK, 256], BF16),
            KT=C.dout("KT", [8, 64, NTOK], BF16), Vd=C.dout("Vd", [8, 128, 16, 64], BF16),
            GBT=C.dout("GBT", [256, NTOK], BF16), UGT=C.dout("UGT", [256, NTOK], BF16))
        S = Sched(nc)
        emit_phase_a(nc, S, C, t, **kw)
        S.emit()
    return nc


def emit_phase_b1(nc, S, C, t, pfx="", ntile=16, nheads=2):
    rb, Os = t["rb"], t.get("Os")
    if "Qsrc" in t:
        Qsrc, Ksrc, Vsrc = t["Qsrc"], t["Ksrc"], t["Vsrc"]
    else:
        Qg, KTg, Vdg = t["Qg"], t["KTg"], t["Vdg"]
        Qsrc = lambda e, s_: Qg[s_].rearrange("(k p) c -> p k c", p=128)
        Ksrc = lambda e, s_, hl: KTg[s_, hl]
        Vsrc = lambda e, s_, hl: Vdg[s_, hl]
    XK = list(t.get("xkeys", []))
    ident, identf, Erows, onehot, masks, tabd = (t[k] for k in ("ident", "identf", "Erows", "onehot", "masks", "tabd"))
    K = lambda s: pfx + s
    PK = lambda s: "PS:" + pfx + s
    idt = C.sb(K("idt"), [128, 128], BF16)
    idf = C.sb(K("idf"), [128, 128], F32)
    Kaug = [C.sb(K("Kaug%d" % i), [96, S_LEN], BF16) for i in range(2)]
    Vaug = [C.sb(K("Vaug%d" % i), [128, 64, 66], BF16) for i in range(2)]
    Qall = C.sb(K("Qall"), [128, 64, 128], BF16)
    Oall = C.sb(K("Oall"), [128, 64, 128], BF16)
    TB = C.sb(K("TB"), [128, 2, 6, 512], BF16)
    msk = C.sb(K("msk"), [128, 3, 1024], F32)
    ksf = C.sb(K("ksf"), [64, 32], F32)
    ksb = [C.sb(K("ksb%d" % i), [64, 32], BF16) for i in range(2)]
    rbx = C.sb(K("rbx"), [32, 2], F32)
    rbrep = [C.sb(K("rbrep%d" % i), [33, 128], F32) for i in range(2)]
    oh = C.sb(K("oh"), [33, NTAB], F32)
    tabf = C.sb(K("tabf"), [128, 2, NTAB], BF16)
    Qaug = [C.sb(K("Qaug%d" % i), [96, 512], BF16) for i in range(2)]
    gsb = C.sb(K("gsb"), [128, 4, 32], F32)
    top8 = C.sb(K("top8"), [128, 4, 8], F32)
    sel = C.sb(K("sel"), [128, 4, 32], F32)
    Mtok = C.sb(K("Mtok"), [128, 4, 32], BF16)
    PT = [C.sb(K("PT%d" % i), [128, 512], BF16) for i in range(4)]
    osb = [C.sb(K("osb%d" % i), [65, 512], F32) for i in range(2)]
    rden = C.sb(K("rden"), [128, 4], F32)
    SPS = [C.ps(K("sps%d" % i), [128, 512], F32) for i in range(4)]
    OT = [C.ps(K("ot%d" % i), [128, 512], F32) for i in range(2)]
    PQM = C.ps(K("pqm"), [128, 1024], BF16)
    PGO = C.ps(K("pgo"), [128, 512], F32)

    S.dma("sp", lambda e: e.dma_start(out=idt[:], in_=ident), writes=[K("idt")])
    S.dma("sp", lambda e: e.dma_start(out=idf[:], in_=identf), writes=[K("idf")])
    S.dma("sp", lambda e: e.dma_start(out=msk[:], in_=masks.rearrange("m p n -> p m n")), writes=[K("msk")])
    S.dma("sp", lambda e: e.dma_start(out=oh[:], in_=onehot), writes=[K("oh")])
    if "Qdirect" in t:
        for j in range(2):
            S.dma("act", lambda e, j=j: e.dma_start(
                out=Qall[:, :, :].rearrange("p (s j k) c -> p s j (k c)", s=4, j=2, k=8)[:, :, j, :],
                in_=t["Qdirect"](e, j)), reads=XK + ["QsG%d" % j], writes=[K("Qall%d_%d" % (s_, j)) for s_ in range(4)])
    else:
        for s_ in range(4):
            S.dma("sp", lambda e, s_=s_: e.dma_start(out=Qall[:, s_ * 16:(s_ + 1) * 16, :], in_=Qsrc(e, s_)),
                  reads=XK, writes=[K("Qall%d_0" % s_), K("Qall%d_1" % s_)])
    for hl in range(nheads):
        S.dma("sp", lambda e, hl=hl: e.dma_start(out=Kaug[hl][64:96, :], in_=Erows), writes=[K("KE%d" % hl)])
        for s in range(4):
            for hf in range(2):
                if "Kdirect" in t:
                    if s == 0:
                        S.dma("act", lambda e, hl=hl, hf=hf: e.dma_start(
                            out=Kaug[hl][0:64, :].rearrange("p (s j t) -> p s j t", s=4, j=2)[:, :, hf, :],
                            in_=t["Kdirect"](e, hl, hf)), reads=XK + ["KTG%d" % hf],
                            writes=[K("Kd%d_%d_%d" % (hl, s_, hf)) for s_ in range(4)])
                else:
                    S.dma("sp", lambda e, hl=hl, s=s, hf=hf: e.dma_start(
                        out=Kaug[hl][0:64, s * 2048 + hf * 1024:s * 2048 + (hf + 1) * 1024],
                        in_=Ksrc(e, s, hl)[:, hf * 1024:(hf + 1) * 1024]), reads=XK,
                        writes=[K("Kd%d_%d_%d" % (hl, s, hf))])
                S.dma("sp", lambda e, hl=hl, s=s, hf=hf: e.dma_start(
                    out=Vaug[hl][:, s * 16 + hf * 8:s * 16 + (hf + 1) * 8, 0:64],
                    in_=Vsrc(e, s, hl)[:, hf * 8:(hf + 1) * 8, :]), reads=XK, writes=[K("Vd%d_%d_%d" % (hl, s, hf))])
        S.pool(lambda e, hl=hl: e.memset(Vaug[hl][:, :, 64:65], 1.0), writes=[K("V1%d" % hl)])
    S.dma("sp", lambda e: e.dma_start(out=rbx[0:32, :], in_=rb), writes=[K("rbx")])
    for hl in range(nheads):
        S.pool(lambda e, hl=hl: e.memset(rbrep[hl][:, :], 1.0), writes=[K("rbrep%d" % hl)])
        S.dve(lambda e, hl=hl: e.tensor_scalar(rbrep[hl][0:32, :], rbrep[hl][0:32, :], rbx[0:32, hl:hl + 1], None,
                                               ALU.mult),
              reads=[K("rbrep%d" % hl), K("rbx")], writes=[K("rbrep%d" % hl)])
        for ci, (c0, c1) in enumerate(((0, 512), (512, 1024), (1024, NTAB))):
            S.pe(lambda e, hl=hl, c0=c0, c1=c1: e.matmul(SPS[0][:, 0:c1 - c0], rbrep[hl][:, :], oh[:, c0:c1],
                                                         start=True, stop=True),
                 reads=[K("rbrep%d" % hl), K("oh")], writes=[PK("sps0")])
            S.dve(lambda e, hl=hl, c0=c0, c1=c1: e.tensor_copy(tabf[:, hl, c0:c1], SPS[0][:, 0:c1 - c0]),
                  reads=[PK("sps0")], writes=[K("tabf%d_%d" % (hl, ci))])
        S.dma("sp", lambda e, hl=hl: e.dma_start(out=tabd[hl], in_=tabf[:, hl, :]),
              reads=[K("tabf%d_%d" % (hl, ci)) for ci in range(3)], writes=[K("tabd%d" % hl)])
        for i in range(6):
            Dv = 256 - 128 * i
            src = bass.AP(tensor=tabd.tensor, offset=hl * 128 * NTAB + TOFF + Dv, ap=[[NTAB - 1, 128], [1, 512]])
            S.dma("sp", lambda e, hl=hl, i=i, src=src: e.dma_start(out=TB[:, hl, i, :], in_=src),
                  reads=[K("tabd%d" % hl)], writes=[K("TB%d" % hl)])
    KALL = lambda hl: [K("KE%d" % hl)] + [K("Kd%d_%d_%d" % (hl, s, hf)) for s in range(4) for hf in range(2)]
    VALL = lambda hl: [K("V1%d" % hl)] + [K("Vd%d_%d_%d" % (hl, s, hf)) for s in range(4) for hf in range(2)]

    def build_steps(idx, hl, T):
        qb = idx % 2
        QA, QM = K("QA%d" % qb), K("QM%d" % qb)
        steps = []

        def s1():
            for sub in range(4):
                blk = T * 4 + sub
                S.pe(lambda e, sub=sub, blk=blk: e.transpose(PQM[0:64, sub * 128:(sub + 1) * 128],
                                                             Qall[:, blk, hl * 64:(hl + 1) * 64], idt[:]),
                     reads=[K("Qall%d_0" % (blk // 16)), K("Qall%d_1" % (blk // 16)), K("idt")], writes=[PK("pqm")])
            S.act(lambda e: e.mul(Qaug[qb][0:64, :], PQM[0:64, 0:512], 0.125), reads=[PK("pqm")], writes=[QA])

        def s2():
            for sub in range(4):
                S.pe(lambda e, sub=sub: e.matmul(PGO[:, sub * 32:(sub + 1) * 32],
                                                 Qaug[qb][0:64, sub * 128:(sub + 1) * 128], ksb[hl][:, :],
                                                 start=True, stop=True),
                     reads=[QA, K("ksb%d" % hl)], writes=[PK("pgo")])

        def s3():
            for sub in range(4):
                own = 2 * T + sub // 2
                osl = slice(own * 32, (own + 1) * 32)
                S.dve(lambda e, sub=sub, osl=osl: e.tensor_tensor(gsb[:, sub, :], PGO[:, sub * 32:(sub + 1) * 32],
                                                                  msk[:, 0, osl], ALU.add),
                      reads=[PK("pgo"), K("msk")], writes=[K("gsb")])
                S.dve(lambda e, sub=sub: e.max(top8[:, sub, :], gsb[:, sub, :]), reads=[K("gsb")], writes=[K("top8")])
                S.dve(lambda e, sub=sub, osl=osl: e.scalar_tensor_tensor(sel[:, sub, :], gsb[:, sub, :],
                                                                         top8[:, sub, 2:3], msk[:, 1, osl],
                                                                         ALU.is_ge, ALU.mult),
                      reads=[K("gsb"), K("top8"), K("msk")], writes=[K("sel")])
                S.dve(lambda e, sub=sub, osl=osl: e.tensor_tensor(sel[:, sub, :], sel[:, sub, :], msk[:, 2, osl],
                                                                  ALU.add),
                      reads=[K("sel"), K("msk")], writes=[K("sel")])
                S.dve(lambda e, sub=sub: e.tensor_scalar(Mtok[:, sub, :], sel[:, sub, :], -1.0, -NEGM,
                                                         ALU.add, ALU.mult),
                      reads=[K("sel")], writes=[K("Mtok")])

        def s4():
            for sub in range(4):
                S.pe(lambda e, sub=sub: e.transpose(PQM[0:32, 512 + sub * 128:512 + (sub + 1) * 128],
                                                    Mtok[:, sub, :], idt[:]),
                     reads=[K("Mtok"), K("idt")], writes=[PK("pqm")])
            S.act(lambda e: e.copy(Qaug[qb][64:96, :], PQM[0:32, 512:1024]), reads=[PK("pqm")], writes=[QM])

        return [s1, s2, s3, s4]

    tiles = [(hl, T) for T in range(ntile) for hl in range(nheads)]
    if nheads < 2 or ntile < 16:
        S.pool(lambda e: e.memset(Oall[:], 0.0),
               writes=[K("Oall%d_%d" % (h_, b_)) for h_ in range(2) for b_ in range(64)])
    state = {'nkt_total': 0}
    for hl in range(nheads):
        S.dve(lambda e, hl=hl: e.tensor_reduce(ksf[:, :], Kaug[hl][0:64, :].rearrange("p (j k) -> p j k", k=256),
                                               AX.X, ALU.add),
              reads=KALL(hl), writes=[K("ksf")])
        S.dve(lambda e, hl=hl: e.tensor_copy(ksb[hl][:, :], ksf[:, :]), reads=[K("ksf")], writes=[K("ksb%d" % hl)])

    NB = len(SPS)
    LAG = NB - 1
    steps = []
    for idx, (hl, T) in enumerate(tiles):
        nkt = 4 * T + 4
        for kt in range(nkt):
            steps.append((idx, hl, T, kt, nkt))
    first_step = {}
    for gi, st in enumerate(steps):
        first_step.setdefault(st[0], gi)
    actions = {}

    def at(gi, fn):
        actions.setdefault(max(gi, 0), []).append(fn)

    for st in build_steps(0, *tiles[0]):
        at(0, st)
    for idx in range(1, len(tiles)):
        f0 = first_step[idx - 1]
        n_prev = first_step[idx] - f0
        ready_by = first_step[idx] - LAG
        bs = build_steps(idx, *tiles[idx])
        offs = (0, 2, 3, 6)
        for k_, st in enumerate(bs):
            at(min(f0 + offs[k_], ready_by), st)

    def col0(T, kt):
        return max(0, kt - 4 * T) * 128

    def emit_qk(gi):
        idx, hl, T, kt, nkt = steps[gi]
        sb = gi % NB
        qb = idx % 2
        QA, QM = K("QA%d" % qb), K("QM%d" % qb)
        special = kt >= 4 * T - 1
        c0 = col0(T, kt)
        S.pe(lambda e: e.matmul(SPS[sb][:, c0:512], Kaug[hl][0:96, kt * 128:(kt + 1) * 128], Qaug[qb][0:96, c0:512],
                                start=True, stop=not special),
             reads=KALL(hl) + [QA, QM], writes=[PK("sps%d" % sb)])
        if special:
            i = kt - (4 * T - 2)
            S.pe(lambda e: e.matmul(SPS[sb][:, c0:512], idt[:, :], TB[:, hl, i, c0:512], start=False, stop=True),
                 reads=[K("idt"), K("TB%d" % hl)], writes=[PK("sps%d" % sb)])
        S.act(lambda e: e.activation(PT[sb][:, c0:512], SPS[sb][:, c0:512], AF.Exp),
              reads=[PK("sps%d" % sb)], writes=[K("PT%d" % sb)])

    def emit_pv(gi):
        idx, hl, T, kt, nkt = steps[gi]
        sb = gi % NB
        ob = idx % 2
        c0 = col0(T, kt)
        S.pe(lambda e: e.matmul(OT[ob][0:65, c0:512], Vaug[hl][:, kt, 0:65], PT[sb][:, c0:512],
                                start=(kt == 0), stop=(kt == nkt - 1)),
             reads=VALL(hl) + [K("PT%d" % sb)], writes=[PK("ot%d" % ob)])
        if kt == nkt - 1:
            S.act(lambda e: e.copy(osb[ob][:, :], OT[ob][0:65, :]), reads=[PK("ot%d" % ob)], writes=[K("osb%d" % ob)])

            def epi2():
                for sub in range(4):
                    S.pe(lambda e, sub=sub: e.transpose(PGO[:, 128 + sub * 66:128 + sub * 66 + 65],
                                                        osb[ob][0:65, sub * 128:(sub + 1) * 128], idf[0:65, 0:65]),
                         reads=[K("osb%d" % ob), K("idf")], writes=[PK("pgo")])
                otp = PGO[:, 128:392].rearrange("p (s c) -> p s c", c=66)
                S.dve(lambda e: e.reciprocal(rden[:, :], otp[:, :, 64]), reads=[PK("pgo")], writes=[K("rden")])
                for sub in range(4):
                    blk = T * 4 + sub
                    S.dve(lambda e, sub=sub, blk=blk: e.tensor_scalar(Oall[:, blk, hl * 64:(hl + 1) * 64],
                                                                      otp[:, sub, 0:64], rden[:, sub:sub + 1], None,
                                                                      ALU.mult),
                          reads=[PK("pgo"), K("rden")], writes=[K("Oall%d_%d" % (hl, blk))])
                if "after_epi" in t:
                    t["after_epi"](S, K, hl, T)
                if "Os2" in t and hl == nheads - 1 and T % 2 == 1:
                    d_, th = T // 4, (T % 4) // 2
                    S.dma("sp", lambda e: e.dma_start(
                        out=t["Os2"][th, d_].rearrange("(k p) c -> p k c", p=128),
                        in_=Oall[:, d_ * 16 + th * 8:d_ * 16 + th * 8 + 8, :]),
                        reads=[K("Oall%d_%d" % (h_, b_)) for h_ in range(nheads)
                               for b_ in range(d_ * 16 + th * 8, d_ * 16 + th * 8 + 8)],
                        writes=[K("Os%d_%d" % (d_, th))])
                    if "after_out" in t and d_ == 3 and th == 0:
                        t["after_out"](S, K, th)
            at(gi + LAG + 3, epi2)

    nsteps = len(steps)
    for gi in range(nsteps + LAG + 4):
        for fn in actions.pop(gi, []):
            fn()
        if gi < nsteps:
            emit_qk(gi)
        if 0 <= gi - LAG < nsteps:
            emit_pv(gi - LAG)
    for gi in sorted(actions):
        for fn in actions[gi]:
            fn()
    OKEYS = [K("Oall%d_%d" % (hl, blk)) for hl in range(nheads) for blk in range(4 * ntile)]
    if "Os2" not in t:
        S.dma("sp", lambda e: e.dma_start(out=Os.rearrange("d (k p) c -> p (d k) c", p=128)[:, 0:4 * ntile, :],
                                          in_=Oall[:, 0:4 * ntile, :]), reads=OKEYS, writes=[K("Os")])


def build_b1(**kw):
    nc = bass.Bass("TRN2", target_bir_lowering=False)
    with ExitStack() as es:
        C = Ctx(nc, es)
        t = dict(
            Qg=C.din("Qg", [4, NTOK, 128], BF16), KTg=C.din("KTg", [4, 2, 64, NTOK], BF16),
            Vdg=C.din("Vdg", [4, 2, 128, 16, 64], BF16), rb=C.din("rb", [32, 2], F32),
            ident=C.din("ident", [128, 128], BF16), identf=C.din("identf", [128, 128], F32),
            Erows=C.din("Erows", [32, S_LEN], BF16), onehot=C.din("onehot", [33, NTAB], F32),
            masks=C.din("masks", [3, 128, 1024], F32),
            tabd=nc.dram_tensor("tabd", [2, 128, NTAB], BF16, kind="Internal").ap(),
            Os=C.dout("Os", [4, NTOK, 128], BF16))
        S = Sched(nc)
        emit_phase_b1(nc, S, C, t, **kw)
        S.emit()
    return nc


def t5_bucket_np(n):
    n = np.asarray(n)
    nf = np.maximum(n, 1).astype(np.float32)
    large = 16 + (np.log(nf / 16) / np.float32(np.log(128 / 16)) * 16).astype(np.int32)
    large = np.minimum(large, 31)
    return np.where(n < 16, n, large)


def make_consts():
    c = {}
    c["ident"] = np.eye(128, dtype=np.float32).astype(ml_dtypes.bfloat16)
    c["identf"] = np.eye(128, dtype=np.float32)
    E = np.zeros((32, S_LEN), np.float32)
    for j in range(32):
        E[j, j * 256:(j + 1) * 256] = 1.0
    c["Erows"] = E.astype(ml_dtypes.bfloat16)
    n = np.arange(NTAB) - TOFF
    bk = t5_bucket_np(np.maximum(n, 0))
    oh = np.zeros((33, NTAB), np.float32)
    oh[bk, np.arange(NTAB)] = 1.0
    oh[31, :] -= 1.0
    oh[:32, n < 0] = 0.0
    oh[32, n < 0] = NEGM
    c["onehot"] = oh
    j = np.arange(32)[None, :]
    o = np.arange(32)[:, None]
    m = np.zeros((3, 32, 32), np.float32)
    m[0] = np.where(j >= o, -1e30, 0.0)
    m[1] = (j < o)
    m[2] = (j == o)
    c["masks"] = np.ascontiguousarray(np.broadcast_to(m.reshape(3, 1, 1024), (3, 128, 1024)))
    return c


def emit_phase_b2(nc, S, C, t, pfx="", last=False, nblk=16):
    K = lambda s: pfx + s
    PK = lambda s: "PS:" + pfx + s
    Og = t.get("Og")
    SGA, VN, XB, GBT, UGT = (t[k] for k in ("SGA", "VN", "XB", "GBT", "UGT"))
    XBlast, halosel, Afirst, Amat, trilm = (t[k] for k in ("XBlast", "halosel", "Afirst", "Amat", "trilm"))
    h, p, hout = t["h"], t["p"], t["hout"]
    idt = C.sb(K("idt"), [128, 128], BF16)
    Oall = C.sb(K("Oall"), [128, 16, 512], BF16)
    SGAs = C.sb(K("SGAs"), [128, 16, 512], BF16)
    VNs = C.sb(K("VNs"), [128, 16, 256], BF16)
    XBs = C.sb(K("XBs"), [128, 16, 256], BF16)
    GBs = C.sb(K("GBs"), [128, 2, NTOK], BF16)
    UGs = C.sb(K("UGs"), [128, 2, NTOK], BF16)
    XBl = C.sb(K("XBl"), [128, 4, 256], BF16)
    hs = C.sb(K("hs"), [128, 4], F32)
    xacc = C.sb(K("xacc"), [128, 256], F32)
    xbp0 = C.sb(K("xbp0"), [128, 256], BF16)
    Af = C.sb(K("Af"), [128, 4, 128], BF16)
    Am = C.sb(K("Am"), [128, 4, 2, 128], BF16)
    wpre = t.get("wpre")
    if wpre is None:
        wst = [C.sb(K("wst%d" % i), [128, 1024], F32) for i in range(4)]
        wob = C.sb(K("wob"), [128, 8, 1024], BF16)
        wgb = C.sb(K("wgb"), [128, 8, 1024], BF16)
        wpb = C.sb(K("wpb"), [128, 2, 1024], BF16)
    else:
        wob, wgb, wpb = wpre["wob"], wpre["wgb"], wpre["wpb"]
    wbdf = C.sb(K("wbdf"), [128, 2, 128], F32)
    wbd = C.sb(K("wbd"), [128, 2, 128], BF16)
    tril = C.sb(K("tril"), [128, 128], F32)
    swf = C.sb(K("swf"), [128, 4, 128], F32)
    swb = C.sb(K("swb"), [128, 4, 128], BF16)
    WsT = C.sb(K("WsT"), [128, 4, 128], BF16)
    bsf = C.sb(K("bsf"), [1, 512], F32)
    bsb = C.sb(K("bsb"), [1, 512], BF16)
    ones = C.sb(K("ones"), [1, 64], BF16)
    psc = C.sb(K("psc"), [128, 2], F32)
    yab = [C.sb(K("yab%d" % i), [128, 512], BF16) for i in range(2)]
    ycT = [C.sb(K("ycT%d" % i), [128, 8, 128], BF16) for i in range(2)]
    plsb = C.sb(K("plsb"), [128, 2, 128], BF16)
    ht = [C.sb(K("ht%d" % i), [128, D], F32) for i in range(3)]
    pt = [C.sb(K("pt%d" % i), [128, 256], F32) for i in range(2)]
    pb = [C.sb(K("pb%d" % i), [128, 256], BF16) for i in range(2)]
    pT = [C.sb(K("pT%d" % i), [128, 2, 128], BF16) for i in range(4)]
    hnew = [C.sb(K("hnew%d" % i), [128, D], F32) for i in range(3)]
    hb = [C.sb(K("hb%d" % i), [128, D], BF16) for i in range(2)]
    hT = [C.sb(K("hT%d" % i), [128, 8, 128], BF16) for i in range(2)]
    sig = C.sb(K("sig"), [128, D], F32)
    tmp = C.sb(K("tmp"), [128, D], F32)
    h2 = [C.sb(K("h2%d" % i), [128, D], F32) for i in range(2)]
    PT1 = C.ps(K("ps_t1"), [128, 1024], BF16)
    PT2 = C.ps(K("ps_t2"), [128, 8, 128], BF16)
    PPS = C.ps(K("pps"), [128, 512], F32)
    BO = [C.ps(K("bo%d" % i), [128, 512], F32) for i in range(2)]
    BG = [C.ps(K("bg%d" % i), [128, 512], F32) for i in range(2)]
    BP = C.ps(K("bp"), [128, 512], F32)
    if last:
        gfin = C.sb(K("gfin"), [128, D], F32)
        junk = C.sb(K("junk"), [128, D], F32)
        ss = C.sb(K("ss"), [128, 1], F32)
        epst = C.sb(K("epst"), [128, 1], F32)
        S.dma("sp", lambda e: e.dma_start(out=gfin[:], in_=t["final_g"].partition_broadcast(128)), writes=[K("gfin")])
        S.pool(lambda e: e.memset(epst[:], EPS), writes=[K("epst")])

    ld = lambda dst, src, key, q="sp": S.dma(q, lambda e: e.dma_start(out=dst, in_=src), writes=[key])
    ld(idt[:], t["ident"], K("idt"))
    ld(hs[:], halosel, K("hs"))
    S.dma("sp", lambda e: e.dma_start(out=XBl[:], in_=XBlast.rearrange("s p c -> p s c")), reads=list(t.get("xkeys", [])),
          writes=[K("XBl")])
    ld(Af[:], Afirst.rearrange("g s t -> s g t"), K("Af"))
    ld(Am[:], Amat.rearrange("g a s t -> s g a t"), K("Am"))
    ld(tril[:], trilm, K("tril"))
    ld(swf[:], t["sgu_w"].rearrange("h t s -> t h s"), K("swf"))
    ld(bsf[:], t["sgu_b"], K("bsf"))
    ld(psc[:], t["pscale"], K("psc"))
    S.pool(lambda e: e.memset(wbdf[:], 0.0), writes=[K("wbdf")])
    for g in range(4):
        c, gi = g // 2, g % 2
        S.dma("sp", lambda e, g=g, c=c, gi=gi: e.dma_start(out=wbdf[gi * 64:(gi + 1) * 64, c, gi * 64:(gi + 1) * 64],
                                                           in_=t["pool_w"][g]),
              reads=[K("wbdf")], writes=[K("wbdf%d" % g)])
    S.dve(lambda e: e.tensor_copy(wbd[:], wbdf[:]), reads=[K("wbdf")] + [K("wbdf%d" % g) for g in range(4)],
          writes=[K("wbd")])
    S.pool(lambda e: e.memset(ones[:], 1.0), writes=[K("ones")])
    S.dve(lambda e: e.tensor_copy(bsb[:], bsf[:]), reads=[K("bsf")], writes=[K("bsb")])
    for hh in range(4):
        S.dve(lambda e, hh=hh: e.tensor_tensor(swb[:, hh, :], swf[:, hh, :], tril[:], ALU.mult),
              reads=[K("swf"), K("tril")], writes=[K("swb%d" % hh)])
        S.pe(lambda e, hh=hh: e.transpose(PT1[:, hh * 128:(hh + 1) * 128], swb[:, hh, :], idt[:]),
             reads=[K("swb%d" % hh), K("idt")], writes=[PK("pt1")])
    S.act(lambda e: e.copy(WsT[:], PT1[:, 0:512].rearrange("p (h t) -> p h t", t=128)), reads=[PK("pt1")],
          writes=[K("WsT")])
    XK = list(t.get("xkeys", []))
    Osrc = t["Osrc"] if "Osrc" in t else (lambda e, hp: Og[hp].rearrange("(k p) c -> p k c", p=128))
    okeys = t.get("okeys", {})

    def load_oall(hf):
        for s_ in range(4):
            S.dma("sp", lambda e, s_=s_: e.dma_start(out=Oall[:, hf * 8:(hf + 1) * 8, s_ * 128:(s_ + 1) * 128],
                                                     in_=Osrc(e, s_)[:, hf * 8:(hf + 1) * 8, :]),
                  reads=XK + list(okeys.get(hf, [])), writes=[K("Oall%d_%d" % (s_, hf))])

    load_oall(0)
    ld(SGAs[:], SGA.rearrange("(k p) c -> p k c", p=128), K("SGAs"))
    ld(XBs[:], XB.rearrange("(k p) c -> p k c", p=128), K("XBs"))
    ld(VNs[:], VN.rearrange("(k p) c -> p k c", p=128), K("VNs"))
    ld(GBs[:], GBT.rearrange("(c p) t -> p c t", p=128), K("GBs"))
    ld(UGs[:], UGT.rearrange("(c p) t -> p c t", p=128), K("UGs"))
    if "after_loads" in t:
        t["after_loads"](S, [K("SGAs"), K("XBs"), K("VNs"), K("GBs"), K("UGs")] + [K("Oall%d_0" % s_) for s_ in range(4)])
    nw = 0
    for (src, dstt, nch, key) in ((t["w_out"], wob, 8, "wob"), (t["ple_gate_w"], wgb, 8, "wgb"), (t["ple_w"], wpb, 2, "wpb")):
        for c in range(nch if wpre is None else 0):
            b = nw % 4
            nw += 1
            S.dma("sp", lambda e, src=src, c=c, b=b: e.dma_start(out=wst[b][:], in_=src[c * 128:(c + 1) * 128, :]),
                  writes=[K("wst%d" % b)])
            if c % 2 == 0:
                S.act(lambda e, dstt=dstt, c=c, b=b: e.copy(dstt[:, c, :], wst[b][:]), reads=[K("wst%d" % b)],
                      writes=[K("%s%d" % (key, c))])
            else:
                S.dve(lambda e, dstt=dstt, c=c, b=b: e.tensor_copy(dstt[:, c, :], wst[b][:]), reads=[K("wst%d" % b)],
                      writes=[K("%s%d" % (key, c))])
    WOB = [K("wob%d" % c) for c in range(8)]
    WGB = [K("wgb%d" % c) for c in range(8)]
    WPB = [K("wpb%d" % c) for c in range(2)]
    S.dve(lambda e: e.tensor_scalar(xacc[:], XBl[:, 0, :], hs[:, 0:1], None, ALU.mult), reads=[K("XBl"), K("hs")],
          writes=[K("xacc")])
    for s in range(1, 4):
        S.dve(lambda e, s=s: e.scalar_tensor_tensor(xacc[:], XBl[:, s, :], hs[:, s:s + 1], xacc[:], ALU.mult, ALU.add),
              reads=[K("XBl"), K("hs"), K("xacc")], writes=[K("xacc")])
    S.dve(lambda e: e.tensor_copy(xbp0[:], xacc[:]), reads=[K("xacc")], writes=[K("xbp0")])

    hv = h.rearrange("(k p) f -> k p f", p=128)
    pv = p.rearrange("(k p) f -> k p f", p=128)
    ov = hout.rearrange("(k p) f -> k p f", p=128)

    def stageP1(blk):
        b2, b3, b4 = blk % 2, blk % 3, blk % 4
        HT, PTK = K("ht%d" % b3), K("ptk%d" % b2)
        S.dma("sp", lambda e: e.dma_start(out=ht[b3][:], in_=hv[blk]), writes=[HT])
        S.dma("sp", lambda e: e.dma_start(out=pt[b2][:], in_=pv[blk]), writes=[PTK])
        S.dve(lambda e: e.tensor_tensor(yab[b2][:], Oall[:, blk, :], SGAs[:, blk, :], ALU.mult),
              reads=[K("Oall%d_%d" % (s_, blk // 8)) for s_ in range(4)] + [K("SGAs")], writes=[K("yab%d" % b2)])
        S.act(lambda e: e.copy(pb[b2][:], pt[b2][:]), reads=[PTK], writes=[K("pb%d" % b2)])
        for c in range(4):
            S.pe(lambda e, c=c: e.transpose(PT1[:, c * 128:(c + 1) * 128], yab[b2][:, c * 128:(c + 1) * 128], idt[:]),
                 reads=[K("yab%d" % b2), K("idt")], writes=[PK("pt1")])
        for c in range(2):
            S.pe(lambda e, c=c: e.transpose(PT1[:, 512 + c * 128:512 + (c + 1) * 128], pb[b2][:, c * 128:(c + 1) * 128],
                                            idt[:]),
                 reads=[K("pb%d" % b2), K("idt")], writes=[PK("pt1")])
        S.act(lambda e: e.copy(ycT[b2][:, 0:4, :], PT1[:, 0:512].rearrange("p (c t) -> p c t", t=128)),
              reads=[PK("pt1")], writes=[K("ycT%da" % b2)])
        S.act(lambda e: e.copy(pT[b4][:], PT1[:, 512:768].rearrange("p (c t) -> p c t", t=128)), reads=[PK("pt1")],
              writes=[K("pT%d" % b4)])
        for c in range(2):
            for gi in range(2):
                g = 2 * c + gi
                osl = slice(gi * 64, (gi + 1) * 64)
                csl = slice(c * 128, (c + 1) * 128)
                a_d = Af[:, g, :] if blk == 0 else Am[:, g, 0, :]
                S.pe(lambda e, g=g, osl=osl, a_d=a_d, csl=csl: e.matmul(PPS[osl, csl], XBs[:, blk, g * 64:(g + 1) * 64], a_d,
                                                                       start=True, stop=False),
                     reads=[K("XBs"), K("Af"), K("Am")], writes=[PK("pps")])
                if blk == 0:
                    S.pe(lambda e, g=g, osl=osl, csl=csl: e.matmul(PPS[osl, csl], xbp0[64:128, g * 64:(g + 1) * 64],
                                                                   Am[64:128, g, 1, :], start=False, stop=True),
                         reads=[K("xbp0"), K("Am")], writes=[PK("pps")])
                else:
                    S.pe(lambda e, g=g, osl=osl, csl=csl: e.matmul(PPS[osl, csl], XBs[64:128, blk - 1, g * 64:(g + 1) * 64],
                                                                   Am[64:128, g, 1, :], start=False, stop=True),
                         reads=[K("XBs"), K("Am")], writes=[PK("pps")])
        for c in range(2):
            for hi in range(2):
                hh = 2 * c + hi
                osl = slice(hi * 64, (hi + 1) * 64)
                csl = slice(256 + c * 128, 256 + (c + 1) * 128)
                S.pe(lambda e, hh=hh, osl=osl, csl=csl: e.matmul(PPS[osl, csl], VNs[:, blk, hh * 64:(hh + 1) * 64],
                                                                 WsT[:, hh, :], start=True, stop=False),
                     reads=[K("VNs"), K("WsT")], writes=[PK("pps")])
                S.pe(lambda e, hh=hh, osl=osl, csl=csl: e.matmul(PPS[osl, csl], ones[0:1, :],
                                                                 bsb[0:1, hh * 128:(hh + 1) * 128], start=False, stop=True),
                     reads=[K("ones"), K("bsb")], writes=[PK("pps")])
        tsl = slice(blk * 128, (blk + 1) * 128)
        S.dve(lambda e: e.tensor_copy(plsb[:], PPS[:, 0:256].rearrange("p (c t) -> p c t", t=128)), reads=[PK("pps")],
              writes=[K("plsb")])
        for c in range(2):
            S.dve(lambda e, c=c: e.tensor_tensor(ycT[b2][:, 6 + c, :], PPS[:, 256 + c * 128:256 + (c + 1) * 128],
                                                 UGs[:, c, tsl], ALU.mult),
                  reads=[PK("pps"), K("UGs")], writes=[K("ycT%dc%d" % (b2, c))])

    def stageP2(blk):
        b2 = blk % 2
        tsl = slice(blk * 128, (blk + 1) * 128)
        for c in range(2):
            S.pe(lambda e, c=c: e.matmul(PPS[:, c * 128:(c + 1) * 128], wbd[:, c, :], plsb[:, c, :], start=True, stop=True),
                 reads=[K("wbd"), K("plsb")], writes=[PK("pps")])
        for c in range(2):
            S.dve(lambda e, c=c: e.scalar_tensor_tensor(ycT[b2][:, 4 + c, :], PPS[:, c * 128:(c + 1) * 128], psc[:, c:c + 1],
                                                        GBs[:, c, tsl], ALU.mult, ALU.mult),
                  reads=[PK("pps"), K("psc"), K("GBs")], writes=[K("ycT%db%d" % (b2, c))])

    def stageO(blk):
        b2, b3 = blk % 2, blk % 3
        HT = K("ht%d" % b3)
        YC = K("ycT%d" % b2)
        YCALL = [YC + "a", YC + "b0", YC + "b1", YC + "c0", YC + "c1"]
        for nb in range(2):
            nsl = slice(nb * 512, (nb + 1) * 512)
            for c in range(8):
                S.pe(lambda e, nb=nb, c=c, nsl=nsl: e.matmul(BO[nb][:, :], ycT[b2][:, c, :], wob[:, c, nsl],
                                                             start=(c == 0), stop=(c == 7)),
                     reads=YCALL + [WOB[c]], writes=[PK("bo%d" % nb)])
            S.dve(lambda e, nb=nb, nsl=nsl: e.tensor_tensor(hnew[b3][:, nsl], BO[nb][:, :], ht[b3][:, nsl], ALU.add),
                  reads=[PK("bo%d" % nb), HT], writes=[K("hnew%d_%d" % (b3, nb))])
            S.act(lambda e, nb=nb, nsl=nsl: e.copy(hb[b2][:, nsl], hnew[b3][:, nsl]),
                  reads=[K("hnew%d_%d" % (b3, nb))], writes=[K("hb%d_%d" % (b2, nb))])

    def stageT(blk):
        b2 = blk % 2
        for c in range(8):
            S.pe(lambda e, c=c: e.transpose(PT2[:, c, :], hb[b2][:, c * 128:(c + 1) * 128], idt[:]),
                 reads=[K("hb%d_%d" % (b2, c // 4)), K("idt")], writes=[PK("pt2")])
        S.act(lambda e: e.copy(hT[b2][:], PT2[:]), reads=[PK("pt2")], writes=[K("hT%d" % b2)])

    def stageG(blk):
        b2, b3, b4 = blk % 2, blk % 3, blk % 4
        for nb in range(2):
            nsl = slice(nb * 512, (nb + 1) * 512)
            for c in range(8):
                S.pe(lambda e, nb=nb, c=c, nsl=nsl: e.matmul(BG[nb][:, :], hT[b2][:, c, :], wgb[:, c, nsl],
                                                             start=(c == 0), stop=(c == 7)),
                     reads=[K("hT%d" % b2), WGB[c]], writes=[PK("bg%d" % nb)])
            S.act(lambda e, nb=nb, nsl=nsl: e.activation(sig[:, nsl], BG[nb][:, :], AF.Sigmoid),
                  reads=[PK("bg%d" % nb)], writes=[K("sig%d" % nb)])
            for c in range(2):
                S.pe(lambda e, nb=nb, c=c, nsl=nsl: e.matmul(BP[:, :], pT[b4][:, c, :], wpb[:, c, nsl],
                                                             start=(c == 0), stop=(c == 1)),
                     reads=[K("pT%d" % b4), WPB[c]], writes=[PK("bp")])
            S.dve(lambda e, nb=nb, nsl=nsl: e.tensor_tensor(tmp[:, nsl], BP[:, :], sig[:, nsl], ALU.mult),
                  reads=[PK("bp"), K("sig%d" % nb)], writes=[K("tmp%d" % nb)])
            S.dve(lambda e, nb=nb, nsl=nsl: e.tensor_tensor(h2[b2][:, nsl], tmp[:, nsl], hnew[b3][:, nsl], ALU.add),
                  reads=[K("tmp%d" % nb), K("hnew%d_%d" % (b3, nb))], writes=[K("h2%d_%d" % (b2, nb))])
        H2 = [K("h2%d_%d" % (b2, nb)) for nb in range(2)]
        if last:
            S.act(lambda e: e.activation(junk[:], h2[b2][:], AF.Square, accum_out=ss[:]), reads=H2,
                  writes=[K("junk"), K("ss")])
            S.act(lambda e: e.activation(ss[:], ss[:], AF.Sqrt, bias=epst[:], scale=1.0 / D), reads=[K("ss"), K("epst")],
                  writes=[K("ss")])
            S.dve(lambda e: e.reciprocal(ss[:], ss[:]), reads=[K("ss")], writes=[K("ss")])
            S.dve(lambda e: e.scalar_tensor_tensor(h2[b2][:], h2[b2][:], ss[:, 0:1], gfin[:], ALU.mult, ALU.mult),
                  reads=H2 + [K("ss"), K("gfin")], writes=H2)
        S.dma("pool", lambda e: e.dma_start(out=ov[blk], in_=h2[b2][:]), reads=H2, writes=[K("hout%d" % blk)])

    for it in range(nblk + 3):
        if it == min(6, nblk - 1):
            if "mid_hook" in t:
                t["mid_hook"](S)
            load_oall(1)
        if it < nblk:
            stageP1(it)
        if 0 <= it - 1 < nblk:
            stageO(it - 1)
        if it < nblk:
            stageP2(it)
        if 0 <= it - 2 < nblk:
            stageT(it - 2)
        if 0 <= it - 3 < nblk:
            stageG(it - 3)


def build_b2(last=False, **kw):
    nc = bass.Bass("TRN2", target_bir_lowering=False)
    with ExitStack() as es:
        C = Ctx(nc, es)
        t = dict(
            Og=C.din("Og", [4, NTOK, 128], BF16), SGA=C.din("SGA", [NTOK, 512], BF16),
            VN=C.din("VN", [NTOK, 256], BF16), XB=C.din("XB", [NTOK, 256], BF16),
            GBT=C.din("GBT", [256, NTOK], BF16), UGT=C.din("UGT", [256, NTOK], BF16),
            XBlast=C.din("XBlast", [4, 128, 256], BF16), halosel=C.din("halosel", [128, 4], F32),
            Afirst=C.din("Afirst", [4, 128, 128], BF16), Amat=C.din("Amat", [4, 2, 128, 128], BF16),
            trilm=C.din("trilm", [128, 128], F32), h=C.din("h", [NTOK, D], F32), p=C.din("p", [NTOK, 256], F32),
            w_out=C.din("w_out", [D, D], F32), ple_w=C.din("ple_w", [256, D], F32),
            ple_gate_w=C.din("ple_gate_w", [D, D], F32), pool_w=C.din("pool_w", [4, 64, 64], F32),
            pscale=C.din("pscale", [128, 2], F32), sgu_w=C.din("sgu_w", [4, 128, 128], F32),
            sgu_b=C.din("sgu_b", [1, 512], F32), ident=C.din("ident", [128, 128], BF16),
            hout=C.dout("hout", [NTOK, D], F32))
        if last:
            t["final_g"] = C.din("final_g", [1, D], F32)
        S = Sched(nc)
        emit_phase_b2(nc, S, C, t, last=last, **kw)
        S.emit()
    return nc


POOL_W = (2, 4, 8, 16)


def make_consts_b2():
    c = {}
    s = np.arange(128)[:, None]
    tt = np.arange(128)[None, :]
    Amat = np.zeros((4, 2, 128, 128), np.float32)
    Afirst = np.zeros((4, 128, 128), np.float32)
    for g, w in enumerate(POOL_W):
        d = tt - s
        Amat[g, 0] = np.where((d >= 0) & (d < w), 1.0 / w, 0.0) - (d == 0)
        dp = tt + 128 - s
        Amat[g, 1] = np.where(dp < w, 1.0 / w, 0.0)
        Afirst[g] = np.where((d >= 0) & (d < w), 1.0 / np.minimum(tt + 1, w), 0.0) - (d == 0)
    c["Amat"] = Amat.astype(ml_dtypes.bfloat16)
    c["Afirst_seq0"] = Afirst.astype(ml_dtypes.bfloat16)
    c["Afirst_other"] = Amat[:, 0].astype(ml_dtypes.bfloat16)
    c["trilm"] = np.tril(np.ones((128, 128), np.float32))
    return c


GROUPS = [[0, 1, 2, 3], [4, 5, 6, 7]]
BYP = mybir.AluOpType.bypass


def build_fused(depth=2, skip=()):
    nc = bass.Bass("TRN2", target_bir_lowering=False)
    ext_in = lambda name, shape, dt: nc.dram_tensor(name, shape, dt, kind="ExternalInput").ap()
    loc = lambda name, shape, dt: nc.dram_tensor(name, shape, dt)
    x = ext_in("x", [NTOK, D], F32)
    p = ext_in("p", [depth, NTOK, 256], F32)
    norm_g = ext_in("norm_g", [depth, 128, 8], F32)
    w_in = ext_in("w_in", [depth, D, DIN], F32)
    w_out = ext_in("w_out", [depth, D, D], F32)
    rb = ext_in("rb", [32, 2], F32)
    pool_w = ext_in("pool_w", [depth, 4, 64, 64], F32)
    pscale = ext_in("pscale", [depth, 128, 2], F32)
    sgu_w = ext_in("sgu_w", [depth, 4, 128, 128], F32)
    sgu_b = ext_in("sgu_b", [depth, 1, 512], F32)
    ple_w = ext_in("ple_w", [depth, 256, D], F32)
    ple_gate_w = ext_in("ple_gate_w", [depth, D, D], F32)
    final_g = ext_in("final_g", [1, D], F32)
    halosel = ext_in("halosel", [128, 4], F32)
    Afirst = ext_in("Afirst", [4, 128, 128], BF16)
    consts = dict(ident=ext_in("ident", [128, 128], BF16), identf=ext_in("identf", [128, 128], F32),
                  Erows=ext_in("Erows", [32, S_LEN], BF16), onehot=ext_in("onehot", [33, NTAB], F32),
                  masks=ext_in("masks", [3, 128, 1024], F32), Amat=ext_in("Amat", [4, 2, 128, 128], BF16),
                  trilm=ext_in("trilm", [128, 128], F32))
    out = nc.dram_tensor("out", [NTOK, D], F32, kind="ExternalOutput").ap()
    hbuf = loc("hbuf", [NTOK, D], F32).ap()
    Qs2 = loc("Qs2", [2, 4, 128, 8, 128], BF16)
    KT2 = loc("KT2", [2, 8, 64, 1024], BF16)
    Vd2 = loc("Vd2", [2, 8, 128, 8, 64], BF16)
    XBl = loc("XBl", [128, 256], BF16)
    QsG = loc("QsG", [2, 4, 4, 128, 8, 128], BF16)
    KTG = loc("KTG", [2, 4, 4, 2, 64, 1024], BF16)
    VdG = loc("VdG", [2, 4, 4, 2, 128, 8, 64], BF16)
    XBlG = loc("XBlG", [4 * 128, 256], BF16)
    Os2 = loc("Os2", [2, 4, 1024, 128], BF16)
    OsG = loc("OsG", [2, 4, 4, 1024, 128], BF16)
    Qg = loc("Qg", [4, NTOK, 128], BF16)
    KTg = loc("KTg", [4, 2, 64, NTOK], BF16)
    Vdg = loc("Vdg", [4, 2, 128, 16, 64], BF16)
    Og = loc("Og", [4, NTOK, 128], BF16)
    SGA = loc("SGA", [NTOK, 512], BF16).ap()
    VN = loc("VN", [NTOK, 256], BF16).ap()
    XB = loc("XB", [NTOK, 256], BF16).ap()
    GBT = loc("GBT", [256, NTOK], BF16).ap()
    UGT = loc("UGT", [256, NTOK], BF16).ap()
    tabd = loc("tabd", [2, 128, NTAB], BF16).ap()

    rank_cache = {}

    def rank(e):
        key = (len(rank_cache_phase), str(e.engine))
        if key not in rank_cache:
            rank_cache[key] = e.partition_id() % 4
        return rank_cache[key]

    rank_cache_phase = []

    with ExitStack() as gs:
        P = SemPool(nc, gs)
        for i in range(depth):
            last = i == depth - 1
            with ExitStack() as es:
                C = Ctx(nc, es)
                S = Sched(nc, P)
                ta = dict(h=x if i == 0 else hbuf, w_in=w_in[i], g=norm_g[i], ident=consts["ident"], Qs2=Qs2.ap(),
                          SGA=SGA, VN=VN, XB=XB, KT2=KT2.ap(), Vd2=Vd2.ap(), GBT=GBT, UGT=UGT)
                def ag_(S_, src2d, dst2d, key, reads=()):
                    S_.cc(lambda e: e.collective_compute("AllGather", BYP, replica_groups=GROUPS, ins=[src2d.opt()],
                                                         outs=[dst2d.opt()]), reads=list(reads), writes=[key])

                deferred = []

                def xchg_half(S_, j, rq=(), rk=(), rv=()):
                    ag_(S_, Vd2.ap()[j].rearrange("h p k d -> (h p) (k d)"),
                        VdG.ap()[j].rearrange("s h l p k d -> (s h l p) (k d)"), "VdG%d" % j, rv)
                    ag_(S_, KT2.ap()[j].rearrange("h d t -> (h d) t"), KTG.ap()[j].rearrange("s h l d t -> (s h l d) t"),
                        "KTG%d" % j, rk)
                    ag_(S_, Qs2.ap()[j].rearrange("d p k c -> (d p) (k c)"),
                        QsG.ap()[j].rearrange("s h p k c -> (s h p) (k c)"), "QsG%d" % j, rq)
                    def copies():
                        S_.dma("sp", lambda e: e.dma_start(
                            out=Vdg.ap().rearrange("s l p k d -> s (l p) k d")[:, :, j * 8:(j + 1) * 8, :].rearrange(
                                "s r k d -> s r (k d)"),
                            in_=VdG.ap()[j, :, bass.ds(rank(e), 1)].rearrange("s o l p k d -> s (o l p) (k d)")),
                            reads=["VdG%d" % j], writes=["Vdg_%d" % j])
                    deferred.append(copies)

                def after_tile(lt, S_, K_):
                    if lt in (1, 3):
                        j = lt // 2
                        xchg_half(S_, j,
                                  rq=[K_("Qtok%d" % r_) for r_ in range(j * 1024, (j + 1) * 1024, 128)],
                                  rk=[K_("KT%d_%d" % (l_, i_)) for l_ in (2 * j, 2 * j + 1) for i_ in range(4)],
                                  rv=[K_("Vd%d_%d" % (l_, h_)) for l_ in (2 * j, 2 * j + 1) for h_ in range(8)])

                if "xa" not in skip:
                    ta["after_tile"] = after_tile

                def after_xb(S_, K_):
                    S_.dma("sp", lambda e: e.dma_start(out=XBl.ap(), in_=XB[NTOK - 128:NTOK, :]),
                           reads=[K_("XB%d" % (NTOK - 128))], writes=["XBl"])
                    ag_(S_, XBl.ap(), XBlG.ap(), "XBlG", reads=["XBl"])

                if "xa" not in skip:
                    ta["after_xb"] = after_xb

                def after_p2(n2):
                    if n2 == 2 and deferred:
                        deferred.pop(0)()

                ta["after_p2"] = after_p2
                if "a" not in skip:
                    emit_phase_a(nc, S, C, ta, pfx="a%d_" % i)
                else:
                    xchg_half(S, 0)
                    xchg_half(S, 1)
                    after_xb(S, lambda k_: k_)
                for fn_ in deferred:
                    fn_()
                S.drain_cc = False
                rank_cache_phase.append(1)
                S.emit()
            wscope = ExitStack()
            wpre = dict(wob=wscope.enter_context(nc.sbuf_tensor("wob_%d" % i, [128, 8, 1024], BF16)),
                        wgb=wscope.enter_context(nc.sbuf_tensor("wgb_%d" % i, [128, 8, 1024], BF16)),
                        wpb=wscope.enter_context(nc.sbuf_tensor("wpb_%d" % i, [128, 2, 1024], BF16)))
            with ExitStack() as es:
                C = Ctx(nc, es)
                S = Sched(nc, P)
                def after_epi(S_, K_, hl_, T_, i=i, wpre=wpre):
                    if hl_ == 0 and T_ == 4:
                        for (src_, key_, nch_) in ((w_out[i], "wob", 8), (ple_gate_w[i], "wgb", 8), (ple_w[i], "wpb", 2)):
                            for c_ in range(nch_):
                                S_.dma("pool", lambda e, src_=src_, key_=key_, c_=c_: e.dma_start(
                                    out=wpre[key_][:, c_, :], in_=src_[c_ * 128:(c_ + 1) * 128, :]),
                                    reads=[K_("Oall0_16")], writes=["%s%d_%d" % (key_, i, c_)])

                tb = dict(rb=rb, Os2=Os2.ap(), tabd=tabd, Qg=Qg.ap(), KTg=KTg.ap(), Vdg=Vdg.ap(), after_epi=after_epi,
                          **consts)
                tb["Qdirect"] = lambda e, j: QsG.ap()[j, :, bass.ds(rank(e), 1)].rearrange("s o p k c -> (o p) s (k c)")
                tb["Kdirect"] = lambda e, hl, j: KTG.ap()[j, :, bass.ds(rank(e), 1), hl].rearrange("s o d t -> (o d) s t")

                def after_out(S_, K_, th, nodep=False):
                    S_.cc(lambda e: e.collective_compute(
                        "AllGather", BYP, replica_groups=GROUPS, ins=[Os2.ap()[th].rearrange("d t c -> (d t) c").opt()],
                        outs=[OsG.ap()[th].rearrange("s h t c -> (s h t) c").opt()]),
                        reads=([] if nodep else [K_("Os%d_%d" % (d_, th)) for d_ in range(4)]), writes=["OsG%d" % th])
                    S_.dma("sp", lambda e: e.dma_start(
                        out=Og.ap()[:, th * 1024:(th + 1) * 1024, :],
                        in_=OsG.ap()[th, :, bass.ds(rank(e), 1)].rearrange("s o t c -> s (o t) c")),
                        reads=["OsG%d" % th], writes=["Og_%d" % th])

                tb["after_out"] = after_out
                if "b1" not in skip:
                    emit_phase_b1(nc, S, C, tb, pfx="b%d_" % i)
                else:
                    after_out(S, lambda k_: k_, 0)
                rank_cache_phase.append(1)
                S.emit()
            with ExitStack() as es:
                C = Ctx(nc, es)
                S = Sched(nc, P)
                tc_ = dict(SGA=SGA, VN=VN, XB=XB, GBT=GBT, UGT=UGT, XBlast=XBlG.ap().rearrange("(s p) c -> s p c", p=128),
                           halosel=halosel, Afirst=Afirst, h=x if i == 0 else hbuf, p=p[i], Og=Og.ap(),
                           hout=out if last else hbuf, w_out=w_out[i], ple_w=ple_w[i], ple_gate_w=ple_gate_w[i],
                           pool_w=pool_w[i], pscale=pscale[i], sgu_w=sgu_w[i], sgu_b=sgu_b[i], final_g=final_g,
                           **consts)
                def after_loads(S_, keys):
                    S_.cc(lambda e: e.collective_compute(
                        "AllGather", BYP, replica_groups=GROUPS, ins=[Os2.ap()[1].rearrange("d t c -> (d t) c").opt()],
                        outs=[OsG.ap()[1].rearrange("s h t c -> (s h t) c").opt()]), reads=keys, writes=["OsG1"])

                tc_["after_loads"] = after_loads
                if "b2" in skip:
                    after_loads(S, [])
                tc_["mid_hook"] = lambda S_: S_.dma("sp", lambda e: e.dma_start(
                    out=Og.ap()[:, 1024:2048, :],
                    in_=OsG.ap()[1, :, bass.ds(rank(e), 1)].rearrange("s o t c -> s (o t) c")),
                    reads=["OsG1"], writes=["Og_1"])
                tc_["okeys"] = {1: ["Og_1"]}
                tc_["wpre"] = wpre
                rank_cache_phase.append(1)
                if "b2" not in skip:
                    emit_phase_b2(nc, S, C, tc_, pfx="c%d_" % i, last=last)
                S.emit()
            wscope.close()
    return nc


_PROGS = {}


def kernel(x, p, norm_g, w_in, w_out, rel_bias, pool_w, pool_scale, sgu_w, sgu_b, ple_w, ple_gate_w, final_g):
    f32 = lambda a: np.ascontiguousarray(np.asarray(a, dtype=np.float32))
    x, p, norm_g, w_in, w_out, rel_bias = map(f32, (x, p, norm_g, w_in, w_out, rel_bias))
    pool_w, pool_scale, sgu_w, sgu_b, ple_w, ple_gate_w, final_g = map(
        f32, (pool_w, pool_scale, sgu_w, sgu_b, ple_w, ple_gate_w, final_g))
    depth = w_in.shape[0]
    cs = make_consts()
    cb = make_consts_b2()
    if "fused" not in _PROGS:
        _PROGS["fused"] = build_fused(depth)
    nc = _PROGS["fused"]
    shared = dict(
        norm_g=np.ascontiguousarray(norm_g.reshape(depth, 8, 128).transpose(0, 2, 1)), w_in=w_in, w_out=w_out,
        pool_w=pool_w, pscale=np.ascontiguousarray(pool_scale.reshape(depth, 2, 128).transpose(0, 2, 1)),
        sgu_w=sgu_w, sgu_b=np.ascontiguousarray(sgu_b.reshape(depth, 1, 512)), ple_w=ple_w, ple_gate_w=ple_gate_w,
        final_g=np.ascontiguousarray(final_g.reshape(1, D)), ident=cs["ident"], identf=cs["identf"],
        Erows=cs["Erows"], onehot=cs["onehot"], masks=cs["masks"], Amat=cb["Amat"], trilm=cb["trilm"])
    in_maps = []
    for c in range(NCORES):
        b, r = c // 4, c % 4
        d = dict(shared)
        d["x"] = np.ascontiguousarray(x[b, r * NTOK:(r + 1) * NTOK])
        d["p"] = np.ascontiguousarray(p[:, b, r * NTOK:(r + 1) * NTOK])
        d["rb"] = np.ascontiguousarray(rel_bias[:, 2 * r:2 * r + 2])
        d["halosel"] = np.ascontiguousarray(np.broadcast_to((np.arange(4) == r - 1).astype(np.float32)[None, :], (128, 4)))
        d["Afirst"] = cb["Afirst_seq0"] if r == 0 else cb["Afirst_other"]
        in_maps.append(d)
    res = run_bass_kernel_spmd(nc, in_maps, core_ids=list(range(NCORES))).results
    out = np.zeros(x.shape, np.float32)
    for c in range(NCORES):
        b, r = c // 4, c % 4
        out[b, r * NTOK:(r + 1) * NTOK] = np.asarray(res[c]["out"])
    return out
```

```python
import numpy as np
import ml_dtypes
from contextlib import ExitStack

import concourse.bass as bass
import concourse.mybir as mybir
from concourse.bass_utils import run_bass_kernel_spmd

F32 = mybir.dt.float32
BF16 = mybir.dt.bfloat16
AF = mybir.ActivationFunctionType
ALU = mybir.AluOpType
AX = mybir.AxisListType

NCORES = 8
S_LEN = 8192
D = 1024
DIN = 3328
NTOK = 2048
NTAB = 1280
TOFF = 511
NEGM = -30000.0
EPS = 1e-6
TL = [sorted([r, 7 - r, 8 + r, 15 - r]) for r in range(4)]
OWNER = {}
for _r in range(4):
    for _lt, _T in enumerate(TL[_r]):
        OWNER[_T] = (_r, _lt)


STRICT = True


class Op:
    __slots__ = ("eng", "fn", "reads", "writes", "dma", "deps", "signal", "count", "sem",
                 "target", "prev_target", "idx", "inc", "carry")

    def __init__(self, eng, fn, reads, writes, dma):
        self.eng = eng
        self.fn = fn
        self.reads = tuple(reads)
        self.writes = tuple(writes)
        self.dma = dma
        self.deps = []
        self.signal = dma
        self.count = 0
        self.sem = None
        self.target = 0
        self.prev_target = 0
        self.inc = 16
        self.carry = []


class SemPool:
    ENGS = ("pe", "act", "dve", "pool", "sp")

    def __init__(self, nc, es, n_dma_sems=12):
        self.n_dma_sems = n_dma_sems
        self.esem = {e: es.enter_context(nc.semaphore("s_" + e)) for e in self.ENGS}
        self.dsem = {(e, j): es.enter_context(nc.semaphore("d_%s_%d" % (e, j)))
                     for e in ("sp", "pool", "act", "cc") for j in range(n_dma_sems if e != "cc" else 4)}
        self.ecount = {e: 0 for e in self.ENGS}
        self.dcount = {k: 0 for k in self.dsem}
        self.dnext = {e: 0 for e in self.ENGS + ("cc",)}
        self.carry = {}


class Sched:
    ENGS = ("pe", "act", "dve", "pool", "sp")

    def __init__(self, nc, pool=None, n_dma_sems=12):
        self.nc = nc
        self.ops = []
        self.n_dma_sems = n_dma_sems
        self.pool_ = pool
        self.drain_cc = True

    def cc(self, fn, reads=(), writes=()):
        op = self.add("pool", fn, reads, writes, dma=True)
        op.inc = 1
        return op

    def add(self, eng, fn, reads=(), writes=(), dma=False):
        ex = [k for k in list(reads) + list(writes) if k.startswith("PS:")]
        reads = list(reads) + [k for k in ex if k not in reads]
        writes = list(writes) + [k for k in ex if k not in writes]
        op = Op(eng, fn, reads, writes, dma)
        op.idx = len(self.ops)
        self.ops.append(op)
        return op

    def pe(self, fn, reads=(), writes=()):
        return self.add("pe", fn, reads, writes)

    def act(self, fn, reads=(), writes=()):
        return self.add("act", fn, reads, writes)

    def dve(self, fn, reads=(), writes=()):
        return self.add("dve", fn, reads, writes)

    def pool(self, fn, reads=(), writes=()):
        return self.add("pool", fn, reads, writes)

    def dma(self, q, fn, reads=(), writes=()):
        return self.add(q, fn, reads, writes, dma=True)

    def analyze(self):
        last_writer = {}
        readers = {}
        for op in self.ops:
            raw = set()
            other = set()
            for k in op.reads:
                w = last_writer.get(k)
                if w is not None:
                    raw.add(w)
            for k in op.writes:
                w = last_writer.get(k)
                if w is not None:
                    other.add(w)
                for r in readers.get(k, ()):
                    other.add(r)
            deps = []
            for d in raw | other:
                if d is op:
                    continue
                if (not d.dma) and (not op.dma) and d.eng == op.eng:
                    if op.eng == "pe" or (d not in raw and not STRICT):
                        continue
                deps.append(d)
            deps.sort(key=lambda o: o.idx)
            if self.pool_ is not None:
                for k in op.reads:
                    if k not in last_writer and k in self.pool_.carry:
                        op.carry.append(self.pool_.carry[k])
            op.deps = deps
            for d in deps:
                d.signal = True
            for k in op.reads:
                readers.setdefault(k, []).append(op)
            for k in op.writes:
                last_writer[k] = op
                readers[k] = []
        P = self.pool_
        cnt = dict(P.ecount) if P else {e: 0 for e in self.ENGS}
        dcnt = dict(P.dnext) if P else {e: 0 for e in self.ENGS}
        self.dma_uses = dict(P.dcount) if P else {}
        for op in self.ops:
            if op.dma:
                if op.inc == 1 and P:
                    j = dcnt["cc"] % 4
                    dcnt["cc"] += 1
                    key = ("cc", j)
                else:
                    j = dcnt[op.eng] % self.n_dma_sems
                    dcnt[op.eng] += 1
                    key = (op.eng, j)
                prev = self.dma_uses.get(key, 0)
                op.sem = key
                op.prev_target = prev
                op.target = prev + op.inc
                self.dma_uses[key] = op.target
            elif op.signal:
                cnt[op.eng] += 1
                op.count = cnt[op.eng]
        if P:
            P.ecount = cnt
            P.dnext = dcnt
            P.dcount = dict(self.dma_uses)
            for op in self.ops:
                if op.dma and op.inc == 1:
                    for k in op.writes:
                        P.carry[k] = (op.sem, op.target)

    def emit(self):
        nc = self.nc
        self.analyze()
        with ExitStack() as es:
            if self.pool_ is not None:
                esem, dsem = self.pool_.esem, self.pool_.dsem
            else:
                esem = {e: es.enter_context(nc.semaphore("s_" + e)) for e in self.ENGS}
                dsem = {}
                for key in self.dma_uses:
                    dsem[key] = es.enter_context(nc.semaphore("d_%s_%d" % key))
            block = es.enter_context(nc.Block())
            per_eng = {e: [o for o in self.ops if o.eng == e] for e in self.ENGS}

            def run(eng_name, h):
                waited = {}

                def w(sem_key, sem, val):
                    if val <= 0 or waited.get(sem_key, 0) >= val:
                        return
                    waited[sem_key] = val
                    h.wait_ge(sem, val)

                for op in per_eng[eng_name]:
                    for d in op.deps:
                        if d.dma:
                            w(d.sem, dsem[d.sem], d.target)
                        else:
                            w(d.eng, esem[d.eng], d.count)
                    for (ck, ct) in op.carry:
                        w(ck, dsem[ck], ct)
                    if op.dma and op.prev_target > 0:
                        w(op.sem, dsem[op.sem], op.prev_target)
                    inst = op.fn(h)
                    if op.dma:
                        inst.then_inc(dsem[op.sem], op.inc)
                    elif op.signal:
                        inst.then_inc(esem[op.eng], 1)
                for key, tgt in self.dma_uses.items():
                    if key[0] == eng_name or (key[0] == "cc" and eng_name == "pool" and self.drain_cc):
                        w(key, dsem[key], tgt)

            @block.tensor
            def _(h):
                run("pe", h)

            @block.scalar
            def _(h):
                run("act", h)

            @block.vector
            def _(h):
                run("dve", h)

            @block.gpsimd
            def _(h):
                run("pool", h)

            @block.sync
            def _(h):
                run("sp", h)


class Ctx:
    def __init__(self, nc, es):
        self.nc = nc
        self.es = es

    def sb(self, name, shape, dt):
        return self.es.enter_context(self.nc.sbuf_tensor(name, shape, dt))

    def ps(self, name, shape, dt):
        return self.es.enter_context(self.nc.psum_tensor(name, shape, dt))

    def din(self, name, shape, dt):
        return self.nc.dram_tensor(name, shape, dt, kind="ExternalInput").ap()

    def dout(self, name, shape, dt):
        return self.nc.dram_tensor(name, shape, dt, kind="ExternalOutput").ap()


C_Q, C_K, C_V, C_GA, C_XB, C_GB, C_UC, C_VC, C_GC = 0, 512, 1024, 1536, 2048, 2304, 2560, 2816, 3072


def emit_phase_a(nc, S, C, t, pfx="", ntiles=4, nsub=4, dofm=True, stage=99):
    h, w_in, g, ident = t["h"], t["w_in"], t["g"], t["ident"]
    Qtok = t.get("Qtok")
    SGA, VN, XB, GBT, UGT = (t[k] for k in ("SGA", "VN", "XB", "GBT", "UGT"))
    KT, Vd = t.get("KT"), t.get("Vd")
    K = lambda s: pfx + s
    PK = lambda s: "PS:" + pfx + s
    wb = C.sb(K("wb"), [128, 8, DIN], BF16)
    wst = [C.sb(K("wst%d" % i), [128, DIN - C_GA], F32) for i in range(3)]
    gt = C.sb(K("gt"), [128, 8], F32)
    idt = C.sb(K("idt"), [128, 128], BF16)
    epst = C.sb(K("epst"), [128, 1], F32)
    ht = [C.sb(K("ht%d" % i), [128, 4, D], F32) for i in range(2)]
    junk = C.sb(K("junk"), [128, D], F32)
    ss = [C.sb(K("ss%d" % i), [128, 4], F32) for i in range(2)]
    rstd = [C.sb(K("rstd%d" % i), [128, 4], F32) for i in range(2)]
    hn = C.sb(K("hn"), [128, 4, D], BF16)
    hnT = [C.sb(K("hnT%d" % i), [128, 8, 512], BF16) for i in range(4)]
    qst = [C.sb(K("qst%d" % i), [128, 512], BF16) for i in range(2)]
    gast = [C.sb(K("gast%d" % i), [128, 512], BF16) for i in range(2)]
    xbst = [C.sb(K("xbst%d" % i), [128, 256], BF16) for i in range(2)]
    vnst = [C.sb(K("vnst%d" % i), [128, 256], BF16) for i in range(2)]
    vst = [C.sb(K("vst%d" % i), [128, 8, 4, 64], BF16) for i in range(2)]
    bst = C.sb(K("bst"), [128, 4, 6], F32)
    mv = C.sb(K("mv"), [128, 4, 2], F32)
    vrs = C.sb(K("vrs"), [128, 4], F32)
    fst = [C.sb(K("fst%d" % i), [128, 512], BF16) for i in range(3)]
    sgc = C.sb(K("sgc"), [128, 2, 512], F32)
    tp = C.ps(K("tp"), [128, 8, 128], BF16)
    pq = C.ps(K("pq"), [128, 512], F32)
    pv = C.ps(K("pv"), [128, 512], F32)
    pg = C.ps(K("pg"), [128, 512], F32)
    px = C.ps(K("px"), [128, 512], F32)
    pf = [C.ps(K("pf%d" % i), [128, 512], F32) for i in range(2)]

    S.dma("sp", lambda e: e.dma_start(out=idt[:], in_=ident), writes=[K("idt")])
    S.dma("sp", lambda e: e.dma_start(out=gt[:], in_=g), writes=[K("gt")])
    S.pool(lambda e: e.memset(epst[:], EPS), writes=[K("epst")])
    nws = 0
    for (c0, c1, tag) in ((0, C_GA, "q"), (C_GA, DIN, "r")):
        for c in range(8):
            b = nws % 3
            nws += 1
            S.dma("sp", lambda e, c=c, b=b, c0=c0, c1=c1: e.dma_start(out=wst[b][:, 0:c1 - c0],
                                                                      in_=w_in[c * 128:(c + 1) * 128, c0:c1]),
                  writes=[K("wst%d" % b)])
            if c % 2 == 0:
                S.act(lambda e, c=c, b=b, c0=c0, c1=c1: e.activation(wb[:, c, c0:c1], wst[b][:, 0:c1 - c0], AF.Copy,
                                                                     scale=gt[:, c:c + 1]),
                      reads=[K("wst%d" % b), K("gt")], writes=[K("wb%s%d" % (tag, c))])
            else:
                S.dve(lambda e, c=c, b=b, c0=c0, c1=c1: e.tensor_scalar(wb[:, c, c0:c1], wst[b][:, 0:c1 - c0],
                                                                        gt[:, c:c + 1], None, ALU.mult),
                      reads=[K("wst%d" % b), K("gt")], writes=[K("wb%s%d" % (tag, c))])
    WBQ = [K("wbq%d" % c) for c in range(8)]
    WBR = [K("wbr%d" % c) for c in range(8)]
    hv = h.rearrange("(l s p) f -> l p s f", s=4, p=128)
    state = {"nfm": 0}

    def norm_tile(lt):
        hb = lt % 2
        HT = K("ht%d" % hb)
        S.dma("sp", lambda e: e.dma_start(out=ht[hb][:], in_=hv[lt]), writes=[HT])
        for sub in range(4):
            S.act(lambda e, sub=sub: e.activation(junk[:], ht[hb][:, sub, :], AF.Square,
                                                  accum_out=ss[hb][:, sub:sub + 1]),
                  reads=[HT], writes=[K("junk"), K("ss%d" % hb)])
        S.act(lambda e: e.activation(rstd[hb][:], ss[hb][:], AF.Sqrt, bias=epst[:], scale=1.0 / D),
              reads=[K("ss%d" % hb), K("epst")], writes=[K("rstd%d" % hb)])
        S.dve(lambda e: e.reciprocal(rstd[hb][:], rstd[hb][:]), reads=[K("rstd%d" % hb)], writes=[K("rstd%d" % hb)])

    def tr_sub(lt, sub):
        hb = lt % 2
        HT = K("ht%d" % hb)
        S.dve(lambda e: e.tensor_scalar(hn[:, sub, :], ht[hb][:, sub, :], rstd[hb][:, sub:sub + 1], None, ALU.mult),
              reads=[HT, K("rstd%d" % hb)], writes=[K("hn%d" % sub)])
        for c in range(8):
            S.pe(lambda e, c=c: e.transpose(tp[:, c, :], hn[:, sub, c * 128:(c + 1) * 128], idt[:]),
                 reads=[K("hn%d" % sub), K("idt")], writes=[PK("tp")])
        S.act(lambda e: e.copy(hnT[lt][:, :, sub * 128:(sub + 1) * 128], tp[:]),
              reads=[PK("tp")], writes=[K("hnT%d_%d" % (lt, sub))])

    def proj_sub(lt, sub, part):
        hb = lt % 2
        sb2 = sub % 2
        HNT = K("hnT%d_%d" % (lt, sub))
        row = (lt * 4 + sub) * 128

        def proj(ps_ap, col, n, pskey):
            for c in range(8):
                S.pe(lambda e, c=c: e.matmul(ps_ap, hnT[lt][:, c, sub * 128:(sub + 1) * 128],
                                             wb[:, c, col:col + n], start=(c == 0), stop=(c == 7)),
                     reads=[HNT, (WBQ if col < C_GA else WBR)[c]], writes=[pskey])

        if part == 2:
            return proj_sub2(lt, sub, proj)
        proj(pq[:], C_Q, 512, PK("pq"))
        S.act(lambda e: e.copy(qst[sb2][:], pq[:]), reads=[PK("pq")], writes=[K("qst%d" % sb2)])
        if "Qs2" in t:
            S.dma("act", lambda e: e.dma_start(
                out=t["Qs2"][row // 1024, :, :, (row % 1024) // 128, :].rearrange("d p c -> p d c"),
                in_=qst[sb2][:].rearrange("p (d c) -> p d c", c=128)),
                reads=[K("qst%d" % sb2)], writes=[K("Qtok%d" % row)])
        else:
            S.dma("sp", lambda e: e.dma_start(out=Qtok[row:row + 128, :], in_=qst[sb2][:]),
                  reads=[K("qst%d" % sb2)], writes=[K("Qtok%d" % row)])
        proj(pv[:], C_V, 512, PK("pv"))
        S.dve(lambda e: e.tensor_copy(vst[hb][:, :, sub, :], pv[:].rearrange("p (h d) -> p h d", d=64)),
              reads=[PK("pv")], writes=[K("vst%d_%d" % (hb, sub))])

    def proj_sub2(lt, sub, proj):
        sb2 = sub % 2
        row = (lt * 4 + sub) * 128
        proj(pg[:], C_GA, 512, PK("pg"))
        S.act(lambda e: e.activation(gast[sb2][:], pg[:], AF.Silu), reads=[PK("pg")], writes=[K("gast%d" % sb2)])
        S.dma("act", lambda e: e.dma_start(out=SGA[row:row + 128, :], in_=gast[sb2][:]),
              reads=[K("gast%d" % sb2)], writes=[K("SGA%d" % row)])
        proj(px[:, 0:256], C_XB, 256, PK("px"))
        proj(px[:, 256:512], C_VC, 256, PK("px"))
        S.act(lambda e: e.copy(xbst[sb2][:], px[:, 0:256]), reads=[PK("px")], writes=[K("xbst%d" % sb2)])
        S.dma("act", lambda e: e.dma_start(out=XB[row:row + 128, :], in_=xbst[sb2][:]),
              reads=[K("xbst%d" % sb2)], writes=[K("XB%d" % row)])
        S.dve(lambda e: e.bn_stats(bst[:, sub, :], px[:, 256:512]), reads=[PK("px")], writes=[K("bst")])
        S.dve(lambda e: e.bn_aggr(mv[:, sub, :], bst[:, sub, :]), reads=[K("bst")], writes=[K("mv")])
        S.act(lambda e: e.activation(vrs[:, sub:sub + 1], mv[:, sub, 1:2], AF.Sqrt, bias=epst[:], scale=1.0),
              reads=[K("mv"), K("epst")], writes=[K("vrs")])
        S.dve(lambda e: e.reciprocal(vrs[:, sub:sub + 1], vrs[:, sub:sub + 1]), reads=[K("vrs")], writes=[K("vrs")])
        S.dve(lambda e: e.tensor_scalar(vnst[sb2][:], px[:, 256:512], mv[:, sub, 0:1], vrs[:, sub:sub + 1],
                                        ALU.subtract, ALU.mult),
              reads=[PK("px"), K("mv"), K("vrs")], writes=[K("vnst%d" % sb2)])
        S.dma("sp", lambda e: e.dma_start(out=VN[row:row + 128, :], in_=vnst[sb2][:]),
              reads=[K("vnst%d" % sb2)], writes=[K("VN%d" % row)])

    def fm_tile(lt, part):
        hb = lt % 2
        if part == 1:
            if "Vd2" in t:
                S.dma("sp", lambda e: e.dma_start(
                    out=t["Vd2"][lt // 2].rearrange("h p k d -> p h (k d)")[:, :, (lt % 2) * 256:(lt % 2) * 256 + 256],
                    in_=vst[hb][:].rearrange("p h s d -> p h (s d)")),
                    reads=[K("vst%d_%d" % (hb, s_)) for s_ in range(4)], writes=[K("Vd%d_%d" % (lt, hh)) for hh in range(8)])
            else:
                for hh in range(8):
                    S.dma("sp", lambda e, hh=hh: e.dma_start(out=Vd[hh, :, lt * 4:(lt + 1) * 4, :], in_=vst[hb][:, hh, :, :]),
                          reads=[K("vst%d_%d" % (hb, s_)) for s_ in range(4)], writes=[K("Vd%d_%d" % (lt, hh))])
        HNTA = [K("hnT%d_%d" % (lt, s_)) for s_ in range(4)]
        tsl = slice(lt * 512, (lt + 1) * 512)

        def fproj(col):
            pb_ = state["nfm"] % 2
            fb_ = state["nfm"] % 3
            state["nfm"] += 1
            for c in range(8):
                S.pe(lambda e, c=c: e.matmul(pf[pb_][:], wb[:, c, col:col + 128], hnT[lt][:, c, :],
                                             start=(c == 0), stop=(c == 7)),
                     reads=HNTA + [(WBQ if col < C_GA else WBR)[c]], writes=[PK("pf%d" % pb_)])
            return pb_, fb_

        for i in range(4 if part == 1 else 0):
            pb_, fb_ = fproj(C_K + i * 128)
            S.dve(lambda e, pb_=pb_, fb_=fb_: e.tensor_copy(fst[fb_][:], pf[pb_][:]), reads=[PK("pf%d" % pb_)],
                  writes=[K("fst%d" % fb_)])
            S.dma("sp", lambda e, fb_=fb_, i=i: e.dma_start(
                out=(t["KT2"][lt // 2, 2 * i:2 * i + 2, :, (lt % 2) * 512:(lt % 2) * 512 + 512] if "KT2" in t
                     else KT[2 * i:2 * i + 2, :, tsl]).rearrange("h d t -> (h d) t"), in_=fst[fb_][:]),
                reads=[K("fst%d" % fb_)], writes=[K("KT%d_%d" % (lt, i))])
        if part == 1:
            return
        for i in range(2):
            pb_, fb_ = fproj(C_GB + i * 128)
            S.act(lambda e, pb_=pb_, fb_=fb_: e.activation(fst[fb_][:], pf[pb_][:], AF.Silu), reads=[PK("pf%d" % pb_)],
                  writes=[K("fst%d" % fb_)])
            S.dma("act", lambda e, fb_=fb_, i=i: e.dma_start(out=GBT[i * 128:(i + 1) * 128, tsl], in_=fst[fb_][:]),
                  reads=[K("fst%d" % fb_)], writes=[K("GBT%d_%d" % (lt, i))])
        for i in range(2):
            pb_ = state["nfm"] % 2
            state["nfm"] += 1
            for c in range(8):
                S.pe(lambda e, c=c, i=i, pb_=pb_: e.matmul(pf[pb_][:], wb[:, c, C_GC + i * 128:C_GC + (i + 1) * 128],
                                                           hnT[lt][:, c, :], start=(c == 0), stop=(c == 7)),
                     reads=HNTA + [WBR[c]], writes=[PK("pf%d" % pb_)])
            S.act(lambda e, pb_=pb_, i=i: e.activation(sgc[:, i, :], pf[pb_][:], AF.Silu), reads=[PK("pf%d" % pb_)],
                  writes=[K("sgc%d" % i)])
        for i in range(2):
            pb_, fb_ = fproj(C_UC + i * 128)
            S.dve(lambda e, pb_=pb_, fb_=fb_, i=i: e.tensor_tensor(fst[fb_][:], pf[pb_][:], sgc[:, i, :], ALU.mult),
                  reads=[PK("pf%d" % pb_), K("sgc%d" % i)], writes=[K("fst%d" % fb_)])
            S.dma("sp", lambda e, fb_=fb_, i=i: e.dma_start(out=UGT[i * 128:(i + 1) * 128, tsl], in_=fst[fb_][:]),
                  reads=[K("fst%d" % fb_)], writes=[K("UGT%d_%d" % (lt, i))])

    norm_tile(0)
    tr_sub(0, 0)
    for lt in range(ntiles):
        if lt + 1 < ntiles:
            norm_tile(lt + 1)
        for sub in range(4):
            if sub < 3:
                tr_sub(lt, sub + 1)
            elif lt + 1 < ntiles:
                tr_sub(lt + 1, 0)
            proj_sub(lt, sub, 1)
        fm_tile(lt, 1)
        if "after_tile" in t:
            t["after_tile"](lt, S, K)
    for n2, lt in enumerate([ntiles - 1] + list(range(ntiles - 1))):
        for sub in range(4):
            proj_sub(lt, sub, 2)
        if lt == ntiles - 1 and "after_xb" in t:
            t["after_xb"](S, K)
        fm_tile(lt, 2)
        if "after_p2" in t:
            t["after_p2"](n2)


def build_a(**kw):
    nc = bass.Bass("TRN2", target_bir_lowering=False)
    with ExitStack() as es:
        C = Ctx(nc, es)
        t = dict(
            h=C.din("h", [NTOK, D], F32), w_in=C.din("w_in", [D, DIN], F32), g=C.din("g", [128, 8], F32),
            ident=C.din("ident", [128, 128], BF16),
            Qtok=C.dout("Qtok", [NTOK, 512], BF16), SGA=C.dout("SGA", [NTOK, 512], BF16),
            VN=C.dout("VN", [NTOK, 256], BF16), XB=C.dout("XB", [NTOK, 256], BF16),
            KT=C.dout("KT", [8, 64, NTOK], BF16), Vd=C.dout("Vd", [8, 128, 16, 64], BF16),
            GBT=C.dout("GBT", [256, NTOK], BF16), UGT=C.dout("UGT", [256, NTOK], BF16))
        S = Sched(nc)
        emit_phase_a(nc, S, C, t, **kw)
        S.emit()
    return nc


def emit_phase_b1(nc, S, C, t, pfx="", ntile=16, nheads=2):
    rb, Os = t["rb"], t.get("Os")
    if "Qsrc" in t:
        Qsrc, Ksrc, Vsrc = t["Qsrc"], t["Ksrc"], t["Vsrc"]
    else:
        Qg, KTg, Vdg = t["Qg"], t["KTg"], t["Vdg"]
        Qsrc = lambda e, s_: Qg[s_].rearrange("(k p) c -> p k c", p=128)
        Ksrc = lambda e, s_, hl: KTg[s_, hl]
        Vsrc = lambda e, s_, hl: Vdg[s_, hl]
    XK = list(t.get("xkeys", []))
    ident, identf, Erows, onehot, masks, tabd = (t[k] for k in ("ident", "identf", "Erows", "onehot", "masks", "tabd"))
    K = lambda s: pfx + s
    PK = lambda s: "PS:" + pfx + s
    idt = C.sb(K("idt"), [128, 128], BF16)
    idf = C.sb(K("idf"), [128, 128], F32)
    Kaug = [C.sb(K("Kaug%d" % i), [96, S_LEN], BF16) for i in range(2)]
    Vaug = [C.sb(K("Vaug%d" % i), [128, 64, 66], BF16) for i in range(2)]
    Qall = C.sb(K("Qall"), [128, 64, 128], BF16)
    Oall = C.sb(K("Oall"), [128, 64, 128], BF16)
    TB = C.sb(K("TB"), [128, 2, 6, 512], BF16)
    msk = C.sb(K("msk"), [128, 3, 1024], F32)
    ksf = C.sb(K("ksf"), [64, 32], F32)
    ksb = [C.sb(K("ksb%d" % i), [64, 32], BF16) for i in range(2)]
    rbx = C.sb(K("rbx"), [32, 2], F32)
    rbrep = [C.sb(K("rbrep%d" % i), [33, 128], F32) for i in range(2)]
    oh = C.sb(K("oh"), [33, NTAB], F32)
    tabf = C.sb(K("tabf"), [128, 2, NTAB], BF16)
    Qaug = [C.sb(K("Qaug%d" % i), [96, 512], BF16) for i in range(2)]
    gsb = C.sb(K("gsb"), [128, 4, 32], F32)
    top8 = C.sb(K("top8"), [128, 4, 8], F32)
    sel = C.sb(K("sel"), [128, 4, 32], F32)
    Mtok = C.sb(K("Mtok"), [128, 4, 32], BF16)
    PT = [C.sb(K("PT%d" % i), [128, 512], BF16) for i in range(4)]
    osb = [C.sb(K("osb%d" % i), [65, 512], F32) for i in range(2)]
    rden = C.sb(K("rden"), [128, 4], F32)
    SPS = [C.ps(K("sps%d" % i), [128, 512], F32) for i in range(4)]
    OT = [C.ps(K("ot%d" % i), [128, 512], F32) for i in range(2)]
    PQM = C.ps(K("pqm"), [128, 1024], BF16)
    PGO = C.ps(K("pgo"), [128, 512], F32)

    S.dma("sp", lambda e: e.dma_start(out=idt[:], in_=ident), writes=[K("idt")])
    S.dma("sp", lambda e: e.dma_start(out=idf[:], in_=identf), writes=[K("idf")])
    S.dma("sp", lambda e: e.dma_start(out=msk[:], in_=masks.rearrange("m p n -> p m n")), writes=[K("msk")])
    S.dma("sp", lambda e: e.dma_start(out=oh[:], in_=onehot), writes=[K("oh")])
    if "Qdirect" in t:
        for j in range(2):
            S.dma("act", lambda e, j=j: e.dma_start(
                out=Qall[:, :, :].rearrange("p (s j k) c -> p s j (k c)", s=4, j=2, k=8)[:, :, j, :],
                in_=t["Qdirect"](e, j)), reads=XK + ["QsG%d" % j], writes=[K("Qall%d_%d" % (s_, j)) for s_ in range(4)])
    else:
        for s_ in range(4):
            S.dma("sp", lambda e, s_=s_: e.dma_start(out=Qall[:, s_ * 16:(s_ + 1) * 16, :], in_=Qsrc(e, s_)),
                  reads=XK, writes=[K("Qall%d_0" % s_), K("Qall%d_1" % s_)])
    for hl in range(nheads):
        S.dma("sp", lambda e, hl=hl: e.dma_start(out=Kaug[hl][64:96, :], in_=Erows), writes=[K("KE%d" % hl)])
        for s in range(4):
            for hf in range(2):
                if "Kdirect" in t:
                    if s == 0:
                        S.dma("act", lambda e, hl=hl, hf=hf: e.dma_start(
                            out=Kaug[hl][0:64, :].rearrange("p (s j t) -> p s j t", s=4, j=2)[:, :, hf, :],
                            in_=t["Kdirect"](e, hl, hf)), reads=XK + ["KTG%d" % hf],
                            writes=[K("Kd%d_%d_%d" % (hl, s_, hf)) for s_ in range(4)])
                else:
                    S.dma("sp", lambda e, hl=hl, s=s, hf=hf: e.dma_start(
                        out=Kaug[hl][0:64, s * 2048 + hf * 1024:s * 2048 + (hf + 1) * 1024],
                        in_=Ksrc(e, s, hl)[:, hf * 1024:(hf + 1) * 1024]), reads=XK,
                        writes=[K("Kd%d_%d_%d" % (hl, s, hf))])
                S.dma("sp", lambda e, hl=hl, s=s, hf=hf: e.dma_start(
                    out=Vaug[hl][:, s * 16 + hf * 8:s * 16 + (hf + 1) * 8, 0:64],
                    in_=Vsrc(e, s, hl)[:, hf * 8:(hf + 1) * 8, :]), reads=XK, writes=[K("Vd%d_%d_%d" % (hl, s, hf))])
        S.pool(lambda e, hl=hl: e.memset(Vaug[hl][:, :, 64:65], 1.0), writes=[K("V1%d" % hl)])
    S.dma("sp", lambda e: e.dma_start(out=rbx[0:32, :], in_=rb), writes=[K("rbx")])
    for hl in range(nheads):
        S.pool(lambda e, hl=hl: e.memset(rbrep[hl][:, :], 1.0), writes=[K("rbrep%d" % hl)])
        S.dve(lambda e, hl=hl: e.tensor_scalar(rbrep[hl][0:32, :], rbrep[hl][0:32, :], rbx[0:32, hl:hl + 1], None,
                                               ALU.mult),
              reads=[K("rbrep%d" % hl), K("rbx")], writes=[K("rbrep%d" % hl)])
        for ci, (c0, c1) in enumerate(((0, 512), (512, 1024), (1024, NTAB))):
            S.pe(lambda e, hl=hl, c0=c0, c1=c1: e.matmul(SPS[0][:, 0:c1 - c0], rbrep[hl][:, :], oh[:, c0:c1],
                                                         start=True, stop=True),
                 reads=[K("rbrep%d" % hl), K("oh")], writes=[PK("sps0")])
            S.dve(lambda e, hl=hl, c0=c0, c1=c1: e.tensor_copy(tabf[:, hl, c0:c1], SPS[0][:, 0:c1 - c0]),
                  reads=[PK("sps0")], writes=[K("tabf%d_%d" % (hl, ci))])
        S.dma("sp", lambda e, hl=hl: e.dma_start(out=tabd[hl], in_=tabf[:, hl, :]),
              reads=[K("tabf%d_%d" % (hl, ci)) for ci in range(3)], writes=[K("tabd%d" % hl)])
        for i in range(6):
            Dv = 256 - 128 * i
            src = bass.AP(tensor=tabd.tensor, offset=hl * 128 * NTAB + TOFF + Dv, ap=[[NTAB - 1, 128], [1, 512]])
            S.dma("sp", lambda e, hl=hl, i=i, src=src: e.dma_start(out=TB[:, hl, i, :], in_=src),
                  reads=[K("tabd%d" % hl)], writes=[K("TB%d" % hl)])
    KALL = lambda hl: [K("KE%d" % hl)] + [K("Kd%d_%d_%d" % (hl, s, hf)) for s in range(4) for hf in range(2)]
    VALL = lambda hl: [K("V1%d" % hl)] + [K("Vd%d_%d_%d" % (hl, s, hf)) for s in range(4) for hf in range(2)]

    def build_steps(idx, hl, T):
        qb = idx % 2
        QA, QM = K("QA%d" % qb), K("QM%d" % qb)
        steps = []

        def s1():
            for sub in range(4):
                blk = T * 4 + sub
                S.pe(lambda e, sub=sub, blk=blk: e.transpose(PQM[0:64, sub * 128:(sub + 1) * 128],
                                                             Qall[:, blk, hl * 64:(hl + 1) * 64], idt[:]),
                     reads=[K("Qall%d_0" % (blk // 16)), K("Qall%d_1" % (blk // 16)), K("idt")], writes=[PK("pqm")])
            S.act(lambda e: e.mul(Qaug[qb][0:64, :], PQM[0:64, 0:512], 0.125), reads=[PK("pqm")], writes=[QA])

        def s2():
            for sub in range(4):
                S.pe(lambda e, sub=sub: e.matmul(PGO[:, sub * 32:(sub + 1) * 32],
                                                 Qaug[qb][0:64, sub * 128:(sub + 1) * 128], ksb[hl][:, :],
                                                 start=True, stop=True),
                     reads=[QA, K("ksb%d" % hl)], writes=[PK("pgo")])

        def s3():
            for sub in range(4):
                own = 2 * T + sub // 2
                osl = slice(own * 32, (own + 1) * 32)
                S.dve(lambda e, sub=sub, osl=osl: e.tensor_tensor(gsb[:, sub, :], PGO[:, sub * 32:(sub + 1) * 32],
                                                                  msk[:, 0, osl], ALU.add),
                      reads=[PK("pgo"), K("msk")], writes=[K("gsb")])
                S.dve(lambda e, sub=sub: e.max(top8[:, sub, :], gsb[:, sub, :]), reads=[K("gsb")], writes=[K("top8")])
                S.dve(lambda e, sub=sub, osl=osl: e.scalar_tensor_tensor(sel[:, sub, :], gsb[:, sub, :],
                                                                         top8[:, sub, 2:3], msk[:, 1, osl],
                                                                         ALU.is_ge, ALU.mult),
                      reads=[K("gsb"), K("top8"), K("msk")], writes=[K("sel")])
                S.dve(lambda e, sub=sub, osl=osl: e.tensor_tensor(sel[:, sub, :], sel[:, sub, :], msk[:, 2, osl],
                                                                  ALU.add),
                      reads=[K("sel"), K("msk")], writes=[K("sel")])
                S.dve(lambda e, sub=sub: e.tensor_scalar(Mtok[:, sub, :], sel[:, sub, :], -1.0, -NEGM,
                                                         ALU.add, ALU.mult),
                      reads=[K("sel")], writes=[K("Mtok")])

        def s4():
            for sub in range(4):
                S.pe(lambda e, sub=sub: e.transpose(PQM[0:32, 512 + sub * 128:512 + (sub + 1) * 128],
                                                    Mtok[:, sub, :], idt[:]),
                     reads=[K("Mtok"), K("idt")], writes=[PK("pqm")])
            S.act(lambda e: e.copy(Qaug[qb][64:96, :], PQM[0:32, 512:1024]), reads=[PK("pqm")], writes=[QM])

        return [s1, s2, s3, s4]

    tiles = [(hl, T) for T in range(ntile) for hl in range(nheads)]
    if nheads < 2 or ntile < 16:
        S.pool(lambda e: e.memset(Oall[:], 0.0),
               writes=[K("Oall%d_%d" % (h_, b_)) for h_ in range(2) for b_ in range(64)])
    state = {'nkt_total': 0}
    for hl in range(nheads):
        S.dve(lambda e, hl=hl: e.tensor_reduce(ksf[:, :], Kaug[hl][0:64, :].rearrange("p (j k) -> p j k", k=256),
                                               AX.X, ALU.add),
              reads=KALL(hl), writes=[K("ksf")])
        S.dve(lambda e, hl=hl: e.tensor_copy(ksb[hl][:, :], ksf[:, :]), reads=[K("ksf")], writes=[K("ksb%d" % hl)])

    NB = len(SPS)
    LAG = NB - 1
    steps = []
    for idx, (hl, T) in enumerate(tiles):
        nkt = 4 * T + 4
        for kt in range(nkt):
            steps.append((idx, hl, T, kt, nkt))
    first_step = {}
    for gi, st in enumerate(steps):
        first_step.setdefault(st[0], gi)
    actions = {}

    def at(gi, fn):
        actions.setdefault(max(gi, 0), []).append(fn)

    for st in build_steps(0, *tiles[0]):
        at(0, st)
    for idx in range(1, len(tiles)):
        f0 = first_step[idx - 1]
        n_prev = first_step[idx] - f0
        ready_by = first_step[idx] - LAG
        bs = build_steps(idx, *tiles[idx])
        offs = (0, 2, 3, 6)
        for k_, st in enumerate(bs):
            at(min(f0 + offs[k_], ready_by), st)

    def col0(T, kt):
        return max(0, kt - 4 * T) * 128

    def emit_qk(gi):
        idx, hl, T, kt, nkt = steps[gi]
        sb = gi % NB
        qb = idx % 2
        QA, QM = K("QA%d" % qb), K("QM%d" % qb)
        special = kt >= 4 * T - 1
        c0 = col0(T, kt)
        S.pe(lambda e: e.matmul(SPS[sb][:, c0:512], Kaug[hl][0:96, kt * 128:(kt + 1) * 128], Qaug[qb][0:96, c0:512],
                                start=True, stop=not special),
             reads=KALL(hl) + [QA, QM], writes=[PK("sps%d" % sb)])
        if special:
            i = kt - (4 * T - 2)
            S.pe(lambda e: e.matmul(SPS[sb][:, c0:512], idt[:, :], TB[:, hl, i, c0:512], start=False, stop=True),
                 reads=[K("idt"), K("TB%d" % hl)], writes=[PK("sps%d" % sb)])
        S.act(lambda e: e.activation(PT[sb][:, c0:512], SPS[sb][:, c0:512], AF.Exp),
              reads=[PK("sps%d" % sb)], writes=[K("PT%d" % sb)])

    def emit_pv(gi):
        idx, hl, T, kt, nkt = steps[gi]
        sb = gi % NB
        ob = idx % 2
        c0 = col0(T, kt)
        S.pe(lambda e: e.matmul(OT[ob][0:65, c0:512], Vaug[hl][:, kt, 0:65], PT[sb][:, c0:512],
                                start=(kt == 0), stop=(kt == nkt - 1)),
             reads=VALL(hl) + [K("PT%d" % sb)], writes=[PK("ot%d" % ob)])
        if kt == nkt - 1:
            S.act(lambda e: e.copy(osb[ob][:, :], OT[ob][0:65, :]), reads=[PK("ot%d" % ob)], writes=[K("osb%d" % ob)])

            def epi2():
                for sub in range(4):
                    S.pe(lambda e, sub=sub: e.transpose(PGO[:, 128 + sub * 66:128 + sub * 66 + 65],
                                                        osb[ob][0:65, sub * 128:(sub + 1) * 128], idf[0:65, 0:65]),
                         reads=[K("osb%d" % ob), K("idf")], writes=[PK("pgo")])
                otp = PGO[:, 128:392].rearrange("p (s c) -> p s c", c=66)
                S.dve(lambda e: e.reciprocal(rden[:, :], otp[:, :, 64]), reads=[PK("pgo")], writes=[K("rden")])
                for sub in range(4):
                    blk = T * 4 + sub
                    S.dve(lambda e, sub=sub, blk=blk: e.tensor_scalar(Oall[:, blk, hl * 64:(hl + 1) * 64],
                                                                      otp[:, sub, 0:64], rden[:, sub:sub + 1], None,
                                                                      ALU.mult),
                          reads=[PK("pgo"), K("rden")], writes=[K("Oall%d_%d" % (hl, blk))])
                if "after_epi" in t:
                    t["after_epi"](S, K, hl, T)
                if "Os2" in t and hl == nheads - 1 and T % 2 == 1:
                    d_, th = T // 4, (T % 4) // 2
                    S.dma("sp", lambda e: e.dma_start(
                        out=t["Os2"][th, d_].rearrange("(k p) c -> p k c", p=128),
                        in_=Oall[:, d_ * 16 + th * 8:d_ * 16 + th * 8 + 8, :]),
                        reads=[K("Oall%d_%d" % (h_, b_)) for h_ in range(nheads)
                               for b_ in range(d_ * 16 + th * 8, d_ * 16 + th * 8 + 8)],
                        writes=[K("Os%d_%d" % (d_, th))])
                    if "after_out" in t and d_ == 3 and th == 0:
                        t["after_out"](S, K, th)
            at(gi + LAG + 3, epi2)

    nsteps = len(steps)
    for gi in range(nsteps + LAG + 4):
        for fn in actions.pop(gi, []):
            fn()
        if gi < nsteps:
            emit_qk(gi)
        if 0 <= gi - LAG < nsteps:
            emit_pv(gi - LAG)
    for gi in sorted(actions):
        for fn in actions[gi]:
            fn()
    OKEYS = [K("Oall%d_%d" % (hl, blk)) for hl in range(nheads) for blk in range(4 * ntile)]
    if "Os2" not in t:
        S.dma("sp", lambda e: e.dma_start(out=Os.rearrange("d (k p) c -> p (d k) c", p=128)[:, 0:4 * ntile, :],
                                          in_=Oall[:, 0:4 * ntile, :]), reads=OKEYS, writes=[K("Os")])


def build_b1(**kw):
    nc = bass.Bass("TRN2", target_bir_lowering=False)
    with ExitStack() as es:
        C = Ctx(nc, es)
        t = dict(
            Qg=C.din("Qg", [4, NTOK, 128], BF16), KTg=C.din("KTg", [4, 2, 64, NTOK], BF16),
            Vdg=C.din("Vdg", [4, 2, 128, 16, 64], BF16), rb=C.din("rb", [32, 2], F32),
            ident=C.din("ident", [128, 128], BF16), identf=C.din("identf", [128, 128], F32),
            Erows=C.din("Erows", [32, S_LEN], BF16), onehot=C.din("onehot", [33, NTAB], F32),
            masks=C.din("masks", [3, 128, 1024], F32),
            tabd=nc.dram_tensor("tabd", [2, 128, NTAB], BF16, kind="Internal").ap(),
            Os=C.dout("Os", [4, NTOK, 128], BF16))
        S = Sched(nc)
        emit_phase_b1(nc, S, C, t, **kw)
        S.emit()
    return nc


def t5_bucket_np(n):
    n = np.asarray(n)
    nf = np.maximum(n, 1).astype(np.float32)
    large = 16 + (np.log(nf / 16) / np.float32(np.log(128 / 16)) * 16).astype(np.int32)
    large = np.minimum(large, 31)
    return np.where(n < 16, n, large)


def make_consts():
    c = {}
    c["ident"] = np.eye(128, dtype=np.float32).astype(ml_dtypes.bfloat16)
    c["identf"] = np.eye(128, dtype=np.float32)
    E = np.zeros((32, S_LEN), np.float32)
    for j in range(32):
        E[j, j * 256:(j + 1) * 256] = 1.0
    c["Erows"] = E.astype(ml_dtypes.bfloat16)
    n = np.arange(NTAB) - TOFF
    bk = t5_bucket_np(np.maximum(n, 0))
    oh = np.zeros((33, NTAB), np.float32)
    oh[bk, np.arange(NTAB)] = 1.0
    oh[31, :] -= 1.0
    oh[:32, n < 0] = 0.0
    oh[32, n < 0] = NEGM
    c["onehot"] = oh
    j = np.arange(32)[None, :]
    o = np.arange(32)[:, None]
    m = np.zeros((3, 32, 32), np.float32)
    m[0] = np.where(j >= o, -1e30, 0.0)
    m[1] = (j < o)
    m[2] = (j == o)
    c["masks"] = np.ascontiguousarray(np.broadcast_to(m.reshape(3, 1, 1024), (3, 128, 1024)))
    return c


def emit_phase_b2(nc, S, C, t, pfx="", last=False, nblk=16):
    K = lambda s: pfx + s
    PK = lambda s: "PS:" + pfx + s
    Og = t.get("Og")
    SGA, VN, XB, GBT, UGT = (t[k] for k in ("SGA", "VN", "XB", "GBT", "UGT"))
    XBlast, halosel, Afirst, Amat, trilm = (t[k] for k in ("XBlast", "halosel", "Afirst", "Amat", "trilm"))
    h, p, hout = t["h"], t["p"], t["hout"]
    idt = C.sb(K("idt"), [128, 128], BF16)
    Oall = C.sb(K("Oall"), [128, 16, 512], BF16)
    SGAs = C.sb(K("SGAs"), [128, 16, 512], BF16)
    VNs = C.sb(K("VNs"), [128, 16, 256], BF16)
    XBs = C.sb(K("XBs"), [128, 16, 256], BF16)
    GBs = C.sb(K("GBs"), [128, 2, NTOK], BF16)
    UGs = C.sb(K("UGs"), [128, 2, NTOK], BF16)
    XBl = C.sb(K("XBl"), [128, 4, 256], BF16)
    hs = C.sb(K("hs"), [128, 4], F32)
    xacc = C.sb(K("xacc"), [128, 256], F32)
    xbp0 = C.sb(K("xbp0"), [128, 256], BF16)
    Af = C.sb(K("Af"), [128, 4, 128], BF16)
    Am = C.sb(K("Am"), [128, 4, 2, 128], BF16)
    wpre = t.get("wpre")
    if wpre is None:
        wst = [C.sb(K("wst%d" % i), [128, 1024], F32) for i in range(4)]
        wob = C.sb(K("wob"), [128, 8, 1024], BF16)
        wgb = C.sb(K("wgb"), [128, 8, 1024], BF16)
        wpb = C.sb(K("wpb"), [128, 2, 1024], BF16)
    else:
        wob, wgb, wpb = wpre["wob"], wpre["wgb"], wpre["wpb"]
    wbdf = C.sb(K("wbdf"), [128, 2, 128], F32)
    wbd = C.sb(K("wbd"), [128, 2, 128], BF16)
    tril = C.sb(K("tril"), [128, 128], F32)
    swf = C.sb(K("swf"), [128, 4, 128], F32)
    swb = C.sb(K("swb"), [128, 4, 128], BF16)
    WsT = C.sb(K("WsT"), [128, 4, 128], BF16)
    bsf = C.sb(K("bsf"), [1, 512], F32)
    bsb = C.sb(K("bsb"), [1, 512], BF16)
    ones = C.sb(K("ones"), [1, 64], BF16)
    psc = C.sb(K("psc"), [128, 2], F32)
    yab = [C.sb(K("yab%d" % i), [128, 512], BF16) for i in range(2)]
    ycT = [C.sb(K("ycT%d" % i), [128, 8, 128], BF16) for i in range(2)]
    plsb = C.sb(K("plsb"), [128, 2, 128], BF16)
    ht = [C.sb(K("ht%d" % i), [128, D], F32) for i in range(3)]
    pt = [C.sb(K("pt%d" % i), [128, 256], F32) for i in range(2)]
    pb = [C.sb(K("pb%d" % i), [128, 256], BF16) for i in range(2)]
    pT = [C.sb(K("pT%d" % i), [128, 2, 128], BF16) for i in range(4)]
    hnew = [C.sb(K("hnew%d" % i), [128, D], F32) for i in range(3)]
    hb = [C.sb(K("hb%d" % i), [128, D], BF16) for i in range(2)]
    hT = [C.sb(K("hT%d" % i), [128, 8, 128], BF16) for i in range(2)]
    sig = C.sb(K("sig"), [128, D], F32)
    tmp = C.sb(K("tmp"), [128, D], F32)
    h2 = [C.sb(K("h2%d" % i), [128, D], F32) for i in range(2)]
    PT1 = C.ps(K("ps_t1"), [128, 1024], BF16)
    PT2 = C.ps(K("ps_t2"), [128, 8, 128], BF16)
    PPS = C.ps(K("pps"), [128, 512], F32)
    BO = [C.ps(K("bo%d" % i), [128, 512], F32) for i in range(2)]
    BG = [C.ps(K("bg%d" % i), [128, 512], F32) for i in range(2)]
    BP = C.ps(K("bp"), [128, 512], F32)
    if last:
        gfin = C.sb(K("gfin"), [128, D], F32)
        junk = C.sb(K("junk"), [128, D], F32)
        ss = C.sb(K("ss"), [128, 1], F32)
        epst = C.sb(K("epst"), [128, 1], F32)
        S.dma("sp", lambda e: e.dma_start(out=gfin[:], in_=t["final_g"].partition_broadcast(128)), writes=[K("gfin")])
        S.pool(lambda e: e.memset(epst[:], EPS), writes=[K("epst")])

    ld = lambda dst, src, key, q="sp": S.dma(q, lambda e: e.dma_start(out=dst, in_=src), writes=[key])
    ld(idt[:], t["ident"], K("idt"))
    ld(hs[:], halosel, K("hs"))
    S.dma("sp", lambda e: e.dma_start(out=XBl[:], in_=XBlast.rearrange("s p c -> p s c")), reads=list(t.get("xkeys", [])),
          writes=[K("XBl")])
    ld(Af[:], Afirst.rearrange("g s t -> s g t"), K("Af"))
    ld(Am[:], Amat.rearrange("g a s t -> s g a t"), K("Am"))
    ld(tril[:], trilm, K("tril"))
    ld(swf[:], t["sgu_w"].rearrange("h t s -> t h s"), K("swf"))
    ld(bsf[:], t["sgu_b"], K("bsf"))
    ld(psc[:], t["pscale"], K("psc"))
    S.pool(lambda e: e.memset(wbdf[:], 0.0), writes=[K("wbdf")])
    for g in range(4):
        c, gi = g // 2, g % 2
        S.dma("sp", lambda e, g=g, c=c, gi=gi: e.dma_start(out=wbdf[gi * 64:(gi + 1) * 64, c, gi * 64:(gi + 1) * 64],
                                                           in_=t["pool_w"][g]),
              reads=[K("wbdf")], writes=[K("wbdf%d" % g)])
    S.dve(lambda e: e.tensor_copy(wbd[:], wbdf[:]), reads=[K("wbdf")] + [K("wbdf%d" % g) for g in range(4)],
          writes=[K("wbd")])
    S.pool(lambda e: e.memset(ones[:], 1.0), writes=[K("ones")])
    S.dve(lambda e: e.tensor_copy(bsb[:], bsf[:]), reads=[K("bsf")], writes=[K("bsb")])
    for hh in range(4):
        S.dve(lambda e, hh=hh: e.tensor_tensor(swb[:, hh, :], swf[:, hh, :], tril[:], ALU.mult),
              reads=[K("swf"), K("tril")], writes=[K("swb%d" % hh)])
        S.pe(lambda e, hh=hh: e.transpose(PT1[:, hh * 128:(hh + 1) * 128], swb[:, hh, :], idt[:]),
             reads=[K("swb%d" % hh), K("idt")], writes=[PK("pt1")])
    S.act(lambda e: e.copy(WsT[:], PT1[:, 0:512].rearrange("p (h t) -> p h t", t=128)), reads=[PK("pt1")],
          writes=[K("WsT")])
    XK = list(t.get("xkeys", []))
    Osrc = t["Osrc"] if "Osrc" in t else (lambda e, hp: Og[hp].rearrange("(k p) c -> p k c", p=128))
    okeys = t.get("okeys", {})

    def load_oall(hf):
        for s_ in range(4):
            S.dma("sp", lambda e, s_=s_: e.dma_start(out=Oall[:, hf * 8:(hf + 1) * 8, s_ * 128:(s_ + 1) * 128],
                                                     in_=Osrc(e, s_)[:, hf * 8:(hf + 1) * 8, :]),
                  reads=XK + list(okeys.get(hf, [])), writes=[K("Oall%d_%d" % (s_, hf))])

    load_oall(0)
    ld(SGAs[:], SGA.rearrange("(k p) c -> p k c", p=128), K("SGAs"))
    ld(XBs[:], XB.rearrange("(k p) c -> p k c", p=128), K("XBs"))
    ld(VNs[:], VN.rearrange("(k p) c -> p k c", p=128), K("VNs"))
    ld(GBs[:], GBT.rearrange("(c p) t -> p c t", p=128), K("GBs"))
    ld(UGs[:], UGT.rearrange("(c p) t -> p c t", p=128), K("UGs"))
    if "after_loads" in t:
        t["after_loads"](S, [K("SGAs"), K("XBs"), K("VNs"), K("GBs"), K("UGs")] + [K("Oall%d_0" % s_) for s_ in range(4)])
    nw = 0
    for (src, dstt, nch, key) in ((t["w_out"], wob, 8, "wob"), (t["ple_gate_w"], wgb, 8, "wgb"), (t["ple_w"], wpb, 2, "wpb")):
        for c in range(nch if wpre is None else 0):
            b = nw % 4
            nw += 1
            S.dma("sp", lambda e, src=src, c=c, b=b: e.dma_start(out=wst[b][:], in_=src[c * 128:(c + 1) * 128, :]),
                  writes=[K("wst%d" % b)])
            if c % 2 == 0:
                S.act(lambda e, dstt=dstt, c=c, b=b: e.copy(dstt[:, c, :], wst[b][:]), reads=[K("wst%d" % b)],
                      writes=[K("%s%d" % (key, c))])
            else:
                S.dve(lambda e, dstt=dstt, c=c, b=b: e.tensor_copy(dstt[:, c, :], wst[b][:]), reads=[K("wst%d" % b)],
                      writes=[K("%s%d" % (key, c))])
    WOB = [K("wob%d" % c) for c in range(8)]
    WGB = [K("wgb%d" % c) for c in range(8)]
    WPB = [K("wpb%d" % c) for c in range(2)]
    S.dve(lambda e: e.tensor_scalar(xacc[:], XBl[:, 0, :], hs[:, 0:1], None, ALU.mult), reads=[K("XBl"), K("hs")],
          writes=[K("xacc")])
    for s in range(1, 4):
        S.dve(lambda e, s=s: e.scalar_tensor_tensor(xacc[:], XBl[:, s, :], hs[:, s:s + 1], xacc[:], ALU.mult, ALU.add),
              reads=[K("XBl"), K("hs"), K("xacc")], writes=[K("xacc")])
    S.dve(lambda e: e.tensor_copy(xbp0[:], xacc[:]), reads=[K("xacc")], writes=[K("xbp0")])

    hv = h.rearrange("(k p) f -> k p f", p=128)
    pv = p.rearrange("(k p) f -> k p f", p=128)
    ov = hout.rearrange("(k p) f -> k p f", p=128)

    def stageP1(blk):
        b2, b3, b4 = blk % 2, blk % 3, blk % 4
        HT, PTK = K("ht%d" % b3), K("ptk%d" % b2)
        S.dma("sp", lambda e: e.dma_start(out=ht[b3][:], in_=hv[blk]), writes=[HT])
        S.dma("sp", lambda e: e.dma_start(out=pt[b2][:], in_=pv[blk]), writes=[PTK])
        S.dve(lambda e: e.tensor_tensor(yab[b2][:], Oall[:, blk, :], SGAs[:, blk, :], ALU.mult),
              reads=[K("Oall%d_%d" % (s_, blk // 8)) for s_ in range(4)] + [K("SGAs")], writes=[K("yab%d" % b2)])
        S.act(lambda e: e.copy(pb[b2][:], pt[b2][:]), reads=[PTK], writes=[K("pb%d" % b2)])
        for c in range(4):
            S.pe(lambda e, c=c: e.transpose(PT1[:, c * 128:(c + 1) * 128], yab[b2][:, c * 128:(c + 1) * 128], idt[:]),
                 reads=[K("yab%d" % b2), K("idt")], writes=[PK("pt1")])
        for c in range(2):
            S.pe(lambda e, c=c: e.transpose(PT1[:, 512 + c * 128:512 + (c + 1) * 128], pb[b2][:, c * 128:(c + 1) * 128],
                                            idt[:]),
                 reads=[K("pb%d" % b2), K("idt")], writes=[PK("pt1")])
        S.act(lambda e: e.copy(ycT[b2][:, 0:4, :], PT1[:, 0:512].rearrange("p (c t) -> p c t", t=128)),
              reads=[PK("pt1")], writes=[K("ycT%da" % b2)])
        S.act(lambda e: e.copy(pT[b4][:], PT1[:, 512:768].rearrange("p (c t) -> p c t", t=128)), reads=[PK("pt1")],
              writes=[K("pT%d" % b4)])
        for c in range(2):
            for gi in range(2):
                g = 2 * c + gi
                osl = slice(gi * 64, (gi + 1) * 64)
                csl = slice(c * 128, (c + 1) * 128)
                a_d = Af[:, g, :] if blk == 0 else Am[:, g, 0, :]
                S.pe(lambda e, g=g, osl=osl, a_d=a_d, csl=csl: e.matmul(PPS[osl, csl], XBs[:, blk, g * 64:(g + 1) * 64], a_d,
                                                                       start=True, stop=False),
                     reads=[K("XBs"), K("Af"), K("Am")], writes=[PK("pps")])
                if blk == 0:
                    S.pe(lambda e, g=g, osl=osl, csl=csl: e.matmul(PPS[osl, csl], xbp0[64:128, g * 64:(g + 1) * 64],
                                                                   Am[64:128, g, 1, :], start=False, stop=True),
                         reads=[K("xbp0"), K("Am")], writes=[PK("pps")])
                else:
                    S.pe(lambda e, g=g, osl=osl, csl=csl: e.matmul(PPS[osl, csl], XBs[64:128, blk - 1, g * 64:(g + 1) * 64],
                                                                   Am[64:128, g, 1, :], start=False, stop=True),
                         reads=[K("XBs"), K("Am")], writes=[PK("pps")])
        for c in range(2):
            for hi in range(2):
                hh = 2 * c + hi
                osl = slice(hi * 64, (hi + 1) * 64)
                csl = slice(256 + c * 128, 256 + (c + 1) * 128)
                S.pe(lambda e, hh=hh, osl=osl, csl=csl: e.matmul(PPS[osl, csl], VNs[:, blk, hh * 64:(hh + 1) * 64],
                                                                 WsT[:, hh, :], start=True, stop=False),
                     reads=[K("VNs"), K("WsT")], writes=[PK("pps")])
                S.pe(lambda e, hh=hh, osl=osl, csl=csl: e.matmul(PPS[osl, csl], ones[0:1, :],
                                                                 bsb[0:1, hh * 128:(hh + 1) * 128], start=False, stop=True),
                     reads=[K("ones"), K("bsb")], writes=[PK("pps")])
        tsl = slice(blk * 128, (blk + 1) * 128)
        S.dve(lambda e: e.tensor_copy(plsb[:], PPS[:, 0:256].rearrange("p (c t) -> p c t", t=128)), reads=[PK("pps")],
              writes=[K("plsb")])
        for c in range(2):
            S.dve(lambda e, c=c: e.tensor_tensor(ycT[b2][:, 6 + c, :], PPS[:, 256 + c * 128:256 + (c + 1) * 128],
                                                 UGs[:, c, tsl], ALU.mult),
                  reads=[PK("pps"), K("UGs")], writes=[K("ycT%dc%d" % (b2, c))])

    def stageP2(blk):
        b2 = blk % 2
        tsl = slice(blk * 128, (blk + 1) * 128)
        for c in range(2):
            S.pe(lambda e, c=c: e.matmul(PPS[:, c * 128:(c + 1) * 128], wbd[:, c, :], plsb[:, c, :], start=True, stop=True),
                 reads=[K("wbd"), K("plsb")], writes=[PK("pps")])
        for c in range(2):
            S.dve(lambda e, c=c: e.scalar_tensor_tensor(ycT[b2][:, 4 + c, :], PPS[:, c * 128:(c + 1) * 128], psc[:, c:c + 1],
                                                        GBs[:, c, tsl], ALU.mult, ALU.mult),
                  reads=[PK("pps"), K("psc"), K("GBs")], writes=[K("ycT%db%d" % (b2, c))])

    def stageO(blk):
        b2, b3 = blk % 2, blk % 3
        HT = K("ht%d" % b3)
        YC = K("ycT%d" % b2)
        YCALL = [YC + "a", YC + "b0", YC + "b1", YC + "c0", YC + "c1"]
        for nb in range(2):
            nsl = slice(nb * 512, (nb + 1) * 512)
            for c in range(8):
                S.pe(lambda e, nb=nb, c=c, nsl=nsl: e.matmul(BO[nb][:, :], ycT[b2][:, c, :], wob[:, c, nsl],
                                                             start=(c == 0), stop=(c == 7)),
                     reads=YCALL + [WOB[c]], writes=[PK("bo%d" % nb)])
            S.dve(lambda e, nb=nb, nsl=nsl: e.tensor_tensor(hnew[b3][:, nsl], BO[nb][:, :], ht[b3][:, nsl], ALU.add),
                  reads=[PK("bo%d" % nb), HT], writes=[K("hnew%d_%d" % (b3, nb))])
            S.act(lambda e, nb=nb, nsl=nsl: e.copy(hb[b2][:, nsl], hnew[b3][:, nsl]),
                  reads=[K("hnew%d_%d" % (b3, nb))], writes=[K("hb%d_%d" % (b2, nb))])

    def stageT(blk):
        b2 = blk % 2
        for c in range(8):
            S.pe(lambda e, c=c: e.transpose(PT2[:, c, :], hb[b2][:, c * 128:(c + 1) * 128], idt[:]),
                 reads=[K("hb%d_%d" % (b2, c // 4)), K("idt")], writes=[PK("pt2")])
        S.act(lambda e: e.copy(hT[b2][:], PT2[:]), reads=[PK("pt2")], writes=[K("hT%d" % b2)])

    def stageG(blk):
        b2, b3, b4 = blk % 2, blk % 3, blk % 4
        for nb in range(2):
            nsl = slice(nb * 512, (nb + 1) * 512)
            for c in range(8):
                S.pe(lambda e, nb=nb, c=c, nsl=nsl: e.matmul(BG[nb][:, :], hT[b2][:, c, :], wgb[:, c, nsl],
                                                             start=(c == 0), stop=(c == 7)),
                     reads=[K("hT%d" % b2), WGB[c]], writes=[PK("bg%d" % nb)])
            S.act(lambda e, nb=nb, nsl=nsl: e.activation(sig[:, nsl], BG[nb][:, :], AF.Sigmoid),
                  reads=[PK("bg%d" % nb)], writes=[K("sig%d" % nb)])
            for c in range(2):
                S.pe(lambda e, nb=nb, c=c, nsl=nsl: e.matmul(BP[:, :], pT[b4][:, c, :], wpb[:, c, nsl],
                                                             start=(c == 0), stop=(c == 1)),
                     reads=[K("pT%d" % b4), WPB[c]], writes=[PK("bp")])
            S.dve(lambda e, nb=nb, nsl=nsl: e.tensor_tensor(tmp[:, nsl], BP[:, :], sig[:, nsl], ALU.mult),
                  reads=[PK("bp"), K("sig%d" % nb)], writes=[K("tmp%d" % nb)])
            S.dve(lambda e, nb=nb, nsl=nsl: e.tensor_tensor(h2[b2][:, nsl], tmp[:, nsl], hnew[b3][:, nsl], ALU.add),
                  reads=[K("tmp%d" % nb), K("hnew%d_%d" % (b3, nb))], writes=[K("h2%d_%d" % (b2, nb))])
        H2 = [K("h2%d_%d" % (b2, nb)) for nb in range(2)]
        if last:
            S.act(lambda e: e.activation(junk[:], h2[b2][:], AF.Square, accum_out=ss[:]), reads=H2,
                  writes=[K("junk"), K("ss")])
            S.act(lambda e: e.activation(ss[:], ss[:], AF.Sqrt, bias=epst[:], scale=1.0 / D), reads=[K("ss"), K("epst")],
                  writes=[K("ss")])
            S.dve(lambda e: e.reciprocal(ss[:], ss[:]), reads=[K("ss")], writes=[K("ss")])
            S.dve(lambda e: e.scalar_tensor_tensor(h2[b2][:], h2[b2][:], ss[:, 0:1], gfin[:], ALU.mult, ALU.mult),
                  reads=H2 + [K("ss"), K("gfin")], writes=H2)
        S.dma("pool", lambda e: e.dma_start(out=ov[blk], in_=h2[b2][:]), reads=H2, writes=[K("hout%d" % blk)])

    for it in range(nblk + 3):
        if it == min(6, nblk - 1):
            if "mid_hook" in t:
                t["mid_hook"](S)
            load_oall(1)
        if it < nblk:
            stageP1(it)
        if 0 <= it - 1 < nblk:
            stageO(it - 1)
        if it < nblk:
            stageP2(it)
        if 0 <= it - 2 < nblk:
            stageT(it - 2)
        if 0 <= it - 3 < nblk:
            stageG(it - 3)


def build_b2(last=False, **kw):
    nc = bass.Bass("TRN2", target_bir_lowering=False)
    with ExitStack() as es:
        C = Ctx(nc, es)
        t = dict(
            Og=C.din("Og", [4, NTOK, 128], BF16), SGA=C.din("SGA", [NTOK, 512], BF16),
            VN=C.din("VN", [NTOK, 256], BF16), XB=C.din("XB", [NTOK, 256], BF16),
            GBT=C.din("GBT", [256, NTOK], BF16), UGT=C.din("UGT", [256, NTOK], BF16),
            XBlast=C.din("XBlast", [4, 128, 256], BF16), halosel=C.din("halosel", [128, 4], F32),
            Afirst=C.din("Afirst", [4, 128, 128], BF16), Amat=C.din("Amat", [4, 2, 128, 128], BF16),
            trilm=C.din("trilm", [128, 128], F32), h=C.din("h", [NTOK, D], F32), p=C.din("p", [NTOK, 256], F32),
            w_out=C.din("w_out", [D, D], F32), ple_w=C.din("ple_w", [256, D], F32),
            ple_gate_w=C.din("ple_gate_w", [D, D], F32), pool_w=C.din("pool_w", [4, 64, 64], F32),
            pscale=C.din("pscale", [128, 2], F32), sgu_w=C.din("sgu_w", [4, 128, 128], F32),
            sgu_b=C.din("sgu_b", [1, 512], F32), ident=C.din("ident", [128, 128], BF16),
            hout=C.dout("hout", [NTOK, D], F32))
        if last:
            t["final_g"] = C.din("final_g", [1, D], F32)
        S = Sched(nc)
        emit_phase_b2(nc, S, C, t, last=last, **kw)
        S.emit()
    return nc


POOL_W = (2, 4, 8, 16)


def make_consts_b2():
    c = {}
    s = np.arange(128)[:, None]
    tt = np.arange(128)[None, :]
    Amat = np.zeros((4, 2, 128, 128), np.float32)
    Afirst = np.zeros((4, 128, 128), np.float32)
    for g, w in enumerate(POOL_W):
        d = tt - s
        Amat[g, 0] = np.where((d >= 0) & (d < w), 1.0 / w, 0.0) - (d == 0)
        dp = tt + 128 - s
        Amat[g, 1] = np.where(dp < w, 1.0 / w, 0.0)
        Afirst[g] = np.where((d >= 0) & (d < w), 1.0 / np.minimum(tt + 1, w), 0.0) - (d == 0)
    c["Amat"] = Amat.astype(ml_dtypes.bfloat16)
    c["Afirst_seq0"] = Afirst.astype(ml_dtypes.bfloat16)
    c["Afirst_other"] = Amat[:, 0].astype(ml_dtypes.bfloat16)
    c["trilm"] = np.tril(np.ones((128, 128), np.float32))
    return c


GROUPS = [[0, 1, 2, 3], [4, 5, 6, 7]]
BYP = mybir.AluOpType.bypass


def build_fused(depth=2, skip=()):
    nc = bass.Bass("TRN2", target_bir_lowering=False)
    ext_in = lambda name, shape, dt: nc.dram_tensor(name, shape, dt, kind="ExternalInput").ap()
    loc = lambda name, shape, dt: nc.dram_tensor(name, shape, dt)
    x = ext_in("x", [NTOK, D], F32)
    p = ext_in("p", [depth, NTOK, 256], F32)
    norm_g = ext_in("norm_g", [depth, 128, 8], F32)
    w_in = ext_in("w_in", [depth, D, DIN], F32)
    w_out = ext_in("w_out", [depth, D, D], F32)
    rb = ext_in("rb", [32, 2], F32)
    pool_w = ext_in("pool_w", [depth, 4, 64, 64], F32)
    pscale = ext_in("pscale", [depth, 128, 2], F32)
    sgu_w = ext_in("sgu_w", [depth, 4, 128, 128], F32)
    sgu_b = ext_in("sgu_b", [depth, 1, 512], F32)
    ple_w = ext_in("ple_w", [depth, 256, D], F32)
    ple_gate_w = ext_in("ple_gate_w", [depth, D, D], F32)
    final_g = ext_in("final_g", [1, D], F32)
    halosel = ext_in("halosel", [128, 4], F32)
    Afirst = ext_in("Afirst", [4, 128, 128], BF16)
    consts = dict(ident=ext_in("ident", [128, 128], BF16), identf=ext_in("identf", [128, 128], F32),
                  Erows=ext_in("Erows", [32, S_LEN], BF16), onehot=ext_in("onehot", [33, NTAB], F32),
                  masks=ext_in("masks", [3, 128, 1024], F32), Amat=ext_in("Amat", [4, 2, 128, 128], BF16),
                  trilm=ext_in("trilm", [128, 128], F32))
    out = nc.dram_tensor("out", [NTOK, D], F32, kind="ExternalOutput").ap()
    hbuf = loc("hbuf", [NTOK, D], F32).ap()
    Qs2 = loc("Qs2", [2, 4, 128, 8, 128], BF16)
    KT2 = loc("KT2", [2, 8, 64, 1024], BF16)
    Vd2 = loc("Vd2", [2, 8, 128, 8, 64], BF16)
    XBl = loc("XBl", [128, 256], BF16)
    QsG = loc("QsG", [2, 4, 4, 128, 8, 128], BF16)
    KTG = loc("KTG", [2, 4, 4, 2, 64, 1024], BF16)
    VdG = loc("VdG", [2, 4, 4, 2, 128, 8, 64], BF16)
    XBlG = loc("XBlG", [4 * 128, 256], BF16)
    Os2 = loc("Os2", [2, 4, 1024, 128], BF16)
    OsG = loc("OsG", [2, 4, 4, 1024, 128], BF16)
    Qg = loc("Qg", [4, NTOK, 128], BF16)
    KTg = loc("KTg", [4, 2, 64, NTOK], BF16)
    Vdg = loc("Vdg", [4, 2, 128, 16, 64], BF16)
    Og = loc("Og", [4, NTOK, 128], BF16)
    SGA = loc("SGA", [NTOK, 512], BF16).ap()
    VN = loc("VN", [NTOK, 256], BF16).ap()
    XB = loc("XB", [NTOK, 256], BF16).ap()
    GBT = loc("GBT", [256, NTOK], BF16).ap()
    UGT = loc("UGT", [256, NTOK], BF16).ap()
    tabd = loc("tabd", [2, 128, NTAB], BF16).ap()

    rank_cache = {}

    def rank(e):
        key = (len(rank_cache_phase), str(e.engine))
        if key not in rank_cache:
            rank_cache[key] = e.partition_id() % 4
        return rank_cache[key]

    rank_cache_phase = []

    with ExitStack() as gs:
        P = SemPool(nc, gs)
        for i in range(depth):
            last = i == depth - 1
            with ExitStack() as es:
                C = Ctx(nc, es)
                S = Sched(nc, P)
                ta = dict(h=x if i == 0 else hbuf, w_in=w_in[i], g=norm_g[i], ident=consts["ident"], Qs2=Qs2.ap(),
                          SGA=SGA, VN=VN, XB=XB, KT2=KT2.ap(), Vd2=Vd2.ap(), GBT=GBT, UGT=UGT)
                def ag_(S_, src2d, dst2d, key, reads=()):
                    S_.cc(lambda e: e.collective_compute("AllGather", BYP, replica_groups=GROUPS, ins=[src2d.opt()],
                                                         outs=[dst2d.opt()]), reads=list(reads), writes=[key])

                deferred = []

                def xchg_half(S_, j, rq=(), rk=(), rv=()):
                    ag_(S_, Vd2.ap()[j].rearrange("h p k d -> (h p) (k d)"),
                        VdG.ap()[j].rearrange("s h l p k d -> (s h l p) (k d)"), "VdG%d" % j, rv)
                    ag_(S_, KT2.ap()[j].rearrange("h d t -> (h d) t"), KTG.ap()[j].rearrange("s h l d t -> (s h l d) t"),
                        "KTG%d" % j, rk)
                    ag_(S_, Qs2.ap()[j].rearrange("d p k c -> (d p) (k c)"),
                        QsG.ap()[j].rearrange("s h p k c -> (s h p) (k c)"), "QsG%d" % j, rq)
                    def copies():
                        S_.dma("sp", lambda e: e.dma_start(
                            out=Vdg.ap().rearrange("s l p k d -> s (l p) k d")[:, :, j * 8:(j + 1) * 8, :].rearrange(
                                "s r k d -> s r (k d)"),
                            in_=VdG.ap()[j, :, bass.ds(rank(e), 1)].rearrange("s o l p k d -> s (o l p) (k d)")),
                            reads=["VdG%d" % j], writes=["Vdg_%d" % j])
                    deferred.append(copies)

                def after_tile(lt, S_, K_):
                    if lt in (1, 3):
                        j = lt // 2
                        xchg_half(S_, j,
                                  rq=[K_("Qtok%d" % r_) for r_ in range(j * 1024, (j + 1) * 1024, 128)],
                                  rk=[K_("KT%d_%d" % (l_, i_)) for l_ in (2 * j, 2 * j + 1) for i_ in range(4)],
                                  rv=[K_("Vd%d_%d" % (l_, h_)) for l_ in (2 * j, 2 * j + 1) for h_ in range(8)])

                if "xa" not in skip:
                    ta["after_tile"] = after_tile

                def after_xb(S_, K_):
                    S_.dma("sp", lambda e: e.dma_start(out=XBl.ap(), in_=XB[NTOK - 128:NTOK, :]),
                           reads=[K_("XB%d" % (NTOK - 128))], writes=["XBl"])
                    ag_(S_, XBl.ap(), XBlG.ap(), "XBlG", reads=["XBl"])

                if "xa" not in skip:
                    ta["after_xb"] = after_xb

                def after_p2(n2):
                    if n2 == 2 and deferred:
                        deferred.pop(0)()

                ta["after_p2"] = after_p2
                if "a" not in skip:
                    emit_phase_a(nc, S, C, ta, pfx="a%d_" % i)
                else:
                    xchg_half(S, 0)
                    xchg_half(S, 1)
                    after_xb(S, lambda k_: k_)
                for fn_ in deferred:
                    fn_()
                S.drain_cc = False
                rank_cache_phase.append(1)
                S.emit()
            wscope = ExitStack()
            wpre = dict(wob=wscope.enter_context(nc.sbuf_tensor("wob_%d" % i, [128, 8, 1024], BF16)),
                        wgb=wscope.enter_context(nc.sbuf_tensor("wgb_%d" % i, [128, 8, 1024], BF16)),
                        wpb=wscope.enter_context(nc.sbuf_tensor("wpb_%d" % i, [128, 2, 1024], BF16)))
            with ExitStack() as es:
                C = Ctx(nc, es)
                S = Sched(nc, P)
                def after_epi(S_, K_, hl_, T_, i=i, wpre=wpre):
                    if hl_ == 0 and T_ == 4:
                        for (src_, key_, nch_) in ((w_out[i], "wob", 8), (ple_gate_w[i], "wgb", 8), (ple_w[i], "wpb", 2)):
                            for c_ in range(nch_):
                                S_.dma("pool", lambda e, src_=src_, key_=key_, c_=c_: e.dma_start(
                                    out=wpre[key_][:, c_, :], in_=src_[c_ * 128:(c_ + 1) * 128, :]),
                                    reads=[K_("Oall0_16")], writes=["%s%d_%d" % (key_, i, c_)])

                tb = dict(rb=rb, Os2=Os2.ap(), tabd=tabd, Qg=Qg.ap(), KTg=KTg.ap(), Vdg=Vdg.ap(), after_epi=after_epi,
                          **consts)
                tb["Qdirect"] = lambda e, j: QsG.ap()[j, :, bass.ds(rank(e), 1)].rearrange("s o p k c -> (o p) s (k c)")
                tb["Kdirect"] = lambda e, hl, j: KTG.ap()[j, :, bass.ds(rank(e), 1), hl].rearrange("s o d t -> (o d) s t")

                def after_out(S_, K_, th, nodep=False):
                    S_.cc(lambda e: e.collective_compute(
                        "AllGather", BYP, replica_groups=GROUPS, ins=[Os2.ap()[th].rearrange("d t c -> (d t) c").opt()],
                        outs=[OsG.ap()[th].rearrange("s h t c -> (s h t) c").opt()]),
                        reads=([] if nodep else [K_("Os%d_%d" % (d_, th)) for d_ in range(4)]), writes=["OsG%d" % th])
                    S_.dma("sp", lambda e: e.dma_start(
                        out=Og.ap()[:, th * 1024:(th + 1) * 1024, :],
                        in_=OsG.ap()[th, :, bass.ds(rank(e), 1)].rearrange("s o t c -> s (o t) c")),
                        reads=["OsG%d" % th], writes=["Og_%d" % th])

                tb["after_out"] = after_out
                if "b1" not in skip:
                    emit_phase_b1(nc, S, C, tb, pfx="b%d_" % i)
                else:
                    after_out(S, lambda k_: k_, 0)
                rank_cache_phase.append(1)
                S.emit()
            with ExitStack() as es:
                C = Ctx(nc, es)
                S = Sched(nc, P)
                tc_ = dict(SGA=SGA, VN=VN, XB=XB, GBT=GBT, UGT=UGT, XBlast=XBlG.ap().rearrange("(s p) c -> s p c", p=128),
                           halosel=halosel, Afirst=Afirst, h=x if i == 0 else hbuf, p=p[i], Og=Og.ap(),
                           hout=out if last else hbuf, w_out=w_out[i], ple_w=ple_w[i], ple_gate_w=ple_gate_w[i],
                           pool_w=pool_w[i], pscale=pscale[i], sgu_w=sgu_w[i], sgu_b=sgu_b[i], final_g=final_g,
                           **consts)
                def after_loads(S_, keys):
                    S_.cc(lambda e: e.collective_compute(
                        "AllGather", BYP, replica_groups=GROUPS, ins=[Os2.ap()[1].rearrange("d t c -> (d t) c").opt()],
                        outs=[OsG.ap()[1].rearrange("s h t c -> (s h t) c").opt()]), reads=keys, writes=["OsG1"])

                tc_["after_loads"] = after_loads
                if "b2" in skip:
                    after_loads(S, [])
                tc_["mid_hook"] = lambda S_: S_.dma("sp", lambda e: e.dma_start(
                    out=Og.ap()[:, 1024:2048, :],
                    in_=OsG.ap()[1, :, bass.ds(rank(e), 1)].rearrange("s o t c -> s (o t) c")),
                    reads=["OsG1"], writes=["Og_1"])
                tc_["okeys"] = {1: ["Og_1"]}
                tc_["wpre"] = wpre
                rank_cache_phase.append(1)
                if "b2" not in skip:
                    emit_phase_b2(nc, S, C, tc_, pfx="c%d_" % i, last=last)
                S.emit()
            wscope.close()
    return nc


_PROGS = {}


def kernel(x, p, norm_g, w_in, w_out, rel_bias, pool_w, pool_scale, sgu_w, sgu_b, ple_w, ple_gate_w, final_g):
    f32 = lambda a: np.ascontiguousarray(np.asarray(a, dtype=np.float32))
    x, p, norm_g, w_in, w_out, rel_bias = map(f32, (x, p, norm_g, w_in, w_out, rel_bias))
    pool_w, pool_scale, sgu_w, sgu_b, ple_w, ple_gate_w, final_g = map(
        f32, (pool_w, pool_scale, sgu_w, sgu_b, ple_w, ple_gate_w, final_g))
    depth = w_in.shape[0]
    cs = make_consts()
    cb = make_consts_b2()
    if "fused" not in _PROGS:
        _PROGS["fused"] = build_fused(depth)
    nc = _PROGS["fused"]
    shared = dict(
        norm_g=np.ascontiguousarray(norm_g.reshape(depth, 8, 128).transpose(0, 2, 1)), w_in=w_in, w_out=w_out,
        pool_w=pool_w, pscale=np.ascontiguousarray(pool_scale.reshape(depth, 2, 128).transpose(0, 2, 1)),
        sgu_w=sgu_w, sgu_b=np.ascontiguousarray(sgu_b.reshape(depth, 1, 512)), ple_w=ple_w, ple_gate_w=ple_gate_w,
        final_g=np.ascontiguousarray(final_g.reshape(1, D)), ident=cs["ident"], identf=cs["identf"],
        Erows=cs["Erows"], onehot=cs["onehot"], masks=cs["masks"], Amat=cb["Amat"], trilm=cb["trilm"])
    in_maps = []
    for c in range(NCORES):
        b, r = c // 4, c % 4
        d = dict(shared)
        d["x"] = np.ascontiguousarray(x[b, r * NTOK:(r + 1) * NTOK])
        d["p"] = np.ascontiguousarray(p[:, b, r * NTOK:(r + 1) * NTOK])
        d["rb"] = np.ascontiguousarray(rel_bias[:, 2 * r:2 * r + 2])
        d["halosel"] = np.ascontiguousarray(np.broadcast_to((np.arange(4) == r - 1).astype(np.float32)[None, :], (128, 4)))
        d["Afirst"] = cb["Afirst_seq0"] if r == 0 else cb["Afirst_other"]
        in_maps.append(d)
    res = run_bass_kernel_spmd(nc, in_maps, core_ids=list(range(NCORES))).results
    out = np.zeros(x.shape, np.float32)
    for c in range(NCORES):
        b, r = c // 4, c % 4
        out[b, r * NTOK:(r + 1) * NTOK] = np.asarray(res[c]["out"])
    return out
```
